# Optimizing a Trainium2 kernel written in Bass

```python
import jax, jax.numpy as jnp
from jax import lax
import numpy as np

D_MODEL = 2048
BATCH = 4
SEQ = 2048
DEPTH = 1
DEC_BATCH = 32
DEC_SEQ = 32
PAST_LEN = 1024

CHUNK = 64
MIX_WIDTH = D_MODEL
ATTN_WIDTH = MIX_WIDTH // 2
GMLP_WIDTH = MIX_WIDTH - ATTN_WIDTH
N_HEADS = 8
NOPE_DIM = 128
ROPE_DIM = 64
QK_DIM = NOPE_DIM + ROPE_DIM
V_DIM = ATTN_WIDTH // N_HEADS
Q_RANK = 512
KV_RANK = 256
GMLP_GROUPS = 8
GMLP_GROUP_DIM = GMLP_WIDTH // GMLP_GROUPS
GMLP_CHUNK = 128
D_FF = 4 * D_MODEL
IN_WIDTH = Q_RANK + KV_RANK + ROPE_DIM + 2 * GMLP_WIDTH
ROPE_THETA = 10000.0
EPS = 1e-6
Q_BLOCK = 128

kernel_name = "hymba_mla_gmlp_stream_step"


def rmsnorm(x, g):
    xf = x.astype(jnp.float32)
    y = xf * lax.rsqrt(jnp.mean(jnp.square(xf), axis=-1, keepdims=True) + EPS)
    return (y * g.astype(jnp.float32)).astype(x.dtype)


def rope(x, pos):
    half = ROPE_DIM // 2
    inv = ROPE_THETA ** (-jnp.arange(half, dtype=jnp.float32) / half)
    ang = pos.astype(jnp.float32)[:, None] * inv[None, :]
    ang = ang.reshape(ang.shape[:1] + (1,) * (x.ndim - 3) + (half,))
    cos, sin = jnp.cos(ang), jnp.sin(ang)
    xf = x.astype(jnp.float32)
    x1, x2 = xf[..., :half], xf[..., half:]
    return jnp.concatenate([x1 * cos - x2 * sin, x1 * sin + x2 * cos], axis=-1).astype(x.dtype)


def qk_gain(g_nope, g_rope):
    return jnp.concatenate([g_nope, g_rope, g_rope], axis=-1)


def project(x, pos, p):
    b, t, _ = x.shape
    z = rmsnorm(x, p["norm_mix"]) @ p["w_in"]
    s1 = Q_RANK
    s2 = s1 + KV_RANK
    s3 = s2 + ROPE_DIM
    s4 = s3 + GMLP_WIDTH
    q_lat, kv_lat, k_pe, u, v = jnp.split(z, [s1, s2, s3, s4], axis=-1)
    q = (rmsnorm(q_lat, p["q_lat_norm"]) @ p["w_uq"]).reshape(b, t, N_HEADS, QK_DIM)
    q = jnp.concatenate([q[..., :NOPE_DIM], rope(q[..., NOPE_DIM:], pos)], axis=-1)
    q = rmsnorm(q, qk_gain(p["q_norm_nope"], p["q_norm_rope"]))
    c_kv = rmsnorm(kv_lat, p["kv_lat_norm"])
    k_pe = rope(k_pe, pos)
    u = jax.nn.gelu(u).reshape(b, t, GMLP_GROUPS, GMLP_GROUP_DIM)
    v = rmsnorm(jax.nn.gelu(v).reshape(b, t, GMLP_GROUPS, GMLP_GROUP_DIM), p["v_norm"])
    return q, c_kv, k_pe, u, v


def expand_kv(c_kv, k_pe, p):
    b, l, _ = c_kv.shape
    k_nope = (c_kv @ p["w_uk"]).reshape(b, l, N_HEADS, NOPE_DIM)
    k_rot = jnp.broadcast_to(k_pe[:, :, None, :], (b, l, N_HEADS, ROPE_DIM))
    k = rmsnorm(jnp.concatenate([k_nope, k_rot], axis=-1), qk_gain(p["k_norm_nope"], p["k_norm_rope"]))
    v = (c_kv @ p["w_uv"]).reshape(b, l, N_HEADS, V_DIM)
    return k, v


def attend(q, k, v, q_pos, k_pos):
    s = jnp.einsum("bqhd,bkhd->bhqk", q, k).astype(jnp.float32) * (QK_DIM ** -0.5)
    mask = (k_pos[None, :] // CHUNK) <= (q_pos[:, None] // CHUNK)
    s = jnp.where(mask[None, None], s, jnp.finfo(jnp.float32).min)
    w = jax.nn.softmax(s, axis=-1).astype(v.dtype)
    return jnp.einsum("bhqk,bkhd->bqhd", w, v)


def spatial_gate(u, v, w_s, b_s):
    l = v.shape[2]
    w = jnp.tril(w_s[:, :l, :l])
    s = jnp.einsum("gts,bnsgc->bntgc", w, v) + b_s[:, :l].T[None, None, :, :, None]
    return u * s


def finish(x, attn_o, gmlp_o, p):
    m = jnp.concatenate([rmsnorm(attn_o, p["out_norm_attn"]), rmsnorm(gmlp_o, p["out_norm_gmlp"])], axis=-1)
    h = x + m @ p["w_out"]
    f = jnp.square(jax.nn.relu(rmsnorm(h, p["norm_ffn"]) @ p["w_up"])) @ p["w_down"]
    return h + f


def layer_prompt(x, p):
    b, t, _ = x.shape
    pos = jnp.arange(t, dtype=jnp.int32)
    q, c_kv, k_pe, u, v = project(x, pos, p)
    k, vv = expand_kv(c_kv, k_pe, p)
    nb = t // Q_BLOCK
    qb = q.reshape(b, nb, Q_BLOCK, N_HEADS, QK_DIM).transpose(1, 0, 2, 3, 4)
    pb = pos.reshape(nb, Q_BLOCK)
    o = lax.map(lambda a: attend(a[0], k, vv, a[1], pos), (qb, pb))
    attn_o = o.transpose(1, 0, 2, 3, 4).reshape(b, t, ATTN_WIDTH)
    nc = t // GMLP_CHUNK
    shp = (b, nc, GMLP_CHUNK, GMLP_GROUPS, GMLP_GROUP_DIM)
    gmlp_o = spatial_gate(u.reshape(shp), v.reshape(shp), p["w_spatial"], p["b_spatial"]).reshape(b, t, GMLP_WIDTH)
    return finish(x, attn_o, gmlp_o, p), c_kv, k_pe


def layer_sample(x, cache_c_kv, cache_k_rope, p):
    b, t, _ = x.shape
    past = cache_c_kv.shape[1]
    q_pos = past + jnp.arange(t, dtype=jnp.int32)
    k_pos = jnp.arange(past + t, dtype=jnp.int32)
    q, c_kv, k_pe, u, v = project(x, q_pos, p)
    all_c = jnp.concatenate([cache_c_kv.astype(c_kv.dtype), c_kv], axis=1)
    all_pe = jnp.concatenate([cache_k_rope.astype(k_pe.dtype), k_pe], axis=1)
    k, vv = expand_kv(all_c, all_pe, p)
    attn_o = attend(q, k, vv, q_pos, k_pos).reshape(b, t, ATTN_WIDTH)
    gmlp_o = spatial_gate(u[:, None], v[:, None], p["w_spatial"], p["b_spatial"]).reshape(b, t, GMLP_WIDTH)
    return finish(x, attn_o, gmlp_o, p), c_kv, k_pe, v.reshape(b, t, GMLP_WIDTH)


def setup_inputs(seed: int = 0) -> dict:
    key = jax.random.key(seed)
    ks = list(jax.random.split(key, 32))
    f32 = jnp.float32

    def w(k, shape, fan_in):
        return jax.random.normal(k, (DEPTH,) + shape, f32) * (fan_in ** -0.5)

    def gain(k, shape):
        return 1.0 + 0.05 * jax.random.normal(k, (DEPTH,) + shape, f32)

    return {
        "x_prompt": jax.random.normal(ks[0], (BATCH, SEQ, D_MODEL), f32),
        "x_sample": jax.random.normal(ks[1], (DEC_BATCH, DEC_SEQ, D_MODEL), f32),
        "cache_c_kv": jax.random.normal(ks[2], (DEPTH, DEC_BATCH, PAST_LEN, KV_RANK), f32),
        "cache_k_rope": jax.random.normal(ks[3], (DEPTH, DEC_BATCH, PAST_LEN, ROPE_DIM), f32),
        "norm_mix": gain(ks[4], (D_MODEL,)),
        "w_in": w(ks[5], (D_MODEL, IN_WIDTH), D_MODEL),
        "q_lat_norm": gain(ks[6], (Q_RANK,)),
        "kv_lat_norm": gain(ks[7], (KV_RANK,)),
        "w_uq": w(ks[8], (Q_RANK, N_HEADS * QK_DIM), Q_RANK),
        "w_uk": w(ks[9], (KV_RANK, N_HEADS * NOPE_DIM), KV_RANK),
        "w_uv": w(ks[10], (KV_RANK, N_HEADS * V_DIM), KV_RANK),
        "q_norm_nope": gain(ks[11], (NOPE_DIM,)),
        "q_norm_rope": gain(ks[12], (ROPE_DIM // 2,)),
        "k_norm_nope": gain(ks[13], (NOPE_DIM,)),
        "k_norm_rope": gain(ks[14], (ROPE_DIM // 2,)),
        "v_norm": gain(ks[15], (GMLP_GROUPS, GMLP_GROUP_DIM)),
        "w_spatial": w(ks[16], (GMLP_GROUPS, GMLP_CHUNK, GMLP_CHUNK), GMLP_CHUNK),
        "b_spatial": gain(ks[17], (GMLP_GROUPS, GMLP_CHUNK)),
        "out_norm_attn": gain(ks[18], (ATTN_WIDTH,)),
        "out_norm_gmlp": gain(ks[19], (GMLP_WIDTH,)),
        "w_out": w(ks[20], (MIX_WIDTH, D_MODEL), MIX_WIDTH),
        "norm_ffn": gain(ks[21], (D_MODEL,)),
        "w_up": w(ks[22], (D_MODEL, D_FF), D_MODEL),
        "w_down": w(ks[23], (D_FF, D_MODEL), D_FF),
    }


def reference(x_prompt, x_sample, cache_c_kv, cache_k_rope, norm_mix, w_in, q_lat_norm, kv_lat_norm,
              w_uq, w_uk, w_uv, q_norm_nope, q_norm_rope, k_norm_nope, k_norm_rope, v_norm,
              w_spatial, b_spatial, out_norm_attn, out_norm_gmlp, w_out, norm_ffn, w_up, w_down):
    xp, xs = x_prompt, x_sample
    ckv_p, kpe_p, ckv_s, kpe_s, v_s = [], [], [], [], []
    for i in range(DEPTH):
        p = {
            "norm_mix": norm_mix[i], "w_in": w_in[i], "q_lat_norm": q_lat_norm[i],
            "kv_lat_norm": kv_lat_norm[i], "w_uq": w_uq[i], "w_uk": w_uk[i], "w_uv": w_uv[i],
            "q_norm_nope": q_norm_nope[i], "q_norm_rope": q_norm_rope[i],
            "k_norm_nope": k_norm_nope[i], "k_norm_rope": k_norm_rope[i], "v_norm": v_norm[i],
            "w_spatial": w_spatial[i], "b_spatial": b_spatial[i],
            "out_norm_attn": out_norm_attn[i], "out_norm_gmlp": out_norm_gmlp[i],
            "w_out": w_out[i], "norm_ffn": norm_ffn[i], "w_up": w_up[i], "w_down": w_down[i],
        }
        xp, c_p, k_p = layer_prompt(xp, p)
        xs, c_s, k_s, vrow = layer_sample(xs, cache_c_kv[i], cache_k_rope[i], p)
        ckv_p.append(c_p)
        kpe_p.append(k_p)
        ckv_s.append(c_s)
        kpe_s.append(k_s)
        v_s.append(vrow)
    new_c_kv_prompt = jnp.stack(ckv_p)
    new_k_rope_prompt = jnp.stack(kpe_p)
    new_c_kv_sample = jnp.stack(ckv_s)
    new_k_rope_sample = jnp.stack(kpe_s)
    new_gmlp_v_sample = jnp.stack(v_s)
    return (xp, xs, new_c_kv_prompt, new_k_rope_prompt, new_c_kv_sample, new_k_rope_sample, new_gmlp_v_sample)
```

```python
import numpy as np
import concourse.bass as bass
import concourse.mybir as mybir
from concourse.bass_utils import run_bass_kernel_spmd

F32 = mybir.dt.float32
BF16 = mybir.dt.bfloat16
AF = mybir.ActivationFunctionType
ALU = mybir.AluOpType
AX = mybir.AxisListType

D = 2048
NT = 9
TOK = NT * 128
NPRE = 8
EPS = 1e-6
DFF = 8192


class Tok:
    __slots__ = ("sem", "val", "eng", "op")

    def __init__(self, sem, val, eng, op):
        self.sem, self.val, self.eng, self.op = sem, val, eng, op


class Buf:
    __slots__ = ("name", "writers", "readers", "excl")

    def __init__(self, name="", excl=False):
        self.name = name
        self.writers = []
        self.readers = []
        self.excl = excl


class Op:
    __slots__ = ("fn", "deps", "tok", "needs_inc", "dma_sem")


class Eng:
    def __init__(self, name, sem, in_order=False):
        self.name, self.sem, self.ops, self.in_order = name, sem, [], in_order


class Prog:
    def __init__(self, nc):
        self.nc = nc
        self.stack = []
        self.engs = {}
        self.dma_sems = []

    def enter(self, cm):
        v = cm.__enter__()
        self.stack.append(cm)
        return v

    def close(self):
        while self.stack:
            self.stack.pop().__exit__(None, None, None)

    def sem(self, name):
        return self.enter(self.nc.semaphore(name))

    def add_engine(self, key, in_order=False):
        e = Eng(key, self.sem("s_" + key), in_order)
        self.engs[key] = e
        return e

    def dma_sem(self, name):
        s = [self.sem(name), 0]
        self.dma_sems.append(s)
        return s

    def op(self, eng, fn, reads=(), writes=(), deps=(), dma_sem=None):
        e = self.engs[eng]
        o = Op()
        o.fn = fn
        o.needs_inc = False
        o.dma_sem = dma_sem
        d = list(deps)
        for b in reads:
            d.extend(b.writers)
            if b.excl:
                d.extend(t for t in b.readers if t.eng is not e)
        for b in writes:
            d.extend(b.readers)
            d.extend(b.writers)
        o.deps = d
        if dma_sem is not None:
            dma_sem[1] += 16
            o.tok = Tok(dma_sem, dma_sem[1], e, o)
        else:
            o.tok = Tok(None, None, e, o)
        for b in reads:
            b.readers.append(o.tok)
            if len(b.readers) > 1:
                b.readers = _compress(b.readers)
        for b in writes:
            if b.readers:
                b.writers = [o.tok]
                b.readers = []
            else:
                b.writers.append(o.tok)
                if len(b.writers) > 1:
                    b.writers = _compress(b.writers)
        e.ops.append(o)
        return o.tok

    def barrier(self):
        toks = []
        for e in self.engs.values():
            for o in reversed(e.ops):
                if o.dma_sem is None and o.fn is not None:
                    toks.append(o.tok)
                    break
        for s in self.dma_sems:
            if s[1] > 0:
                toks.append(Tok(s, s[1], None, None))
        for k in self.engs:
            self.op(k, None, deps=toks)

    def finalize(self):
        for e in self.engs.values():
            for o in e.ops:
                for t in o.deps:
                    if t.sem is None:
                        if t.eng is e and e.in_order:
                            continue
                        t.op.needs_inc = True
        for e in self.engs.values():
            c = 0
            for o in e.ops:
                if o.dma_sem is None and o.needs_inc:
                    c += 1
                    o.tok.val = c

    def emit(self, ekey, h):
        e = self.engs[ekey]
        waited = {}
        for o in e.ops:
            need = {}
            for t in o.deps:
                if t.sem is None:
                    if t.eng is e and e.in_order:
                        continue
                    key = ("c", t.eng.name)
                    semh = t.eng.sem
                else:
                    key = ("d", id(t.sem))
                    semh = t.sem[0]
                v = t.val
                if v > waited.get(key, 0) and v > need.get(key, (None, 0))[1]:
                    need[key] = (semh, v)
            for key, (semh, v) in need.items():
                h.wait_ge(semh, v)
                waited[key] = v
            if o.fn is None:
                continue
            ins = o.fn(h)
            if o.dma_sem is not None:
                ins.then_inc(o.dma_sem[0], 16)
            elif o.needs_inc:
                ins.then_inc(e.sem, 1)


def _compress(toks):
    best = {}
    for t in toks:
        k = ("c", t.eng.name) if t.sem is None else ("d", id(t.sem))
        best[k] = t
    return list(best.values())


def build_program():
    nc = bass.Bass("TRN2", target_bir_lowering=False)
    P = Prog(nc)
    for k in ["sp", "act", "dve", "pool"]:
        P.add_engine(k)
    P.add_engine("pe", in_order=True)

    def din(name, shape):
        return nc.dram_tensor(name, list(shape), F32, kind="ExternalInput").ap()

    def dout(name, shape):
        return nc.dram_tensor(name, list(shape), F32, kind="ExternalOutput").ap()

    x_own = din("x_own", [TOK, D])
    x_pre = din("x_pre", [1024, D])
    cs_own_d = din("cs_own", [TOK, 64])
    cs_pre_d = din("cs_pre", [1024, 64])
    pmask_d = din("pmask", [128, 1])
    cache_ckv = din("cache_ckv", [4, 1024, 256])
    cache_kr = din("cache_kr", [4, 1024, 64])
    w_in = din("w_in", [D, 2880])
    w_uq = din("w_uq", [512, 1536])
    w_uk = din("w_uk", [256, 1024])
    w_uv = din("w_uv", [256, 1024])
    w_out = din("w_out", [D, D])
    w_up = din("w_up", [D, DFF])
    w_down = din("w_down", [DFF, D])
    gmix_d = din("gmix_fm", [128, 16])
    gffn_d = din("gffn_fm", [128, 16])
    gqf_d = din("gq_fm", [128, 4])
    gkv_d = din("kv_lat_norm", [1, 256])
    qn_nope_d = din("q_norm_nope", [1, 128])
    qn_rope_d = din("q_norm_rope", [1, 32])
    kn_nope_d = din("k_norm_nope", [1, 128])
    kn_rope_d = din("k_norm_rope", [1, 32])
    gv_d = din("v_norm", [1, 1024])
    ga_d = din("out_norm_attn", [1, 1024])
    gg_d = din("out_norm_gmlp", [1, 1024])
    wsT_d = din("wsT", [128, 2, 8, 128])
    trilT_d = din("trilT", [128, 128])
    bT_d = din("bT", [128, 2, 8])
    ident_d = din("ident", [128, 128])
    y_d = dout("y", [TOK, D])
    ockv_d = dout("o_ckv", [TOK, 256])
    okr_d = dout("o_kr", [TOK, 64])
    ov_d = dout("o_v", [128, 1024])

    BASE = 16512
    P0, R1, R2, R3, R4 = 0, 14336, 51200, 88064, 161792
    SLOT = 16384
    cnt = [0]

    def A(off, shape, dt):
        cnt[0] += 1
        return nc.alloc_sbuf_tensor_at(f"t{cnt[0]}", list(shape), dt, offset=BASE + off)

    ident = A(P0 + 0, [128, 128], BF16)
    ones = A(P0 + 256, [128, 2], BF16)
    gmix = A(P0 + 320, [128, 16], F32)
    gffn = A(P0 + 384, [128, 16], F32)
    gqf = A(P0 + 448, [128, 4], F32)
    gkv_b = A(P0 + 512, [128, 256], F32)
    gqk = A(P0 + 1536, [128, 192], F32)
    cs_own = A(P0 + 2304, [128, NT, 64], F32)
    cs_pre = A(P0 + 4608, [128, NPRE, 64], F32)
    bT = A(P0 + 6656, [128, 2, 8], F32)
    WT = A(P0 + 6784, [128, 2, 8, 128], BF16)
    pmask = A(P0 + 10880, [128, 1], F32)
    st = A(P0 + 10944, [128, 384], F32)
    rbuf = A(P0 + 12480, [128, 2, 384], BF16)
    xnT = A(R1, [128, 16, TOK], BF16)
    KT2 = A(R1, [128, 2, 2048], BF16)
    V4 = A(R1 + 8192, [128, 16, 4, 128], BF16)
    PT = A(R1 + 24576, [128, 2, 512], BF16)
    ksq = A(R1 + 26624, [128, 512], BF16)
    PTs = A(R1 + 27648, [128, 4, 9, 128], BF16)
    mT = A(R2, [128, 16, TOK], BF16)
    aT = A(R2, [128, 4, TOK], BF16)
    xt = A(R2, [128, 2048], F32)
    xs = A(R2 + 8192, [128, 2048], BF16)
    u_g = A(R2 + 12288, [128, 256], F32)
    v_g = A(R2 + 13312, [128, 256], F32)
    v_n = A(R2 + 14336, [128, 256], F32)
    vsq = A(R2 + 15360, [128, 256], F32)
    v_nb = A(R2 + 16384, [128, 256], BF16)
    gm = A(R2 + 16896, [128, 256], F32)
    gmb = A(R2 + 17920, [128, 256], BF16)
    kvf = A(R2 + 12288, [128, 256], F32)
    kvb = A(R2 + 13312, [128, 256], BF16)
    krt = A(R2 + 13824, [128, 6, 32], F32)
    krf = A(R2 + 14592, [128, 64], F32)
    krb = A(R2 + 14848, [128, 128], BF16)
    qf = A(R2, [128, 8, 192], F32)
    qrb = A(R2 + 6144, [128, 8, 64], BF16)
    qn = A(R2 + 7168, [128, 8, 128], BF16)
    qs = A(R2 + 9216, [128, 512], BF16)
    qlTb = A(R2 + 10240, [128, 4, 128], BF16)
    qsq = A(R2 + 11264, [128, 1536], BF16)
    qrt = A(R2 + 14336, [128, 4, 8, 32], F32)
    y_acc = A(R3, [128, NT, D], F32)
    QT = A(R3, [128, 8, TOK], BF16)
    QrT = A(R3 + 18432, [128, 4, TOK], BF16)
    NK = 2048 + 128
    ckvT = A(R3 + 27648, [128, 2, NK], BF16)
    krT = A(R3 + 36352, [128, NK], BF16)
    wuk = A(R3 + 40704, [128, 2, 1024], BF16)
    wuv = A(R3 + 44800, [128, 2, 1024], BF16)
    wuq = A(R3 + 48896, [128, 4, 1536], BF16)
    ob = A(R3 + 48896, [128, 4, 128], BF16)
    junkb = A(R3 + 49920, [128, 512], BF16)
    scf = A(R3 + 50944, [128, 9, 32], F32)
    gv_b = A(R3 + 61184, [128, 1024], F32)
    gg_b = A(R3 + 65280, [128, 1024], F32)
    ga_b = A(R3 + 65280, [128, 1024], F32)
    rstdk = A(R3 + 69376, [128, 17, 8], F32)
    tmpk = A(R3 + 69920, [128, 17, 8], F32)
    wslot = [A(R4 + i * SLOT, [128, 8192], BF16) for i in range(3)]
    WB = [Buf(f"W{i}") for i in range(3)]
    wsem = [P.dma_sem(f"wsem{i}") for i in range(3)]
    hsb = A(210944, [128, 512], BF16)
    xpT = A(R4 + 2 * SLOT, [128, 2, 16, 128], BF16)
    cck = A(R4 + 2 * SLOT, [128, 8, 256], BF16)
    ckrf = A(R4 + 2 * SLOT + 4096, [128, 8, 64], F32)
    ckrd = A(R4 + 2 * SLOT + 6144, [128, 8, 128], BF16)
    ckvTs = A(R4 + 2 * SLOT + 8192, [128, 2, 1056], BF16)
    krTs = A(R4 + 2 * SLOT + 12416, [128, 1056], BF16)

    pz = [P.enter(nc.psum_tensor(f"pz{i}", [128, 512], F32)) for i in range(2)]
    psu = P.enter(nc.psum_tensor("psu", [128, 512], F32))
    PSU = Buf("psu", True)
    ptrs = [P.enter(nc.psum_tensor(f"ptr{i}", [128, 8, 128], BF16)) for i in range(2)]
    psm = P.enter(nc.psum_tensor("psm", [128, 512], F32))
    pacc = [P.enter(nc.psum_tensor(f"pacc{i}", [128, 4, 128], F32)) for i in range(2)]
    PZ = [Buf(f"pz{i}", True) for i in range(2)]
    PTR = [Buf("ptr0", True), Buf("ptr1", True)]
    PSM = Buf("psm", True)
    PACC = [Buf("pacc0", True), Buf("pacc1", True)]
    pzi = [0]
    ptri = [0]

    pz2 = [(pz[0], PZ[0]), (pz[1], PZ[1])]
    pz6 = pz2 + [(pacc[0][:].rearrange("p a b -> p (a b)"), PACC[0]), (pacc[1][:].rearrange("p a b -> p (a b)"), PACC[1]),
                 (psm, PSM), (psu, PSU)]
    pzpool = [pz6]

    def next_pz():
        pool = pzpool[0]
        i = pzi[0] % len(pool)
        pzi[0] += 1
        return pool[i]

    def next_ptr():
        i = ptri[0] % 2
        ptri[0] += 1
        return ptrs[i][:, 0:4, :], PTR[i]

    ldsem = P.dma_sem("ldsem")
    xsem = P.dma_sem("xsem")
    osem_kv = [P.dma_sem(f"osem_kv{i}") for i in range(3)]
    osem_kr = [P.dma_sem(f"osem_kr{i}") for i in range(3)]
    osem_v = P.dma_sem("osem_v")
    osem_y = P.dma_sem("osem_y")
    ysem = P.dma_sem("ysem")
    csem = P.dma_sem("csem")
    out_toks = []

    def MM(out, lhsT, rhs, start, stop, reads, writes, skip=False):
        if skip:
            return P.op("pe", lambda e: e.matmul(out, lhsT=lhsT, rhs=rhs, start=False, stop=False, skip_group_check=True),
                        reads=reads, writes=writes)
        return P.op("pe", lambda e: e.matmul(out, lhsT=lhsT, rhs=rhs, start=start, stop=stop), reads=reads, writes=writes)

    def TR(out, in_, reads, writes):
        return P.op("pe", lambda e: e.transpose(out=out, in_=in_, identity=ident[:]), reads=reads + [B_const], writes=writes)

    def ACT(out, in_, func, reads, writes, scale=1.0, bias=0.0, accum_out=None):
        if accum_out is None:
            return P.op("act", lambda e: e.activation(out=out, in_=in_, func=func, bias=bias, scale=scale), reads=reads, writes=writes)
        return P.op("act", lambda e: e.activation(out=out, in_=in_, func=func, bias=bias, scale=scale, accum_out=accum_out), reads=reads, writes=writes)

    def TT(eng, out, in0, in1, op, reads, writes):
        return P.op(eng, lambda e: e.tensor_tensor(out=out, in0=in0, in1=in1, op=op), reads=reads, writes=writes)

    def TS(eng, out, in0, s1, s2, op0, op1, reads, writes):
        if s2 is None:
            return P.op(eng, lambda e: e.tensor_scalar(out=out, in0=in0, scalar1=s1, scalar2=None, op0=op0), reads=reads, writes=writes)
        return P.op(eng, lambda e: e.tensor_scalar(out=out, in0=in0, scalar1=s1, scalar2=s2, op0=op0, op1=op1), reads=reads, writes=writes)

    def STT(out, in0, scalar, in1, op0, op1, reads, writes):
        return P.op("dve", lambda e: e.scalar_tensor_tensor(out=out, in0=in0, scalar=scalar, in1=in1, op0=op0, op1=op1), reads=reads, writes=writes)

    def CP(eng, out, in_, reads, writes):
        if eng == "act":
            return P.op("act", lambda e: e.copy(out=out, in_=in_), reads=reads, writes=writes)
        return P.op(eng, lambda e: e.tensor_copy(out=out, in_=in_), reads=reads, writes=writes)

    def RED(out, in_, reads, writes):
        return P.op("dve", lambda e: e.tensor_reduce(out=out, in_=in_, axis=AX.X, op=ALU.add), reads=reads, writes=writes)

    def RECIP(out, in_, reads, writes):
        return P.op("dve", lambda e: e.reciprocal(out=out, in_=in_), reads=reads, writes=writes)

    def DMA(eng, out, in_, reads, writes, sem):
        return P.op(eng, lambda e: e.dma_start(out=out, in_=in_), reads=reads, writes=writes, dma_sem=sem)

    def rstd_of(out, ss, n, reads, writes, tmp):
        ACT(tmp, ss, AF.Sqrt, reads, writes, scale=1.0 / n, bias=EPS)
        RECIP(out, tmp, writes, writes)

    w_in_v = w_in.rearrange("(kc p) n -> p kc n", p=128)

    def wA3_view(slot):
        return wslot[slot][:, 0:8192].rearrange("p (k n) -> p k n", k=16)

    def load_A3(q, slot):
        v = wA3_view(slot)
        u0 = 832 + q * 256
        v0 = 1856 + q * 256
        for k0 in (0, 8):
            DMA("pool", v[:, k0:k0 + 8, 0:256], w_in_v[:, k0:k0 + 8, u0:u0 + 256], [], [WB[slot]], wsem[slot])
            DMA("pool", v[:, k0:k0 + 8, 256:512], w_in_v[:, k0:k0 + 8, v0:v0 + 256], [], [WB[slot]], wsem[slot])

    B_const = Buf("const")
    B_st = Buf("st")
    stage = A(R2, [128, 2, 8, 128], F32)
    stage2 = A(R2 + 8192, [128, 128], F32)
    stage3 = A(R2 + 8704, [128, 128], F32)
    stage4 = A(R2 + 9216, [128, 2, 192], F32)
    B_stage = Buf("stage")
    ldsem2 = P.dma_sem("ldsem2")
    DMA("sp", stage2[:], ident_d[:, :], [], [], ldsem)
    DMA("act", stage3[:], trilT_d[:, :], [], [], ldsem2)
    DMA("sp", stage[:], wsT_d[:, :, :, :], [], [], ldsem)
    DMA("act", gmix[:], gmix_d[:, :], [], [], ldsem2)
    DMA("sp", gffn[:], gffn_d[:, :], [], [], ldsem)
    DMA("act", gqf[:], gqf_d[:, :], [], [], ldsem2)
    DMA("sp", gkv_b[:], gkv_d.partition_broadcast(128), [], [], ldsem)
    DMA("act", stage4[:, 0, 0:128], qn_nope_d.partition_broadcast(128), [], [], ldsem2)
    DMA("sp", stage4[:, 0, 128:160], qn_rope_d.partition_broadcast(128), [], [], ldsem)
    DMA("act", stage4[:, 0, 160:192], qn_rope_d.partition_broadcast(128), [], [], ldsem2)
    DMA("sp", stage4[:, 1, 0:128], kn_nope_d.partition_broadcast(128), [], [], ldsem)
    DMA("act", stage4[:, 1, 128:160], kn_rope_d.partition_broadcast(128), [], [], ldsem2)
    DMA("sp", stage4[:, 1, 160:192], kn_rope_d.partition_broadcast(128), [], [], ldsem)
    DMA("act", cs_own[:], cs_own_d.rearrange("(t p) c -> p t c", p=128), [], [], ldsem2)
    DMA("sp", cs_pre[:], cs_pre_d.rearrange("(t p) c -> p t c", p=128), [], [], ldsem)
    DMA("act", bT[:], bT_d[:, :, :], [], [], ldsem2)
    DMA("sp", pmask[:], pmask_d[:, :], [], [], ldsem)
    DMA("act", gv_b[:], gv_d.partition_broadcast(128), [], [], ldsem2)
    DMA("sp", gg_b[:], gg_d.partition_broadcast(128), [], [], ldsem)
    load_A3(0, 0)
    load_A3(1, 1)
    P.op("dve", lambda e: e.memset(st[:], 0.0), writes=[B_st])
    P.barrier()
    CP("dve", ident[:], stage2[:], [B_stage], [B_const])
    P.op("dve", lambda e: e.memset(ones[:], 1.0), writes=[B_const])
    for v in range(2):
        TT("dve", WT[:, v], stage[:, v], stage3[:].unsqueeze(1).to_broadcast([128, 8, 128]), ALU.mult, [B_stage], [B_const])
    TT("dve", gqk[:], stage4[:, 0, :], stage4[:, 1, :], ALU.mult, [B_stage], [B_const])

    w_in_v = w_in.rearrange("(kc p) n -> p kc n", p=128)

    def load_w(slot, dst_view, src_view, nsplit=2):
        K = dst_view.shape[1]
        step = (K + nsplit - 1) // nsplit
        for k0 in range(0, K, step):
            k1 = min(K, k0 + step)
            DMA("pool", dst_view[:, k0:k1, :], src_view[:, k0:k1, :], [], [WB[slot]], wsem[slot])

    import os as _os
    STOP = _os.environ.get('MK_STOP', '')

    def finish():
        P.op("sp", lambda e: e.nop(), deps=out_toks)

        P.finalize()
        with nc.Block() as block:
            @block.sync
            def _(e):
                P.emit("sp", e)

            @block.scalar
            def _(e):
                P.emit("act", e)

            @block.vector
            def _(e):
                P.emit("dve", e)

            @block.gpsimd
            def _(e):
                P.emit("pool", e)

            @block.tensor
            def _(e):
                P.emit("pe", e)
        P.close()
        return nc

    B_xt, B_xs, B_xnT = Buf("xt"), Buf("xs"), [Buf(f"xnT{t}") for t in range(NT)]
    B_mT = [Buf(f"mT{t}") for t in range(NT)]
    B_tmp = Buf("tmpA")
    SSG = 0
    RSTD_A = 40
    RSTD_G = 50
    SCR = 64

    xsem1 = P.dma_sem("xsem1")
    nt_sets = [
        {"xt": xt, "xs": xs, "Bxt": B_xt, "Bxs": B_xs, "Bs": Buf("nts0"), "sc": 336, "sem": xsem},
        {"xt": A(R3 + 8448, [128, 2048], F32), "xs": A(R3 + 8448 + 8192, [128, 2048], BF16), "Bxt": Buf("xt1"), "Bxs": Buf("xs1"),
         "Bs": Buf("nts1"), "sc": 376, "sem": xsem1},
    ]

    xsem2 = P.dma_sem("xsem2")
    nt_sets_a3 = [nt_sets[0],
                  {"xt": A(R3 + 30720, [128, 2048], F32), "xs": A(R3 + 30720 + 8192, [128, 2048], BF16), "Bxt": Buf("xt2"), "Bxs": Buf("xs2"),
                   "Bs": Buf("nts2"), "sc": 380, "sem": xsem2}]

    def nt_pre(src_dram_tile, S_):
        xt_, xs_, Bxt, Bxs, Bs, sc = S_["xt"], S_["xs"], S_["Bxt"], S_["Bxs"], S_["Bs"], S_["sc"]
        DMA("sp", xt_[:], src_dram_tile, [], [Bxt], S_["sem"])
        ACT(xs_[:], xt_[:], AF.Square, [Bxt], [Bxs, Bs], accum_out=st[:, sc:sc + 1])
        rstd_of(st[:, sc + 2:sc + 3], st[:, sc:sc + 1], D, [Bs], [Bs], st[:, sc + 1:sc + 2])
        TS("dve", xs_[:], xt_[:], st[:, sc + 2:sc + 3], None, ALU.mult, None, [Bxt, Bs], [Bxs])

    def nt_group(g4, dstT, B_dst, gain_fm, S_):
        xs_, Bxs = S_["xs"], S_["Bxs"]
        pt_, PB = next_ptr()
        for j in range(4):
            kc = g4 * 4 + j
            TR(pt_[:, j, :], xs_[:, kc * 128:(kc + 1) * 128], [Bxs], [PB])
        TT("dve", dstT(g4), pt_, gain_fm[:, g4 * 4:(g4 + 1) * 4].unsqueeze(2).to_broadcast([128, 4, 128]), ALU.mult,
           [PB, B_const], [B_dst])

    def norm_transpose(src_dram_tile, dstT, B_dst, gain_fm, t_writes, S_=None):
        S_ = S_ or nt_sets[0]
        nt_pre(src_dram_tile, S_)
        for g4 in range(4):
            nt_group(g4, dstT, B_dst, gain_fm, S_)

    P.barrier()
    if STOP == 'C0':
        return finish()
    a3sets = []
    for si_ in range(5):
        o_ = R3 + si_ * 6144
        tens = (A(o_, [128, 256], F32), A(o_ + 1024, [128, 256], F32), A(o_ + 2048, [128, 256], F32), A(o_ + 3072, [128, 256], F32),
                A(o_ + 4096, [128, 256], BF16), A(o_ + 4608, [128, 256], F32), A(o_ + 5632, [128, 256], BF16))
        a3sets.append({"t": tens, "b": tuple(Buf(f"a3_{si_}_{k}") for k in range(8)), "sa": 280 + si_ * 8})
    a3i = [0]
    B_ssg = Buf("ssg")
    def a3_stage0(it):
        q, t = divmod(it, NT)
        slot = q % 3
        if t == 0 and q == 1:
            load_A3(2, 2)
        if t == 0 and q == 2:
            load_A3(3, 0)
        wv = wA3_view(slot)
        S_ = a3sets[it % 5]
        u_g, v_g, v_n, vsq, v_nb, gm, gmb = S_["t"]
        Bu, Bv, Bq, Bn, Bnb, Bgm, Bgb, Bs = S_["b"]
        nxt = t + 1 if (q == 0 and t + 1 < NT) else None
        if nxt is not None:
            nt_pre(x_own[nxt * 128:(nxt + 1) * 128, :], nt_sets_a3[nxt % 2])
        pu, PU = next_pz()
        for g4 in range(4):
            if nxt is not None:
                nt_group(g4, lambda g, n_=nxt: xnT[:, g * 4:(g + 1) * 4, n_ * 128:(n_ + 1) * 128], B_xnT[nxt], gmix, nt_sets_a3[nxt % 2])
            for kc in range(g4 * 4, g4 * 4 + 4):
                MM(pu[:, 0:512], xnT[:, kc, t * 128:(t + 1) * 128], wv[:, kc, 0:512], kc == 0, kc == 15, [B_xnT[t], WB[slot]], [PU])
        ACT(u_g[:], pu[:, 0:256], AF.Gelu_apprx_tanh, [PU], [Bu])
        ACT(v_g[:], pu[:, 256:512], AF.Gelu_apprx_tanh, [PU], [Bv])

    def a3_stage1(it):
        q, t = divmod(it, NT)
        v = 0 if t < 8 else 1
        S_ = a3sets[it % 5]
        u_g, v_g, v_n, vsq, v_nb, gm, gmb = S_["t"]
        Bu, Bv, Bq, Bn, Bnb, Bgm, Bgb, Bs = S_["b"]
        SA = S_["sa"]
        TT("dve", vsq[:], v_g[:], v_g[:], ALU.mult, [Bv], [Bq])
        RED(st[:, SA:SA + 2], vsq[:].rearrange("p (g c) -> p g c", g=2), [Bq], [Bs])
        rstd_of(st[:, SA + 4:SA + 6], st[:, SA:SA + 2], 128, [Bs], [Bs], st[:, SA + 2:SA + 4])
        TT("dve", vsq[:], v_g[:], gv_b[:, q * 256:(q + 1) * 256], ALU.mult, [Bv, B_const], [Bq])
        TT("dve", v_n[:].rearrange("p (g c) -> p g c", g=2), vsq[:].rearrange("p (g c) -> p g c", g=2),
           st[:, SA + 4:SA + 6].unsqueeze(2).to_broadcast([128, 2, 128]), ALU.mult, [Bq, Bs], [Bn])
        if t == 8:
            out_toks.append(DMA("sp", ov_d[:, q * 256:(q + 1) * 256], v_n[:], [Bn], [], osem_v))
        CP("pool", v_nb[:], v_n[:], [Bn], [Bnb])

    def a3_stage1b(it):
        q, t = divmod(it, NT)
        v = 0 if t < 8 else 1
        S_ = a3sets[it % 5]
        u_g, v_g, v_n, vsq, v_nb, gm, gmb = S_["t"]
        Bu, Bv, Bq, Bn, Bnb, Bgm, Bgb, Bs = S_["b"]
        ps_, PS = next_pz()
        for g in range(2):
            MM(ps_[:, g * 128:(g + 1) * 128], WT[:, v, q * 2 + g, :], v_nb[:, g * 128:(g + 1) * 128], True, True, [B_const, Bnb], [PS])
        for g in range(2):
            STT(gm[:, g * 128:(g + 1) * 128], ps_[:, g * 128:(g + 1) * 128], bT[:, v, q * 2 + g:q * 2 + g + 1],
                u_g[:, g * 128:(g + 1) * 128], ALU.add, ALU.mult, [PS, B_const, Bu], [Bgm])
        ACT(vsq[:], gm[:], AF.Square, [Bgm], [Bq, B_ssg], accum_out=st[:, SSG + t * 4 + q:SSG + t * 4 + q + 1])
        TT("dve", gmb[:], gm[:], gg_b[:, q * 256:(q + 1) * 256], ALU.mult, [Bgm, B_const], [Bgb])

    def a3_stage2(it):
        q, t = divmod(it, NT)
        S_ = a3sets[it % 5]
        gmb = S_["t"][6]
        Bgb = S_["b"][6]
        pt_, PB = next_ptr()
        for g in range(2):
            TR(pt_[:, g, :], gmb[:, g * 128:(g + 1) * 128], [Bgb], [PB])
        CP("act", mT[:, 8 + q * 2:8 + q * 2 + 2, t * 128:(t + 1) * 128], pt_[:, 0:2, :], [PB], [B_mT[t]])

    norm_transpose(x_own[0:128, :], lambda g4: xnT[:, g4 * 4:(g4 + 1) * 4, 0:128], B_xnT[0], gmix, None, nt_sets_a3[0])
    NIT = 4 * NT
    for k_ in range(NIT + 4):
        if 0 <= k_ - 2 < NIT:
            a3_stage1(k_ - 2)
        if k_ < NIT:
            a3_stage0(k_)
        if 0 <= k_ - 2 < NIT:
            a3_stage1b(k_ - 2)
        if 0 <= k_ - 4 < NIT:
            a3_stage2(k_ - 4)
    RED(st[:, SCR + 16:SCR + 25], st[:, SSG:SSG + 36].rearrange("p (t q) -> p t q", q=4), [B_st, B_ssg], [B_st])
    rstd_of(st[:, RSTD_G:RSTD_G + 9], st[:, SCR + 16:SCR + 25], 1024, [B_st], [B_st], st[:, SCR + 26:SCR + 35])

    if STOP == 'A3':
        return finish()
    wkv = wslot[1][:, 0:5120].rearrange("p (k n) -> p k n", k=16)
    load_w(1, wkv, w_in_v[:, :, 512:832])
    P.barrier()
    wq = wslot[0][:, 0:8192].rearrange("p (k n) -> p k n", k=16)
    load_w(0, wq, w_in_v[:, :, 0:512])
    B_wsm = Buf("wsmall")
    wsm_sem = P.dma_sem("wsmsem")
    DMA("pool", wuq[:], w_uq.rearrange("(kc p) n -> p kc n", p=128), [], [B_wsm], wsm_sem)
    DMA("pool", wuk[:], w_uk.rearrange("(kc p) n -> p kc n", p=128), [], [B_wsm], wsm_sem)
    DMA("pool", wuv[:], w_uv.rearrange("(kc p) n -> p kc n", p=128), [], [B_wsm], wsm_sem)
    for kc in range(4):
        TS("dve", wuq[:, kc, :], wuq[:, kc, :], gqf[:, kc:kc + 1], None, ALU.mult, None, [B_wsm, B_const], [B_wsm])

    B_xpT = [Buf("xpT0"), Buf("xpT1")]
    B_ckvT, B_krT = Buf("ckvT"), Buf("krT")
    SSKR = 96

    a1sets = []
    for si_ in range(3):
        o_ = R3 + si_ * 2816
        a1sets.append({"kvf": A(o_, [128, 256], F32), "kvb": A(o_ + 1024, [128, 256], BF16), "krt": A(o_ + 1536, [128, 6, 32], F32),
                       "krf": A(o_ + 2304, [128, 64], F32), "krb": A(o_ + 2560, [128, 128], BF16),
                       "B": [Buf(f"a1_{si_}_{k}") for k in range(6)], "sc": 300 + si_ * 8})
    B_sskr = Buf("sskr")
    a1_items = [("pre", p_) for p_ in range(NPRE)] + [("own", t) for t in range(NT)]
    a1_pk = {}

    def a1_info(it):
        kind, idx = a1_items[it]
        if kind == "pre":
            i = idx % 2
            return (lambda kc, i=i: xpT[:, i, kc, :]), B_xpT[i], cs_pre[:, idx, :], idx * 128, idx, None
        return (lambda kc, t=idx: xnT[:, kc, t * 128:(t + 1) * 128]), B_xnT[idx], cs_own[:, idx, :], 1024 + idx * 128, 8 + idx, \
            slice(idx * 128, (idx + 1) * 128)

    def a1_stage0(it):
        kind, idx = a1_items[it]
        if kind == "pre":
            i = idx % 2
            norm_transpose(x_pre[idx * 128:(idx + 1) * 128, :], lambda g4, i=i: xpT[:, i, g4 * 4:(g4 + 1) * 4, :], B_xpT[i], gmix, None,
                           nt_sets[idx % 2])
        lhs_fn, B_lhs, cs_tile, keycol, kti, out_rows = a1_info(it)
        pk, PK = next_pz()
        a1_pk[it] = (pk, PK)
        for kc in range(16):
            MM(pk[:, 0:320], lhs_fn(kc), wkv[:, kc, :], kc == 0, kc == 15, [B_lhs, WB[1]], [PK])

    def a1_stage1(it):
        lhs_fn, B_lhs, cs_tile, keycol, kti, out_rows = a1_info(it)
        pk, PK = a1_pk[it]
        S_ = a1sets[it % 3]
        kvf, kvb, krt, krf, krb = S_["kvf"], S_["kvb"], S_["krt"], S_["krf"], S_["krb"]
        Bkvf, Bkvb, Bkrt, Bkrf, Bkrb, Bs = S_["B"]
        sc = S_["sc"]
        ACT(kvb[:], pk[:, 0:256], AF.Square, [PK], [Bkvb, Bs], accum_out=st[:, sc:sc + 1])
        rstd_of(st[:, sc + 2:sc + 3], st[:, sc:sc + 1], 256, [Bs], [Bs], st[:, sc + 1:sc + 2])
        cos, sin = cs_tile[:, 0:32], cs_tile[:, 32:64]
        x1, x2 = pk[:, 256:288], pk[:, 288:320]
        TT("dve", krt[:, 0, :], x1, cos, ALU.mult, [PK, B_const], [Bkrt])
        TT("dve", krt[:, 1, :], x2, sin, ALU.mult, [PK, B_const], [Bkrt])
        TT("dve", krt[:, 2, :], x1, sin, ALU.mult, [PK, B_const], [Bkrt])
        TT("dve", krt[:, 3, :], x2, cos, ALU.mult, [PK, B_const], [Bkrt])
        STT(kvf[:], pk[:, 0:256], st[:, sc + 2:sc + 3], gkv_b[:], ALU.mult, ALU.mult, [PK, Bs, B_const], [Bkvf])
        if out_rows is not None:
            out_toks.append(DMA("sp", ockv_d[out_rows, :], kvf[:], [Bkvf], [], osem_kv[it % 3]))
        CP("pool", kvb[:], kvf[:], [Bkvf], [Bkvb])
        TT("dve", krf[:, 0:32], krt[:, 0, :], krt[:, 1, :], ALU.subtract, [Bkrt], [Bkrf])
        TT("dve", krf[:, 32:64], krt[:, 2, :], krt[:, 3, :], ALU.add, [Bkrt], [Bkrf])
        if out_rows is not None:
            out_toks.append(DMA("sp", okr_d[out_rows, :], krf[:], [Bkrf], [], osem_kr[it % 3]))
        ACT(krt[:, 4:6, :].rearrange("p a b -> p (a b)"), krf[:], AF.Square, [Bkrf], [Bkrt, B_sskr],
            accum_out=st[:, SSKR + kti:SSKR + kti + 1])
        CP("pool", krb[:, 0:64], krf[:], [Bkrf], [Bkrb])
        CP("pool", krb[:, 64:128], krf[:], [Bkrf], [Bkrb])

    def a1_stage2(it):
        lhs_fn, B_lhs, cs_tile, keycol, kti, out_rows = a1_info(it)
        S_ = a1sets[it % 3]
        kvb, krb = S_["kvb"], S_["krb"]
        Bkvb, Bkrb = S_["B"][1], S_["B"][4]
        pt_, PB = next_ptr()
        for j in range(2):
            TR(pt_[:, j, :], kvb[:, j * 128:(j + 1) * 128], [Bkvb], [PB])
        TR(pt_[:, 2, :], krb[:], [Bkrb], [PB])
        CP("act", ckvT[:, :, keycol:keycol + 128], pt_[:, 0:2, :], [PB], [B_ckvT])
        CP("act", krT[:, keycol:keycol + 128], pt_[:, 2, :], [PB], [B_krT])

    NA1 = len(a1_items)
    for k_ in range(NA1 + 2):
        if k_ < NA1:
            a1_stage0(k_)
        if 0 <= k_ - 1 < NA1:
            a1_stage1(k_ - 1)
        if 0 <= k_ - 2 < NA1:
            a1_stage2(k_ - 2)

    if STOP == 'A1':
        return finish()
    P.barrier()
    B_QT = Buf("QT")
    a2sets = []
    for si_, (o_, scb) in enumerate(((R2, 64), (R4 + 2 * SLOT, 300))):
        a2sets.append({
            "qf": A(o_, [128, 8, 192], F32), "qrb": A(o_ + 6144, [128, 8, 64], BF16), "qn": A(o_ + 7168, [128, 8, 128], BF16),
            "qs": A(o_ + 9216, [128, 512], BF16), "qlTb": A(o_ + 10240, [128, 4, 128], BF16), "qsq": A(o_ + 11264, [128, 192], BF16),
            "qrt": A(o_ + 11648, [128, 4, 8, 32], F32), "sc": scb,
            "B": {k: Buf(f"a2_{si_}_{k}") for k in ("qs", "ql", "qf", "jk", "qr", "qn", "s")}})
    a2_pq = {}

    def a2_s0(t):
        S_ = a2sets[t % 2]
        Bd, sc = S_["B"], S_["sc"]
        pq, PQ = next_pz()
        for kc in range(16):
            MM(pq[:, 0:512], xnT[:, kc, t * 128:(t + 1) * 128], wq[:, kc, :], kc == 0, kc == 15, [B_xnT[t], WB[0]], [PQ])
        ACT(S_["qs"][:], pq[:, 0:512], AF.Square, [PQ], [Bd["qs"], Bd["s"]], accum_out=st[:, sc:sc + 1])
        rstd_of(st[:, sc + 2:sc + 3], st[:, sc:sc + 1], 512, [Bd["s"]], [Bd["s"]], st[:, sc + 1:sc + 2])
        TS("dve", S_["qs"][:], pq[:, 0:512], st[:, sc + 2:sc + 3], None, ALU.mult, None, [PQ, Bd["s"]], [Bd["qs"]])

    def a2_s1(t):
        S_ = a2sets[t % 2]
        Bd = S_["B"]
        qs, qlTb, qf = S_["qs"], S_["qlTb"], S_["qf"]
        pt_, PB = next_ptr()
        for j in range(4):
            TR(pt_[:, j, :], qs[:, j * 128:(j + 1) * 128], [Bd["qs"]], [PB])
        CP("act", qlTb[:], pt_, [PB], [Bd["ql"]])
        qff = qf[:].rearrange("p h d -> p (h d)")
        for c in range(3):
            pr, PR = next_pz()
            for kc in range(4):
                MM(pr[:, 0:512], qlTb[:, kc, :], wuq[:, kc, c * 512:(c + 1) * 512], kc == 0, kc == 3, [Bd["ql"], B_wsm], [PR])
            CP("act" if c != 1 else "dve", qff[:, c * 512:(c + 1) * 512], pr[:, 0:512], [PR], [Bd["qf"]])

    def a2_s2(t):
        S_ = a2sets[t % 2]
        Bd, sc = S_["B"], S_["sc"]
        qf, qsq, qrt, qn, qrb = S_["qf"], S_["qsq"], S_["qrt"], S_["qn"], S_["qrb"]
        for h in range(8):
            ACT(qsq[:, 0:192], qf[:, h, :], AF.Square, [Bd["qf"]], [Bd["jk"], Bd["s"]], accum_out=st[:, sc + 8 + h:sc + 9 + h])
        ACT(st[:, sc + 24:sc + 32], st[:, sc + 8:sc + 16], AF.Sqrt, [Bd["s"]], [Bd["s"]], scale=1.0 / 192, bias=EPS)
        RECIP(st[:, sc + 16:sc + 24], st[:, sc + 24:sc + 32], [Bd["s"]], [Bd["s"]])
        TS("dve", st[:, sc + 24:sc + 32], st[:, sc + 16:sc + 24], 192.0 ** -0.5, None, ALU.mult, None, [Bd["s"]], [Bd["s"]])
        rq = st[:, sc + 24:sc + 32]
        cos = cs_own[:, t, 0:32].unsqueeze(1).to_broadcast([128, 8, 32])
        sin = cs_own[:, t, 32:64].unsqueeze(1).to_broadcast([128, 8, 32])
        x1, x2 = qf[:, :, 128:160], qf[:, :, 160:192]
        TT("dve", qrt[:, 0], x1, cos, ALU.mult, [Bd["qf"], B_const], [Bd["qr"]])
        TT("dve", qrt[:, 1], x2, sin, ALU.mult, [Bd["qf"], B_const], [Bd["qr"]])
        TT("dve", qrt[:, 2], x1, sin, ALU.mult, [Bd["qf"], B_const], [Bd["qr"]])
        TT("dve", qrt[:, 3], x2, cos, ALU.mult, [Bd["qf"], B_const], [Bd["qr"]])
        TT("dve", qf[:, :, 128:160], qrt[:, 0], qrt[:, 1], ALU.subtract, [Bd["qr"]], [Bd["qf"]])
        TT("dve", qf[:, :, 160:192], qrt[:, 2], qrt[:, 3], ALU.add, [Bd["qr"]], [Bd["qf"]])
        TT("dve", qf[:], qf[:], rq.unsqueeze(2).to_broadcast([128, 8, 192]), ALU.mult, [Bd["s"]], [Bd["qf"]])
        gq3 = gqk[:].unsqueeze(1).to_broadcast([128, 8, 192])
        TT("dve", qn[:], qf[:, :, 0:128], gq3[:, :, 0:128], ALU.mult, [Bd["qf"], B_const], [Bd["qn"]])
        TT("dve", qrb[:], qf[:, :, 128:192], gq3[:, :, 128:192], ALU.mult, [Bd["qf"], B_const], [Bd["qn"]])

    def a2_s3(t):
        S_ = a2sets[t % 2]
        Bd = S_["B"]
        qn, qrb = S_["qn"], S_["qrb"]
        for hg in range(2):
            pt_, PB = next_ptr()
            for j in range(4):
                TR(pt_[:, j, :], qn[:, hg * 4 + j, :], [Bd["qn"]], [PB])
            CP("act" if hg == 0 else "dve", QT[:, hg * 4:(hg + 1) * 4, t * 128:(t + 1) * 128], pt_, [PB], [B_QT])
        pt_, PB = next_ptr()
        for j in range(4):
            TR(pt_[:, j, :], qrb[:, 2 * j:2 * j + 2, :].rearrange("p a b -> p (a b)"), [Bd["qn"]], [PB])
        CP("act", QrT[:, :, t * 128:(t + 1) * 128], pt_, [PB], [B_QT])

    for k_ in range(NT + 3):
        if 0 <= k_ - 2 < NT:
            a2_s2(k_ - 2)
        if 0 <= k_ - 3 < NT:
            a2_s3(k_ - 3)
        if 0 <= k_ - 1 < NT:
            a2_s1(k_ - 1)
        if k_ < NT:
            a2_s0(k_)

    if STOP == 'A2':
        return finish()
    P.barrier()
    pzpool[0] = pz2
    DMA("sp", ga_b[:], ga_d.partition_broadcast(128), [], [B_const], ldsem)
    w_out_v = w_out.rearrange("(kc p) n -> p kc n", p=128)

    def wout_view(slot):
        return wslot[slot][:, 0:8192].rearrange("p (k n) -> p k n", k=16)

    load_w(0, wout_view(0), w_out_v[:, :, 0:512])
    load_w(1, wout_view(1), w_out_v[:, :, 512:1024])

    B_KT = [Buf("KT0"), Buf("KT1")]
    B_V4 = Buf("V4")
    B_PT = [Buf("PT0"), Buf("PT1")]
    B_ksq = Buf("ksq")
    B_rk = Buf("rstdk")
    B_ob = Buf("ob")
    PSS = PSM
    PSUMS = [PSU, PSU]
    SSA = 200
    SUMS = 130
    SSKS = 140
    P.op("dve", lambda e: e.memset(tmpk[:], 1.0), writes=[B_rk])
    P.op("dve", lambda e: e.memset(PTs[:], 0.0), writes=[B_PT[0]])

    kst = [0]

    def expand_K(cT, B_cT, nkeys, h, kbuf, pss_col0):
        for k0 in range(0, nkeys, 512):
            n = min(512, nkeys - k0)
            pk, PK = next_pz()
            for kc in range(2):
                MM(pk[:, 0:n], wuk[:, kc, h * 128:(h + 1) * 128], cT[:, kc, k0:k0 + n], kc == 0, kc == 1, [B_wsm, B_cT], [PK])
            kst[0] += 1
            if STOP == f"K{kst[0]}":
                return True
            CP("dve", KT2[:, kbuf, k0:k0 + n], pk[:, 0:n], [PK], [B_KT[kbuf]])
            kst[0] += 1
            if STOP == f"K{kst[0]}":
                return True
            ACT(ksq[:, 0:n], pk[:, 0:n], AF.Square, [PK], [B_ksq])
            kst[0] += 1
            if STOP == f"K{kst[0]}":
                return True
            for j in range(0, n, 128):
                m = min(128, n - j)
                kt = (k0 + j) // 128
                col = pss_col0 + kt * 8 + h
                MM(psm[0:m, col:col + 1], ksq[:, j:j + m], ones[:, 0:1], True, True, [B_ksq, B_const], [PSS])
            kst[0] += 1
            if STOP == f"K{kst[0]}":
                return True
        return False

    def expand_V(cT, B_cT, nkeys, hg):
        for kt in range((nkeys + 127) // 128):
            m = min(128, nkeys - kt * 128)
            pv, PV = next_pz()
            for kc in range(2):
                MM(pv[0:m, 0:512], cT[:, kc, kt * 128:kt * 128 + m], wuv[:, kc, hg * 512:(hg + 1) * 512], kc == 0, kc == 1,
                   [B_cT, B_wsm], [PV])
            CP("act" if kt % 2 == 0 else "dve", V4[0:m, kt, :, :].rearrange("p a b -> p (a b)"), pv[0:m, 0:512], [PV], [B_V4])

    ob2 = [ob, A(R3 + 52096, [128, 4, 128], BF16)]
    B_ob2 = [B_ob, Buf("ob1")]
    B_fst = [Buf("fst0"), Buf("fst1")]
    fh_i = [0]

    def finish_heads(pa, PA, sums_ap, B_sums, h, tiles, sc):
        n = len(tiles)
        k_ = fh_i[0] % 2
        fh_i[0] += 1
        obk, Bobk, Bf = ob2[k_], B_ob2[k_], B_fst[k_]
        RECIP(st[:, sc:sc + n], sums_ap, [B_sums], [Bf])
        for i, tq in enumerate(tiles):
            ACT(junkb[:, 0:128], pa[:, i, :], AF.Square, [PA, Bf], [B_tmp, B_ssa], scale=st[:, sc + i:sc + i + 1],
                accum_out=st[:, SSA + tq * 8 + h:SSA + tq * 8 + h + 1])
            STT(obk[:, i, :], pa[:, i, :], st[:, sc + i:sc + i + 1], ga_b[:, h * 128:(h + 1) * 128], ALU.mult, ALU.mult,
                [PA, Bf, B_const], [Bobk])

        def later():
            pt_, PB = next_ptr()
            for i, tq in enumerate(tiles):
                TR(pt_[:, i, :], obk[:, i, :], [Bobk], [PB])
            for i, tq in enumerate(tiles):
                CP("act", mT[:, h, tq * 128:(tq + 1) * 128], pt_[:, i, :], [PB], [B_mT[tq]])
        return later

    B_ssa = Buf("ssa")
    pending_fh = []

    if STOP == 'B1a0':
        return finish()
    for hg in range(2):
        expand_V(ckvT, B_ckvT, 2048, hg)
        if STOP == 'B1a1':
            return finish()
        for hh in range(4):
            h = hg * 4 + hh
            kb = h % 2
            if expand_K(ckvT, B_ckvT, 2048, h, kb, 0):
                return finish()
            B_tk = Buf("tk")
            TT("dve", tmpk[:, 0:16, h], psm[:, 0:128].rearrange("p (k h) -> p k h", h=8)[:, :, h], st[:, SSKR:SSKR + 16], ALU.add,
               [PSS, B_st], [B_tk])
            ACT(tmpk[:, 0:16, h], tmpk[:, 0:16, h], AF.Ln, [B_tk], [B_tk], scale=1.0 / 192, bias=EPS)
            ACT(rstdk[:, 0:16, h], tmpk[:, 0:16, h], AF.Exp, [B_tk], [B_rk], scale=-0.5)
            r0 = (h % 2) * 64
            if STOP == 'B1a':
                return finish()
            for qg in range(2):
                nkt_own = 4 * qg + 4
                steps = [(kt, True) for kt in range(8)] + [(8 + kt, False) for kt in range(nkt_own)]
                pa, PA = pacc[qg], PACC[qg]
                first = [True] * 4
                q0 = qg * 512
                scol = qg * 4
                P.op("dve", lambda e, pa=pa: e.memset(pa[:], 0.0), writes=[PA])
                P.op("dve", lambda e, scol=scol: e.memset(psu[:, scol:scol + 4], 0.0), writes=[PSU])
                def emit_S(si, h=h, hh=hh, kb=kb, r0=r0, qg=qg, q0=q0, steps=steps):
                    kt, is_pre = steps[si]
                    okt = kt - 8
                    i0 = 0 if is_pre else max(0, okt - 4 * qg)
                    c0 = i0 * 128
                    ps_, PS = next_pz()
                    MM(ps_[:, c0:512], KT2[:, kb, kt * 128:(kt + 1) * 128], QT[:, h, q0 + c0:q0 + 512], True, False,
                       [B_KT[kb], B_QT], [PS])
                    MM(ps_[:, c0:512], krT[r0:r0 + 64, kt * 128:(kt + 1) * 128], QrT[r0:r0 + 64, h // 2, q0 + c0:q0 + 512], False, True,
                       [B_krT, B_QT], [PS])
                    pb = si % 2
                    if is_pre:
                        ACT(PT[:, pb, c0:512], ps_[:, c0:512], AF.Exp, [PS, B_rk, B_const], [B_PT[pb]],
                            scale=rstdk[:, kt, h:h + 1], bias=pmask[:, 0:1])
                    else:
                        ACT(PT[:, pb, c0:512], ps_[:, c0:512], AF.Exp, [PS, B_rk], [B_PT[pb]], scale=rstdk[:, kt, h:h + 1])
                        if okt >= 4 * qg:
                            P.op("dve", lambda e, pb=pb, c0=c0: e.memset(PT[64:128, pb, c0:c0 + 64], 0.0), reads=[B_PT[pb]], writes=[B_PT[pb]])

                def emit_PV(si, hh=hh, qg=qg, steps=steps, pa=pa, PA=PA, scol=scol):
                    kt, is_pre = steps[si]
                    okt = kt - 8
                    i0 = 0 if is_pre else max(0, okt - 4 * qg)
                    pb = si % 2
                    for i in range(i0, 4):
                        MM(pa[:, i, :], PT[:, pb, i * 128:(i + 1) * 128], V4[:, kt, hh, :], False, False, [B_PT[pb], B_V4], [PA], skip=True)
                        MM(psu[:, scol + i:scol + i + 1], PT[:, pb, i * 128:(i + 1) * 128], ones[:, 0:1], False, False,
                           [B_PT[pb], B_const], [PSU], skip=True)

                emit_S(0)
                for si in range(len(steps)):
                    if si + 1 < len(steps):
                        emit_S(si + 1)
                    emit_PV(si)
                if STOP == 'B1b':
                    return finish()
                pending_fh.append(finish_heads(pa, PA, psu[:, scol:scol + 4], PSU, h, [qg * 4 + i for i in range(4)], SUMS + qg * 4))
                if len(pending_fh) > 1:
                    pending_fh.pop(0)()
                if STOP == 'B1c':
                    return finish()

    while pending_fh:
        pending_fh.pop(0)()
    if STOP == 'B1':
        return finish()
    P.barrier()
    B_cck, B_ckr, B_ckvTs, B_krTs = Buf("cck"), Buf("ckr"), Buf("ckvTs"), Buf("krTs")
    csem_p = P.dma_sem("csem_p")
    B_PTs = [Buf(f"PTs{b}") for b in range(4)]
    B_scf = Buf("scf")
    PSN = PSM
    for hg_ in range(2):
        P.op("dve", lambda e, hg_=hg_: e.memset(pacc[hg_][:], 0.0), writes=[PACC[hg_]])
    P.op("dve", lambda e: e.memset(psu[:, 8:16], 0.0), writes=[PSU])
    P.op("dve", lambda e: e.memset(psm[:], 0.0), writes=[PSM])
    for b in range(4):
        DMA("pool", cck[:], cache_ckv[b].rearrange("(t p) r -> p t r", p=128), [], [B_cck], csem_p)
        DMA("sp", ckrf[:], cache_kr[b].rearrange("(t p) r -> p t r", p=128), [], [B_ckr], csem)
        for kt in range(8):
            ACT(junkb[:, 0:64], ckrf[:, kt, :], AF.Square, [B_ckr], [B_tmp, B_st], accum_out=st[:, SSKS + kt:SSKS + kt + 1])
        B_ckrd = Buf("ckrd")
        CP("pool", ckrd[:, :, 0:64], ckrf[:], [B_ckr], [B_ckrd])
        CP("pool", ckrd[:, :, 64:128], ckrf[:], [B_ckr], [B_ckrd])
        for kt in range(8):
            pt_, PB = next_ptr()
            for j in range(2):
                TR(pt_[:, j, :], cck[:, kt, j * 128:(j + 1) * 128], [B_cck], [PB])
            TR(pt_[:, 2, :], ckrd[:, kt, :], [B_ckrd], [PB])
            CP("act", ckvTs[:, :, kt * 128:(kt + 1) * 128], pt_[:, 0:2, :], [PB], [B_ckvTs])
            CP("dve", krTs[:, kt * 128:(kt + 1) * 128], pt_[:, 2, :], [PB], [B_krTs])
        nc0 = 2048 + b * 32
        CP("pool", ckvTs[:, :, 1024:1056], ckvT[:, :, nc0:nc0 + 32], [B_ckvT], [B_ckvTs])
        CP("pool", krTs[:, 1024:1056], krT[:, nc0:nc0 + 32], [B_krT], [B_krTs])
        TT("dve", ksq[0:64, 0:32], krTs[0:64, 1024:1056], krTs[0:64, 1024:1056], ALU.mult, [B_krTs], [B_ksq])
        MM(psm[0:32, 300:301], ksq[0:64, 0:32], ones[0:64, 0:1], True, True, [B_ksq, B_const], [PSN])
        CP("dve", st[0:32, SSKS + 8:SSKS + 9], psm[0:32, 300:301], [PSN], [B_st])
        def samp_A(h, b=b):
            kb = h % 2
            expand_K(ckvTs, B_ckvTs, 1056, h, kb, 128)
            B_tk = Buf("tk")
            TT("dve", tmpk[:, 0:9, h], psm[:, 128:200].rearrange("p (k h) -> p k h", h=8)[:, :, h], st[:, SSKS:SSKS + 9], ALU.add,
               [PSS, B_st], [B_tk])
            ACT(tmpk[:, 0:9, h], tmpk[:, 0:9, h], AF.Ln, [B_tk], [B_tk], scale=1.0 / 192, bias=EPS)
            ACT(rstdk[:, 0:9, h], tmpk[:, 0:9, h], AF.Exp, [B_tk], [B_rk], scale=-0.5)

        def samp_B(h, b=b):
            hg, hh = divmod(h, 4)
            kb = h % 2
            r0 = (h % 2) * 64
            qc = 1024 + b * 32
            ps_, PS = next_pz()
            psv = ps_[:, 0:288].rearrange("p (k q) -> p k q", k=9)
            for kt in range(9):
                m = 128 if kt < 8 else 32
                MM(psv[0:m, kt, :], KT2[:, kb, kt * 128:kt * 128 + m], QT[:, h, qc:qc + 32], True, False, [B_KT[kb], B_QT], [PS])
                MM(psv[0:m, kt, :], krTs[r0:r0 + 64, kt * 128:kt * 128 + m], QrT[r0:r0 + 64, h // 2, qc:qc + 32], False, True,
                   [B_krTs, B_QT], [PS])
            TT("dve", scf[:, 0:8, :], psv[:, 0:8, :], rstdk[:, 0:8, h].unsqueeze(2).to_broadcast([128, 8, 32]), ALU.mult,
               [PS, B_rk], [B_scf])
            TT("dve", scf[0:32, 8, :], psv[0:32, 8, :], rstdk[0:32, 8, h:h + 1].to_broadcast([32, 32]), ALU.mult, [PS, B_rk], [B_scf])
            ACT(PTs[:, b, 0:8, b * 32:(b + 1) * 32], scf[:, 0:8, :], AF.Exp, [B_scf], [B_PTs[b]])
            ACT(PTs[0:32, b, 8, b * 32:(b + 1) * 32], scf[0:32, 8, :], AF.Exp, [B_scf], [B_PTs[b]])
            pa, PA = pacc[hg], PACC[hg]
            for kt in range(9):
                m = 128 if kt < 8 else 32
                MM(pa[:, hh, :], PTs[0:m, b, kt, :], V4[0:m, kt, hh, :], False, False, [B_PTs[b], B_V4], [PA], skip=True)
                MM(psu[:, 8 + h:9 + h], PTs[0:m, b, kt, :], ones[0:m, 0:1], False, False, [B_PTs[b], B_const], [PSU], skip=True)

        expand_V(ckvTs, B_ckvTs, 1056, 0)
        for h in range(8):
            samp_A(h)
            if h >= 1:
                samp_B(h - 1)
            if h == 4:
                expand_V(ckvTs, B_ckvTs, 1056, 1)
        samp_B(7)
    for hg in range(2):
        for hh in range(4):
            h = hg * 4 + hh
            finish_heads(pacc[hg][:, hh:hh + 1, :], PACC[hg], psu[:, 8 + h:9 + h], PSU, h, [8], SUMS + h)()
    RED(st[:, SCR + 16:SCR + 25], st[:, SSA:SSA + 72].rearrange("p (t h) -> p t h", h=8), [B_st, B_ssa], [B_st])
    rstd_of(st[:, RSTD_A:RSTD_A + 9], st[:, SCR + 16:SCR + 25], 1024, [B_st], [B_st], st[:, SCR + 26:SCR + 35])

    if STOP == 'B2':
        return finish()
    P.barrier()
    pzpool[0] = pz6
    B_y = [Buf(f"y{t}") for t in range(NT)]
    ysem0 = P.dma_sem("ysem0")
    for t in range(NT):
        ytok = DMA("sp", y_acc[:, t, :], x_own[t * 128:(t + 1) * 128, :], [], [B_y[t]], ysem0 if t < 2 else ysem)
        if t == 1:
            ytok0 = ytok
    for t in range(NT):
        B_y[t].writers = [ytok0 if t < 2 else ytok]
    B_hnT = Buf("hnT")
    B_hs = Buf("hs")
    rflat = rbuf[:].rearrange("p a b -> p (a b)")

    hsb2 = [hsb[:, 0:256], hsb[:, 256:512]]
    B_hs2 = [Buf("hs0"), Buf("hs1")]
    B_hst = [Buf(f"hst{t}") for t in range(NT)]
    HST = 340
    hi_ = [0]

    def hn_stats(t):
        c = HST + t * 4
        for n in range(4):
            ACT(rflat[:, 0:512], y_acc[:, t, n * 512:(n + 1) * 512], AF.Square, [B_y[t]], [B_tmp, B_hst[t]], accum_out=st[:, c + n:c + n + 1])
        RED(st[:, c:c + 1], st[:, c:c + 4], [B_hst[t]], [B_hst[t]])
        rstd_of(st[:, c + 2:c + 3], st[:, c:c + 1], D, [B_hst[t]], [B_hst[t]], st[:, c + 1:c + 2])

    def hn_trans(t):
        c = HST + t * 4
        for n in range(4):
            pt_, PB = next_ptr()
            for hf in range(2):
                k_ = hi_[0] % 2
                hi_[0] += 1
                cs_ = slice(n * 512 + hf * 256, n * 512 + hf * 256 + 256)
                TS("dve", hsb2[k_], y_acc[:, t, cs_], st[:, c + 2:c + 3], None, ALU.mult, None, [B_y[t], B_hst[t]], [B_hs2[k_]])
                for j in range(2):
                    TR(pt_[:, hf * 2 + j, :], hsb2[k_][:, j * 128:(j + 1) * 128], [B_hs2[k_]], [PB])
            TT("dve", xnT[:, n * 4:(n + 1) * 4, t * 128:(t + 1) * 128], pt_, gffn[:, n * 4:(n + 1) * 4].unsqueeze(2).to_broadcast([128, 4, 128]),
               ALU.mult, [PB, B_const], [B_hnT])

    w_up_v = w_up.rearrange("(kc p) n -> p kc n", p=128)
    w_dn_v = w_down.rearrange("(j p) n -> p j n", p=128)

    def up_view(slot):
        return wslot[slot][:, 0:8192].rearrange("p (k n) -> p k n", k=16)

    def dn_view(slot):
        return wslot[slot][:, 0:8192].rearrange("p (j n) -> p j n", j=4)

    def load_up(c):
        load_w((2 * c) % 3, up_view((2 * c) % 3), w_up_v[:, :, c * 512:(c + 1) * 512])

    def load_dn(c):
        load_w((2 * c + 1) % 3, dn_view((2 * c + 1) % 3), w_dn_v[:, c * 4:(c + 1) * 4, :])

    for n in range(4):
        slot = n % 3
        if n == 0:
            load_w(2, wout_view(2), w_out_v[:, :, 1024:1536])
        if n == 1:
            load_w(0, wout_view(0), w_out_v[:, :, 1536:2048])
        if n == 2:
            load_dn(0)
        wo = wout_view(slot)
        cols = slice(n * 512, (n + 1) * 512)
        for t in range(NT):
            pa_, PA_ = next_pz()
            for kc in range(8):
                MM(pa_[:, 0:512], mT[:, kc, t * 128:(t + 1) * 128], wo[:, kc, :], kc == 0, kc == 7, [B_mT[t], WB[slot]], [PA_])
            pg_, PG_ = next_pz()
            for kc in range(8, 16):
                MM(pg_[:, 0:512], mT[:, kc, t * 128:(t + 1) * 128], wo[:, kc, :], kc == 8, kc == 15, [B_mT[t], WB[slot]], [PG_])
            STT(y_acc[:, t, cols], pa_[:, 0:512], st[:, RSTD_A + t:RSTD_A + t + 1], y_acc[:, t, cols], ALU.mult, ALU.add,
                [PA_, B_st], [B_y[t]])
            STT(y_acc[:, t, cols], pg_[:, 0:512], st[:, RSTD_G + t:RSTD_G + t + 1], y_acc[:, t, cols], ALU.mult, ALU.add,
                [PG_, B_st], [B_y[t]])
            if n == 3:
                hn_stats(t)
                if t >= 2:
                    hn_trans(t - 2)
    load_up(0)
    hn_trans(NT - 2)
    hn_trans(NT - 1)
    if STOP == 'C':
        for t in range(NT):
            out_toks.append(DMA("sp", y_d[t * 128:(t + 1) * 128, :], y_acc[:, t, :], [B_y[t]], [], osem_y))
        return finish()
    P.barrier()
    B_aT = Buf("aT")
    B_r = [Buf("r0"), Buf("r1")]
    NCH = DFF // 512
    ri = 0
    for c in range(NCH):
        if c + 1 < NCH:
            load_up(c + 1)
        su, sd = (2 * c) % 3, (2 * c + 1) % 3
        wup, wdn = up_view(su), dn_view(sd)
        for j in range(4):
            for tg in range(3):
                pu, PU = next_pz()
                for kc in range(16):
                    MM(pu[:, 0:384], wup[:, kc, j * 128:(j + 1) * 128], xnT[:, kc, tg * 384:(tg + 1) * 384], kc == 0, kc == 15,
                       [WB[su], B_hnT], [PU])
                rb = ri % 2
                ri += 1
                ACT(rbuf[:, rb, :], pu[:, 0:384], AF.Relu, [PU], [B_r[rb]])
                TT("dve", aT[:, j, tg * 384:(tg + 1) * 384], rbuf[:, rb, :], rbuf[:, rb, :], ALU.mult, [B_r[rb]], [B_aT])
        if c + 1 < NCH:
            load_dn(c + 1)
        for t in range(NT):
            for n in range(4):
                pd, PD = next_pz()
                for j in range(4):
                    MM(pd[:, 0:512], aT[:, j, t * 128:(t + 1) * 128], wdn[:, j, n * 512:(n + 1) * 512], j == 0, j == 3, [B_aT, WB[sd]], [PD])
                cols = slice(n * 512, (n + 1) * 512)
                TT("dve", y_acc[:, t, cols], pd[:, 0:512], y_acc[:, t, cols], ALU.add, [PD], [B_y[t]])
            if c == NCH - 1:
                out_toks.append(DMA("sp", y_d[t * 128:(t + 1) * 128, :], y_acc[:, t, :], [B_y[t]], [], osem_y))
    return finish()


_NC_CACHE = {}


def _rope_table(pos):
    half = 32
    inv = 10000.0 ** (-np.arange(half, dtype=np.float64) / half)
    ang = pos.astype(np.float64)[:, None] * inv[None, :]
    return np.concatenate([np.cos(ang), np.sin(ang)], axis=-1).astype(np.float32)


def kernel(x_prompt, x_sample, cache_c_kv, cache_k_rope, norm_mix, w_in, q_lat_norm, kv_lat_norm,
           w_uq, w_uk, w_uv, q_norm_nope, q_norm_rope, k_norm_nope, k_norm_rope, v_norm,
           w_spatial, b_spatial, out_norm_attn, out_norm_gmlp, w_out, norm_ffn, w_up, w_down):
    f = lambda a: np.ascontiguousarray(np.asarray(a, dtype=np.float32))
    x_prompt, x_sample = f(x_prompt), f(x_sample)
    cache_c_kv, cache_k_rope = f(cache_c_kv), f(cache_k_rope)
    if "nc" not in _NC_CACHE:
        _NC_CACHE["nc"] = build_program()
    nc = _NC_CACHE["nc"]

    ws = f(w_spatial)[0]
    bs = f(b_spatial)[0]
    wsT = np.zeros((128, 2, 8, 128), np.float32)
    wsT[:, 0] = ws.transpose(2, 0, 1)
    for j in range(4):
        wsT[32 * j:32 * j + 32, 1, :, 32 * j:32 * j + 32] = ws[:, :32, :32].transpose(2, 0, 1)
    bT = np.zeros((128, 2, 8), np.float32)
    bT[:, 0] = bs.T
    bT[:, 1] = np.tile(bs[:, :32].T, (4, 1))
    sidx = np.arange(128)
    trilT = (sidx[:, None] <= sidx[None, :]).astype(np.float32)
    common = {
        "w_in": f(w_in)[0], "w_uq": f(w_uq)[0], "w_uk": f(w_uk)[0], "w_uv": f(w_uv)[0],
        "w_out": f(w_out)[0], "w_up": f(w_up)[0], "w_down": f(w_down)[0],
        "gmix_fm": f(f(norm_mix)[0].reshape(16, 128).T), "gffn_fm": f(f(norm_ffn)[0].reshape(16, 128).T),
        "gq_fm": f(f(q_lat_norm)[0].reshape(4, 128).T),
        "kv_lat_norm": f(kv_lat_norm).reshape(1, 256),
        "q_norm_nope": f(q_norm_nope).reshape(1, 128), "q_norm_rope": f(q_norm_rope).reshape(1, 32),
        "k_norm_nope": f(k_norm_nope).reshape(1, 128), "k_norm_rope": f(k_norm_rope).reshape(1, 32),
        "v_norm": f(v_norm).reshape(1, 1024),
        "out_norm_attn": f(out_norm_attn).reshape(1, 1024), "out_norm_gmlp": f(out_norm_gmlp).reshape(1, 1024),
        "wsT": wsT, "trilT": trilT, "bT": bT, "ident": np.eye(128, dtype=np.float32),
    }
    cs_pre = _rope_table(np.arange(1024))
    in_maps = []
    for c in range(8):
        b, half = c // 2, c % 2
        xo = np.concatenate([x_prompt[b, half * 1024:(half + 1) * 1024], x_sample[4 * c:4 * c + 4].reshape(128, D)], axis=0)
        pos = np.concatenate([half * 1024 + np.arange(1024), np.tile(1024 + np.arange(32), 4)])
        m = dict(common)
        m["x_own"] = f(xo)
        m["x_pre"] = f(x_prompt[b, 0:1024])
        m["cs_own"] = _rope_table(pos)
        m["cs_pre"] = cs_pre
        m["pmask"] = np.full((128, 1), 0.0 if half == 1 else -30000.0, np.float32)
        m["cache_ckv"] = f(cache_c_kv[0, 4 * c:4 * c + 4])
        m["cache_kr"] = f(cache_k_rope[0, 4 * c:4 * c + 4])
        in_maps.append(m)
    res = run_bass_kernel_spmd(nc, in_maps, core_ids=list(range(8)))
    y_p = np.zeros((4, 2048, D), np.float32)
    y_s = np.zeros((32, 32, D), np.float32)
    ckv_p = np.zeros((1, 4, 2048, 256), np.float32)
    kr_p = np.zeros((1, 4, 2048, 64), np.float32)
    ckv_s = np.zeros((1, 32, 32, 256), np.float32)
    kr_s = np.zeros((1, 32, 32, 64), np.float32)
    v_s = np.zeros((1, 32, 32, 1024), np.float32)
    for c in range(8):
        b, half = c // 2, c % 2
        r = res.results[c]
        sl = slice(half * 1024, (half + 1) * 1024)
        y_p[b, sl] = r["y"][0:1024]
        y_s[4 * c:4 * c + 4] = r["y"][1024:].reshape(4, 32, D)
        ckv_p[0, b, sl] = r["o_ckv"][0:1024]
        ckv_s[0, 4 * c:4 * c + 4] = r["o_ckv"][1024:].reshape(4, 32, 256)
        kr_p[0, b, sl] = r["o_kr"][0:1024]
        kr_s[0, 4 * c:4 * c + 4] = r["o_kr"][1024:].reshape(4, 32, 64)
        v_s[0, 4 * c:4 * c + 4] = r["o_v"].reshape(4, 32, 1024)
    return (y_p, y_s, ckv_p, kr_p, ckv_s, kr_s, v_s)
```

```python
import numpy as np
import concourse.bass as bass
import concourse.mybir as mybir
from concourse.bass_utils import run_bass_kernel_spmd

F32 = mybir.dt.float32
BF16 = mybir.dt.bfloat16
AF = mybir.ActivationFunctionType
ALU = mybir.AluOpType
AX = mybir.AxisListType

D = 2048
NT = 9
TOK = NT * 128
NPRE = 8
EPS = 1e-6
DFF = 8192


class Tok:
    __slots__ = ("sem", "val", "eng", "op")

    def __init__(self, sem, val, eng, op):
        self.sem, self.val, self.eng, self.op = sem, val, eng, op


class Buf:
    __slots__ = ("name", "writers", "readers", "excl")

    def __init__(self, name="", excl=False):
        self.name = name
        self.writers = []
        self.readers = []
        self.excl = excl


class Op:
    __slots__ = ("fn", "deps", "tok", "needs_inc", "dma_sem")


class Eng:
    def __init__(self, name, sem, in_order=False):
        self.name, self.sem, self.ops, self.in_order = name, sem, [], in_order


class Prog:
    def __init__(self, nc):
        self.nc = nc
        self.stack = []
        self.engs = {}
        self.dma_sems = []

    def enter(self, cm):
        v = cm.__enter__()
        self.stack.append(cm)
        return v

    def close(self):
        while self.stack:
            self.stack.pop().__exit__(None, None, None)

    def sem(self, name):
        return self.enter(self.nc.semaphore(name))

    def add_engine(self, key, in_order=False):
        e = Eng(key, self.sem("s_" + key), in_order)
        self.engs[key] = e
        return e

    def dma_sem(self, name):
        s = [self.sem(name), 0]
        self.dma_sems.append(s)
        return s

    def op(self, eng, fn, reads=(), writes=(), deps=(), dma_sem=None):
        e = self.engs[eng]
        o = Op()
        o.fn = fn
        o.needs_inc = False
        o.dma_sem = dma_sem
        d = list(deps)
        for b in reads:
            d.extend(b.writers)
            if b.excl:
                d.extend(t for t in b.readers if t.eng is not e)
        for b in writes:
            d.extend(b.readers)
            d.extend(b.writers)
        o.deps = d
        if dma_sem is not None:
            dma_sem[1] += 16
            o.tok = Tok(dma_sem, dma_sem[1], e, o)
        else:
            o.tok = Tok(None, None, e, o)
        for b in reads:
            b.readers.append(o.tok)
            if len(b.readers) > 1:
                b.readers = _compress(b.readers)
        for b in writes:
            if b.readers:
                b.writers = [o.tok]
                b.readers = []
            else:
                b.writers.append(o.tok)
                if len(b.writers) > 1:
                    b.writers = _compress(b.writers)
        e.ops.append(o)
        return o.tok

    def barrier(self):
        toks = []
        for e in self.engs.values():
            for o in reversed(e.ops):
                if o.dma_sem is None and o.fn is not None:
                    toks.append(o.tok)
                    break
        for s in self.dma_sems:
            if s[1] > 0:
                toks.append(Tok(s, s[1], None, None))
        for k in self.engs:
            self.op(k, None, deps=toks)

    def finalize(self):
        for e in self.engs.values():
            for o in e.ops:
                for t in o.deps:
                    if t.sem is None:
                        if t.eng is e and e.in_order:
                            continue
                        t.op.needs_inc = True
        for e in self.engs.values():
            c = 0
            for o in e.ops:
                if o.dma_sem is None and o.needs_inc:
                    c += 1
                    o.tok.val = c

    def emit(self, ekey, h):
        e = self.engs[ekey]
        waited = {}
        for o in e.ops:
            need = {}
            for t in o.deps:
                if t.sem is None:
                    if t.eng is e and e.in_order:
                        continue
                    key = ("c", t.eng.name)
                    semh = t.eng.sem
                else:
                    key = ("d", id(t.sem))
                    semh = t.sem[0]
                v = t.val
                if v > waited.get(key, 0) and v > need.get(key, (None, 0))[1]:
                    need[key] = (semh, v)
            for key, (semh, v) in need.items():
                h.wait_ge(semh, v)
                waited[key] = v
            if o.fn is None:
                continue
            ins = o.fn(h)
            if o.dma_sem is not None:
                ins.then_inc(o.dma_sem[0], 16)
            elif o.needs_inc:
                ins.then_inc(e.sem, 1)


def _compress(toks):
    best = {}
    for t in toks:
        k = ("c", t.eng.name) if t.sem is None else ("d", id(t.sem))
        best[k] = t
    return list(best.values())


def build_program():
    nc = bass.Bass("TRN2", target_bir_lowering=False)
    P = Prog(nc)
    for k in ["sp", "act", "dve", "pool"]:
        P.add_engine(k)
    P.add_engine("pe", in_order=True)

    def din(name, shape):
        return nc.dram_tensor(name, list(shape), F32, kind="ExternalInput").ap()

    def dout(name, shape):
        return nc.dram_tensor(name, list(shape), F32, kind="ExternalOutput").ap()

    x_own = din("x_own", [TOK, D])
    x_pre = din("x_pre", [1024, D])
    cs_own_d = din("cs_own", [TOK, 64])
    cs_pre_d = din("cs_pre", [1024, 64])
    pmask_d = din("pmask", [128, 1])
    cache_ckv = din("cache_ckv", [4, 1024, 256])
    cache_kr = din("cache_kr", [4, 1024, 64])
    w_in = din("w_in", [D, 2880])
    w_uq = din("w_uq", [512, 1536])
    w_uk = din("w_uk", [256, 1024])
    w_uv = din("w_uv", [256, 1024])
    w_out = din("w_out", [D, D])
    w_up = din("w_up", [D, DFF])
    w_down = din("w_down", [DFF, D])
    gmix_d = din("gmix_fm", [128, 16])
    gffn_d = din("gffn_fm", [128, 16])
    gqf_d = din("gq_fm", [128, 4])
    gkv_d = din("kv_lat_norm", [1, 256])
    qn_nope_d = din("q_norm_nope", [1, 128])
    qn_rope_d = din("q_norm_rope", [1, 32])
    kn_nope_d = din("k_norm_nope", [1, 128])
    kn_rope_d = din("k_norm_rope", [1, 32])
    gv_d = din("v_norm", [1, 1024])
    ga_d = din("out_norm_attn", [1, 1024])
    gg_d = din("out_norm_gmlp", [1, 1024])
    wsT_d = din("wsT", [128, 2, 8, 128])
    trilT_d = din("trilT", [128, 128])
    bT_d = din("bT", [128, 2, 8])
    ident_d = din("ident", [128, 128])
    y_d = dout("y", [TOK, D])
    ockv_d = dout("o_ckv", [TOK, 256])
    okr_d = dout("o_kr", [TOK, 64])
    ov_d = dout("o_v", [128, 1024])

    BASE = 16512
    P0, R1, R2, R3, R4 = 0, 14336, 51200, 88064, 161792
    SLOT = 16384
    cnt = [0]

    def A(off, shape, dt):
        cnt[0] += 1
        return nc.alloc_sbuf_tensor_at(f"t{cnt[0]}", list(shape), dt, offset=BASE + off)

    ident = A(P0 + 0, [128, 128], BF16)
    ones = A(P0 + 256, [128, 2], BF16)
    gmix = A(P0 + 320, [128, 16], F32)
    gffn = A(P0 + 384, [128, 16], F32)
    gqf = A(P0 + 448, [128, 4], F32)
    gkv_b = A(P0 + 512, [128, 256], F32)
    gqk = A(P0 + 1536, [128, 192], F32)
    cs_own = A(P0 + 2304, [128, NT, 64], F32)
    cs_pre = A(P0 + 4608, [128, NPRE, 64], F32)
    bT = A(P0 + 6656, [128, 2, 8], F32)
    WT = A(P0 + 6784, [128, 2, 8, 128], BF16)
    pmask = A(P0 + 10880, [128, 1], F32)
    st = A(P0 + 10944, [128, 384], F32)
    rbuf = A(P0 + 12480, [128, 2, 384], BF16)
    xnT = A(R1, [128, 16, TOK], BF16)
    KT2 = A(R1, [128, 2, 2048], BF16)
    V4 = A(R1 + 8192, [128, 16, 4, 128], BF16)
    PT = A(R1 + 24576, [128, 2, 512], BF16)
    ksq = A(R1 + 26624, [128, 512], BF16)
    PTs = A(R1 + 27648, [128, 4, 9, 128], BF16)
    mT = A(R2, [128, 16, TOK], BF16)
    aT = A(R2, [128, 4, TOK], BF16)
    xt = A(R2, [128, 2048], F32)
    xs = A(R2 + 8192, [128, 2048], BF16)
    u_g = A(R2 + 12288, [128, 256], F32)
    v_g = A(R2 + 13312, [128, 256], F32)
    v_n = A(R2 + 14336, [128, 256], F32)
    vsq = A(R2 + 15360, [128, 256], F32)
    v_nb = A(R2 + 16384, [128, 256], BF16)
    gm = A(R2 + 16896, [128, 256], F32)
    gmb = A(R2 + 17920, [128, 256], BF16)
    kvf = A(R2 + 12288, [128, 256], F32)
    kvb = A(R2 + 13312, [128, 256], BF16)
    krt = A(R2 + 13824, [128, 6, 32], F32)
    krf = A(R2 + 14592, [128, 64], F32)
    krb = A(R2 + 14848, [128, 128], BF16)
    qf = A(R2, [128, 8, 192], F32)
    qrb = A(R2 + 6144, [128, 8, 64], BF16)
    qn = A(R2 + 7168, [128, 8, 128], BF16)
    qs = A(R2 + 9216, [128, 512], BF16)
    qlTb = A(R2 + 10240, [128, 4, 128], BF16)
    qsq = A(R2 + 11264, [128, 1536], BF16)
    qrt = A(R2 + 14336, [128, 4, 8, 32], F32)
    y_acc = A(R3, [128, NT, D], F32)
    QT = A(R3, [128, 8, TOK], BF16)
    QrT = A(R3 + 18432, [128, 4, TOK], BF16)
    NK = 2048 + 128
    ckvT = A(R3 + 27648, [128, 2, NK], BF16)
    krT = A(R3 + 36352, [128, NK], BF16)
    wuk = A(R3 + 40704, [128, 2, 1024], BF16)
    wuv = A(R3 + 44800, [128, 2, 1024], BF16)
    wuq = A(R3 + 48896, [128, 4, 1536], BF16)
    ob = A(R3 + 48896, [128, 4, 128], BF16)
    junkb = A(R3 + 49920, [128, 512], BF16)
    scf = A(R3 + 50944, [128, 9, 32], F32)
    gv_b = A(R3 + 61184, [128, 1024], F32)
    gg_b = A(R3 + 65280, [128, 1024], F32)
    ga_b = A(R3 + 65280, [128, 1024], F32)
    rstdk = A(R3 + 69376, [128, 17, 8], F32)
    tmpk = A(R3 + 69920, [128, 17, 8], F32)
    wslot = [A(R4 + i * SLOT, [128, 8192], BF16) for i in range(3)]
    WB = [Buf(f"W{i}") for i in range(3)]
    wsem = [P.dma_sem(f"wsem{i}") for i in range(3)]
    hsb = A(210944, [128, 512], BF16)
    xpT = A(R4 + 2 * SLOT, [128, 2, 16, 128], BF16)
    cck = A(R4 + 2 * SLOT, [128, 8, 256], BF16)
    ckrf = A(R4 + 2 * SLOT + 4096, [128, 8, 64], F32)
    ckrd = A(R4 + 2 * SLOT + 6144, [128, 8, 128], BF16)
    ckvTs = A(R4 + 2 * SLOT + 8192, [128, 2, 1056], BF16)
    krTs = A(R4 + 2 * SLOT + 12416, [128, 1056], BF16)

    pz = [P.enter(nc.psum_tensor(f"pz{i}", [128, 512], F32)) for i in range(2)]
    psu = P.enter(nc.psum_tensor("psu", [128, 512], F32))
    PSU = Buf("psu", True)
    ptrs = [P.enter(nc.psum_tensor(f"ptr{i}", [128, 8, 128], BF16)) for i in range(2)]
    psm = P.enter(nc.psum_tensor("psm", [128, 512], F32))
    pacc = [P.enter(nc.psum_tensor(f"pacc{i}", [128, 4, 128], F32)) for i in range(2)]
    PZ = [Buf(f"pz{i}", True) for i in range(2)]
    PTR = [Buf("ptr0", True), Buf("ptr1", True)]
    PSM = Buf("psm", True)
    PACC = [Buf("pacc0", True), Buf("pacc1", True)]
    pzi = [0]
    ptri = [0]

    pz2 = [(pz[0], PZ[0]), (pz[1], PZ[1])]
    pz6 = pz2 + [(pacc[0][:].rearrange("p a b -> p (a b)"), PACC[0]), (pacc[1][:].rearrange("p a b -> p (a b)"), PACC[1]),
                 (psm, PSM), (psu, PSU)]
    pzpool = [pz6]

    def next_pz():
        pool = pzpool[0]
        i = pzi[0] % len(pool)
        pzi[0] += 1
        return pool[i]

    def next_ptr():
        i = ptri[0] % 2
        ptri[0] += 1
        return ptrs[i][:, 0:4, :], PTR[i]

    ldsem = P.dma_sem("ldsem")
    xsem = P.dma_sem("xsem")
    osem_kv = [P.dma_sem(f"osem_kv{i}") for i in range(3)]
    osem_kr = [P.dma_sem(f"osem_kr{i}") for i in range(3)]
    osem_v = P.dma_sem("osem_v")
    osem_y = P.dma_sem("osem_y")
    ysem = P.dma_sem("ysem")
    csem = P.dma_sem("csem")
    out_toks = []

    def MM(out, lhsT, rhs, start, stop, reads, writes, skip=False):
        if skip:
            return P.op("pe", lambda e: e.matmul(out, lhsT=lhsT, rhs=rhs, start=False, stop=False, skip_group_check=True),
                        reads=reads, writes=writes)
        return P.op("pe", lambda e: e.matmul(out, lhsT=lhsT, rhs=rhs, start=start, stop=stop), reads=reads, writes=writes)

    def TR(out, in_, reads, writes):
        return P.op("pe", lambda e: e.transpose(out=out, in_=in_, identity=ident[:]), reads=reads + [B_const], writes=writes)

    def ACT(out, in_, func, reads, writes, scale=1.0, bias=0.0, accum_out=None):
        if accum_out is None:
            return P.op("act", lambda e: e.activation(out=out, in_=in_, func=func, bias=bias, scale=scale), reads=reads, writes=writes)
        return P.op("act", lambda e: e.activation(out=out, in_=in_, func=func, bias=bias, scale=scale, accum_out=accum_out), reads=reads, writes=writes)

    def TT(eng, out, in0, in1, op, reads, writes):
        return P.op(eng, lambda e: e.tensor_tensor(out=out, in0=in0, in1=in1, op=op), reads=reads, writes=writes)

    def TS(eng, out, in0, s1, s2, op0, op1, reads, writes):
        if s2 is None:
            return P.op(eng, lambda e: e.tensor_scalar(out=out, in0=in0, scalar1=s1, scalar2=None, op0=op0), reads=reads, writes=writes)
        return P.op(eng, lambda e: e.tensor_scalar(out=out, in0=in0, scalar1=s1, scalar2=s2, op0=op0, op1=op1), reads=reads, writes=writes)

    def STT(out, in0, scalar, in1, op0, op1, reads, writes):
        return P.op("dve", lambda e: e.scalar_tensor_tensor(out=out, in0=in0, scalar=scalar, in1=in1, op0=op0, op1=op1), reads=reads, writes=writes)

    def CP(eng, out, in_, reads, writes):
        if eng == "act":
            return P.op("act", lambda e: e.copy(out=out, in_=in_), reads=reads, writes=writes)
        return P.op(eng, lambda e: e.tensor_copy(out=out, in_=in_), reads=reads, writes=writes)

    def RED(out, in_, reads, writes):
        return P.op("dve", lambda e: e.tensor_reduce(out=out, in_=in_, axis=AX.X, op=ALU.add), reads=reads, writes=writes)

    def RECIP(out, in_, reads, writes):
        return P.op("dve", lambda e: e.reciprocal(out=out, in_=in_), reads=reads, writes=writes)

    def DMA(eng, out, in_, reads, writes, sem):
        return P.op(eng, lambda e: e.dma_start(out=out, in_=in_), reads=reads, writes=writes, dma_sem=sem)

    def rstd_of(out, ss, n, reads, writes, tmp):
        ACT(tmp, ss, AF.Sqrt, reads, writes, scale=1.0 / n, bias=EPS)
        RECIP(out, tmp, writes, writes)

    w_in_v = w_in.rearrange("(kc p) n -> p kc n", p=128)

    def wA3_view(slot):
        return wslot[slot][:, 0:8192].rearrange("p (k n) -> p k n", k=16)

    def load_A3(q, slot):
        v = wA3_view(slot)
        u0 = 832 + q * 256
        v0 = 1856 + q * 256
        for k0 in (0, 8):
            DMA("pool", v[:, k0:k0 + 8, 0:256], w_in_v[:, k0:k0 + 8, u0:u0 + 256], [], [WB[slot]], wsem[slot])
            DMA("pool", v[:, k0:k0 + 8, 256:512], w_in_v[:, k0:k0 + 8, v0:v0 + 256], [], [WB[slot]], wsem[slot])

    B_const = Buf("const")
    B_st = Buf("st")
    stage = A(R2, [128, 2, 8, 128], F32)
    stage2 = A(R2 + 8192, [128, 128], F32)
    stage3 = A(R2 + 8704, [128, 128], F32)
    stage4 = A(R2 + 9216, [128, 2, 192], F32)
    B_stage = Buf("stage")
    ldsem2 = P.dma_sem("ldsem2")
    DMA("sp", stage2[:], ident_d[:, :], [], [], ldsem)
    DMA("act", stage3[:], trilT_d[:, :], [], [], ldsem2)
    DMA("sp", stage[:], wsT_d[:, :, :, :], [], [], ldsem)
    DMA("act", gmix[:], gmix_d[:, :], [], [], ldsem2)
    DMA("sp", gffn[:], gffn_d[:, :], [], [], ldsem)
    DMA("act", gqf[:], gqf_d[:, :], [], [], ldsem2)
    DMA("sp", gkv_b[:], gkv_d.partition_broadcast(128), [], [], ldsem)
    DMA("act", stage4[:, 0, 0:128], qn_nope_d.partition_broadcast(128), [], [], ldsem2)
    DMA("sp", stage4[:, 0, 128:160], qn_rope_d.partition_broadcast(128), [], [], ldsem)
    DMA("act", stage4[:, 0, 160:192], qn_rope_d.partition_broadcast(128), [], [], ldsem2)
    DMA("sp", stage4[:, 1, 0:128], kn_nope_d.partition_broadcast(128), [], [], ldsem)
    DMA("act", stage4[:, 1, 128:160], kn_rope_d.partition_broadcast(128), [], [], ldsem2)
    DMA("sp", stage4[:, 1, 160:192], kn_rope_d.partition_broadcast(128), [], [], ldsem)
    DMA("act", cs_own[:], cs_own_d.rearrange("(t p) c -> p t c", p=128), [], [], ldsem2)
    DMA("sp", cs_pre[:], cs_pre_d.rearrange("(t p) c -> p t c", p=128), [], [], ldsem)
    DMA("act", bT[:], bT_d[:, :, :], [], [], ldsem2)
    DMA("sp", pmask[:], pmask_d[:, :], [], [], ldsem)
    DMA("act", gv_b[:], gv_d.partition_broadcast(128), [], [], ldsem2)
    DMA("sp", gg_b[:], gg_d.partition_broadcast(128), [], [], ldsem)
    load_A3(0, 0)
    load_A3(1, 1)
    P.op("dve", lambda e: e.memset(st[:], 0.0), writes=[B_st])
    P.barrier()
    CP("dve", ident[:], stage2[:], [B_stage], [B_const])
    P.op("dve", lambda e: e.memset(ones[:], 1.0), writes=[B_const])
    for v in range(2):
        TT("dve", WT[:, v], stage[:, v], stage3[:].unsqueeze(1).to_broadcast([128, 8, 128]), ALU.mult, [B_stage], [B_const])
    TT("dve", gqk[:], stage4[:, 0, :], stage4[:, 1, :], ALU.mult, [B_stage], [B_const])

    w_in_v = w_in.rearrange("(kc p) n -> p kc n", p=128)

    def load_w(slot, dst_view, src_view, nsplit=2):
        K = dst_view.shape[1]
        step = (K + nsplit - 1) // nsplit
        for k0 in range(0, K, step):
            k1 = min(K, k0 + step)
            DMA("pool", dst_view[:, k0:k1, :], src_view[:, k0:k1, :], [], [WB[slot]], wsem[slot])

    import os as _os
    STOP = _os.environ.get('MK_STOP', '')

    def finish():
        P.op("sp", lambda e: e.nop(), deps=out_toks)

        P.finalize()
        with nc.Block() as block:
            @block.sync
            def _(e):
                P.emit("sp", e)

            @block.scalar
            def _(e):
                P.emit("act", e)

            @block.vector
            def _(e):
                P.emit("dve", e)

            @block.gpsimd
            def _(e):
                P.emit("pool", e)

            @block.tensor
            def _(e):
                P.emit("pe", e)
        P.close()
        return nc

    B_xt, B_xs, B_xnT = Buf("xt"), Buf("xs"), [Buf(f"xnT{t}") for t in range(NT)]
    B_mT = [Buf(f"mT{t}") for t in range(NT)]
    B_tmp = Buf("tmpA")
    SSG = 0
    RSTD_A = 40
    RSTD_G = 50
    SCR = 64

    xsem1 = P.dma_sem("xsem1")
    nt_sets = [
        {"xt": xt, "xs": xs, "Bxt": B_xt, "Bxs": B_xs, "Bs": Buf("nts0"), "sc": 336, "sem": xsem},
        {"xt": A(R3 + 8448, [128, 2048], F32), "xs": A(R3 + 8448 + 8192, [128, 2048], BF16), "Bxt": Buf("xt1"), "Bxs": Buf("xs1"),
         "Bs": Buf("nts1"), "sc": 376, "sem": xsem1},
    ]

    xsem2 = P.dma_sem("xsem2")
    nt_sets_a3 = [nt_sets[0],
                  {"xt": A(R3 + 30720, [128, 2048], F32), "xs": A(R3 + 30720 + 8192, [128, 2048], BF16), "Bxt": Buf("xt2"), "Bxs": Buf("xs2"),
                   "Bs": Buf("nts2"), "sc": 380, "sem": xsem2}]

    def nt_pre(src_dram_tile, S_):
        xt_, xs_, Bxt, Bxs, Bs, sc = S_["xt"], S_["xs"], S_["Bxt"], S_["Bxs"], S_["Bs"], S_["sc"]
        DMA("sp", xt_[:], src_dram_tile, [], [Bxt], S_["sem"])
        ACT(xs_[:], xt_[:], AF.Square, [Bxt], [Bxs, Bs], accum_out=st[:, sc:sc + 1])
        rstd_of(st[:, sc + 2:sc + 3], st[:, sc:sc + 1], D, [Bs], [Bs], st[:, sc + 1:sc + 2])
        TS("dve", xs_[:], xt_[:], st[:, sc + 2:sc + 3], None, ALU.mult, None, [Bxt, Bs], [Bxs])

    def nt_group(g4, dstT, B_dst, gain_fm, S_):
        xs_, Bxs = S_["xs"], S_["Bxs"]
        pt_, PB = next_ptr()
        for j in range(4):
            kc = g4 * 4 + j
            TR(pt_[:, j, :], xs_[:, kc * 128:(kc + 1) * 128], [Bxs], [PB])
        TT("dve", dstT(g4), pt_, gain_fm[:, g4 * 4:(g4 + 1) * 4].unsqueeze(2).to_broadcast([128, 4, 128]), ALU.mult,
           [PB, B_const], [B_dst])

    def norm_transpose(src_dram_tile, dstT, B_dst, gain_fm, t_writes, S_=None):
        S_ = S_ or nt_sets[0]
        nt_pre(src_dram_tile, S_)
        for g4 in range(4):
            nt_group(g4, dstT, B_dst, gain_fm, S_)

    P.barrier()
    if STOP == 'C0':
        return finish()
    a3sets = []
    for si_ in range(5):
        o_ = R3 + si_ * 6144
        tens = (A(o_, [128, 256], F32), A(o_ + 1024, [128, 256], F32), A(o_ + 2048, [128, 256], F32), A(o_ + 3072, [128, 256], F32),
                A(o_ + 4096, [128, 256], BF16), A(o_ + 4608, [128, 256], F32), A(o_ + 5632, [128, 256], BF16))
        a3sets.append({"t": tens, "b": tuple(Buf(f"a3_{si_}_{k}") for k in range(8)), "sa": 280 + si_ * 8})
    a3i = [0]
    B_ssg = Buf("ssg")
    def a3_stage0(it):
        q, t = divmod(it, NT)
        slot = q % 3
        if t == 0 and q == 1:
            load_A3(2, 2)
        if t == 0 and q == 2:
            load_A3(3, 0)
        wv = wA3_view(slot)
        S_ = a3sets[it % 5]
        u_g, v_g, v_n, vsq, v_nb, gm, gmb = S_["t"]
        Bu, Bv, Bq, Bn, Bnb, Bgm, Bgb, Bs = S_["b"]
        nxt = t + 1 if (q == 0 and t + 1 < NT) else None
        if nxt is not None:
            nt_pre(x_own[nxt * 128:(nxt + 1) * 128, :], nt_sets_a3[nxt % 2])
        pu, PU = next_pz()
        for g4 in range(4):
            if nxt is not None:
                nt_group(g4, lambda g, n_=nxt: xnT[:, g * 4:(g + 1) * 4, n_ * 128:(n_ + 1) * 128], B_xnT[nxt], gmix, nt_sets_a3[nxt % 2])
            for kc in range(g4 * 4, g4 * 4 + 4):
                MM(pu[:, 0:512], xnT[:, kc, t * 128:(t + 1) * 128], wv[:, kc, 0:512], kc == 0, kc == 15, [B_xnT[t], WB[slot]], [PU])
        ACT(u_g[:], pu[:, 0:256], AF.Gelu_apprx_tanh, [PU], [Bu])
        ACT(v_g[:], pu[:, 256:512], AF.Gelu_apprx_tanh, [PU], [Bv])

    def a3_stage1(it):
        q, t = divmod(it, NT)
        v = 0 if t < 8 else 1
        S_ = a3sets[it % 5]
        u_g, v_g, v_n, vsq, v_nb, gm, gmb = S_["t"]
        Bu, Bv, Bq, Bn, Bnb, Bgm, Bgb, Bs = S_["b"]
        SA = S_["sa"]
        TT("dve", vsq[:], v_g[:], v_g[:], ALU.mult, [Bv], [Bq])
        RED(st[:, SA:SA + 2], vsq[:].rearrange("p (g c) -> p g c", g=2), [Bq], [Bs])
        rstd_of(st[:, SA + 4:SA + 6], st[:, SA:SA + 2], 128, [Bs], [Bs], st[:, SA + 2:SA + 4])
        TT("dve", vsq[:], v_g[:], gv_b[:, q * 256:(q + 1) * 256], ALU.mult, [Bv, B_const], [Bq])
        TT("dve", v_n[:].rearrange("p (g c) -> p g c", g=2), vsq[:].rearrange("p (g c) -> p g c", g=2),
           st[:, SA + 4:SA + 6].unsqueeze(2).to_broadcast([128, 2, 128]), ALU.mult, [Bq, Bs], [Bn])
        if t == 8:
            out_toks.append(DMA("sp", ov_d[:, q * 256:(q + 1) * 256], v_n[:], [Bn], [], osem_v))
        CP("pool", v_nb[:], v_n[:], [Bn], [Bnb])

    def a3_stage1b(it):
        q, t = divmod(it, NT)
        v = 0 if t < 8 else 1
        S_ = a3sets[it % 5]
        u_g, v_g, v_n, vsq, v_nb, gm, gmb = S_["t"]
        Bu, Bv, Bq, Bn, Bnb, Bgm, Bgb, Bs = S_["b"]
        ps_, PS = next_pz()
        for g in range(2):
            MM(ps_[:, g * 128:(g + 1) * 128], WT[:, v, q * 2 + g, :], v_nb[:, g * 128:(g + 1) * 128], True, True, [B_const, Bnb], [PS])
        for g in range(2):
            STT(gm[:, g * 128:(g + 1) * 128], ps_[:, g * 128:(g + 1) * 128], bT[:, v, q * 2 + g:q * 2 + g + 1],
                u_g[:, g * 128:(g + 1) * 128], ALU.add, ALU.mult, [PS, B_const, Bu], [Bgm])
        ACT(vsq[:], gm[:], AF.Square, [Bgm], [Bq, B_ssg], accum_out=st[:, SSG + t * 4 + q:SSG + t * 4 + q + 1])
        TT("dve", gmb[:], gm[:], gg_b[:, q * 256:(q + 1) * 256], ALU.mult, [Bgm, B_const], [Bgb])

    def a3_stage2(it):
        q, t = divmod(it, NT)
        S_ = a3sets[it % 5]
        gmb = S_["t"][6]
        Bgb = S_["b"][6]
        pt_, PB = next_ptr()
        for g in range(2):
            TR(pt_[:, g, :], gmb[:, g * 128:(g + 1) * 128], [Bgb], [PB])
        CP("act", mT[:, 8 + q * 2:8 + q * 2 + 2, t * 128:(t + 1) * 128], pt_[:, 0:2, :], [PB], [B_mT[t]])

    norm_transpose(x_own[0:128, :], lambda g4: xnT[:, g4 * 4:(g4 + 1) * 4, 0:128], B_xnT[0], gmix, None, nt_sets_a3[0])
    NIT = 4 * NT
    for k_ in range(NIT + 4):
        if 0 <= k_ - 2 < NIT:
            a3_stage1(k_ - 2)
        if k_ < NIT:
            a3_stage0(k_)
        if 0 <= k_ - 2 < NIT:
            a3_stage1b(k_ - 2)
        if 0 <= k_ - 4 < NIT:
            a3_stage2(k_ - 4)
    RED(st[:, SCR + 16:SCR + 25], st[:, SSG:SSG + 36].rearrange("p (t q) -> p t q", q=4), [B_st, B_ssg], [B_st])
    rstd_of(st[:, RSTD_G:RSTD_G + 9], st[:, SCR + 16:SCR + 25], 1024, [B_st], [B_st], st[:, SCR + 26:SCR + 35])

    if STOP == 'A3':
        return finish()
    wkv = wslot[1][:, 0:5120].rearrange("p (k n) -> p k n", k=16)
    load_w(1, wkv, w_in_v[:, :, 512:832])
    P.barrier()
    wq = wslot[0][:, 0:8192].rearrange("p (k n) -> p k n", k=16)
    load_w(0, wq, w_in_v[:, :, 0:512])
    B_wsm = Buf("wsmall")
    wsm_sem = P.dma_sem("wsmsem")
    DMA("pool", wuq[:], w_uq.rearrange("(kc p) n -> p kc n", p=128), [], [B_wsm], wsm_sem)
    DMA("pool", wuk[:], w_uk.rearrange("(kc p) n -> p kc n", p=128), [], [B_wsm], wsm_sem)
    DMA("pool", wuv[:], w_uv.rearrange("(kc p) n -> p kc n", p=128), [], [B_wsm], wsm_sem)
    for kc in range(4):
        TS("dve", wuq[:, kc, :], wuq[:, kc, :], gqf[:, kc:kc + 1], None, ALU.mult, None, [B_wsm, B_const], [B_wsm])

    B_xpT = [Buf("xpT0"), Buf("xpT1")]
    B_ckvT, B_krT = Buf("ckvT"), Buf("krT")
    SSKR = 96

    a1sets = []
    for si_ in range(3):
        o_ = R3 + si_ * 2816
        a1sets.append({"kvf": A(o_, [128, 256], F32), "kvb": A(o_ + 1024, [128, 256], BF16), "krt": A(o_ + 1536, [128, 6, 32], F32),
                       "krf": A(o_ + 2304, [128, 64], F32), "krb": A(o_ + 2560, [128, 128], BF16),
                       "B": [Buf(f"a1_{si_}_{k}") for k in range(6)], "sc": 300 + si_ * 8})
    B_sskr = Buf("sskr")
    a1_items = [("pre", p_) for p_ in range(NPRE)] + [("own", t) for t in range(NT)]
    a1_pk = {}

    def a1_info(it):
        kind, idx = a1_items[it]
        if kind == "pre":
            i = idx % 2
            return (lambda kc, i=i: xpT[:, i, kc, :]), B_xpT[i], cs_pre[:, idx, :], idx * 128, idx, None
        return (lambda kc, t=idx: xnT[:, kc, t * 128:(t + 1) * 128]), B_xnT[idx], cs_own[:, idx, :], 1024 + idx * 128, 8 + idx, \
            slice(idx * 128, (idx + 1) * 128)

    def a1_stage0(it):
        kind, idx = a1_items[it]
        if kind == "pre":
            i = idx % 2
            norm_transpose(x_pre[idx * 128:(idx + 1) * 128, :], lambda g4, i=i: xpT[:, i, g4 * 4:(g4 + 1) * 4, :], B_xpT[i], gmix, None,
                           nt_sets[idx % 2])
        lhs_fn, B_lhs, cs_tile, keycol, kti, out_rows = a1_info(it)
        pk, PK = next_pz()
        a1_pk[it] = (pk, PK)
        for kc in range(16):
            MM(pk[:, 0:320], lhs_fn(kc), wkv[:, kc, :], kc == 0, kc == 15, [B_lhs, WB[1]], [PK])

    def a1_stage1(it):
        lhs_fn, B_lhs, cs_tile, keycol, kti, out_rows = a1_info(it)
        pk, PK = a1_pk[it]
        S_ = a1sets[it % 3]
        kvf, kvb, krt, krf, krb = S_["kvf"], S_["kvb"], S_["krt"], S_["krf"], S_["krb"]
        Bkvf, Bkvb, Bkrt, Bkrf, Bkrb, Bs = S_["B"]
        sc = S_["sc"]
        ACT(kvb[:], pk[:, 0:256], AF.Square, [PK], [Bkvb, Bs], accum_out=st[:, sc:sc + 1])
        rstd_of(st[:, sc + 2:sc + 3], st[:, sc:sc + 1], 256, [Bs], [Bs], st[:, sc + 1:sc + 2])
        cos, sin = cs_tile[:, 0:32], cs_tile[:, 32:64]
        x1, x2 = pk[:, 256:288], pk[:, 288:320]
        TT("dve", krt[:, 0, :], x1, cos, ALU.mult, [PK, B_const], [Bkrt])
        TT("dve", krt[:, 1, :], x2, sin, ALU.mult, [PK, B_const], [Bkrt])
        TT("dve", krt[:, 2, :], x1, sin, ALU.mult, [PK, B_const], [Bkrt])
        TT("dve", krt[:, 3, :], x2, cos, ALU.mult, [PK, B_const], [Bkrt])
        STT(kvf[:], pk[:, 0:256], st[:, sc + 2:sc + 3], gkv_b[:], ALU.mult, ALU.mult, [PK, Bs, B_const], [Bkvf])
        if out_rows is not None:
            out_toks.append(DMA("sp", ockv_d[out_rows, :], kvf[:], [Bkvf], [], osem_kv[it % 3]))
        CP("pool", kvb[:], kvf[:], [Bkvf], [Bkvb])
        TT("dve", krf[:, 0:32], krt[:, 0, :], krt[:, 1, :], ALU.subtract, [Bkrt], [Bkrf])
        TT("dve", krf[:, 32:64], krt[:, 2, :], krt[:, 3, :], ALU.add, [Bkrt], [Bkrf])
        if out_rows is not None:
            out_toks.append(DMA("sp", okr_d[out_rows, :], krf[:], [Bkrf], [], osem_kr[it % 3]))
        ACT(krt[:, 4:6, :].rearrange("p a b -> p (a b)"), krf[:], AF.Square, [Bkrf], [Bkrt, B_sskr],
            accum_out=st[:, SSKR + kti:SSKR + kti + 1])
        CP("pool", krb[:, 0:64], krf[:], [Bkrf], [Bkrb])
        CP("pool", krb[:, 64:128], krf[:], [Bkrf], [Bkrb])

    def a1_stage2(it):
        lhs_fn, B_lhs, cs_tile, keycol, kti, out_rows = a1_info(it)
        S_ = a1sets[it % 3]
        kvb, krb = S_["kvb"], S_["krb"]
        Bkvb, Bkrb = S_["B"][1], S_["B"][4]
        pt_, PB = next_ptr()
        for j in range(2):
            TR(pt_[:, j, :], kvb[:, j * 128:(j + 1) * 128], [Bkvb], [PB])
        TR(pt_[:, 2, :], krb[:], [Bkrb], [PB])
        CP("act", ckvT[:, :, keycol:keycol + 128], pt_[:, 0:2, :], [PB], [B_ckvT])
        CP("act", krT[:, keycol:keycol + 128], pt_[:, 2, :], [PB], [B_krT])

    NA1 = len(a1_items)
    for k_ in range(NA1 + 2):
        if k_ < NA1:
            a1_stage0(k_)
        if 0 <= k_ - 1 < NA1:
            a1_stage1(k_ - 1)
        if 0 <= k_ - 2 < NA1:
            a1_stage2(k_ - 2)

    if STOP == 'A1':
        return finish()
    P.barrier()
    B_QT = Buf("QT")
    a2sets = []
    for si_, (o_, scb) in enumerate(((R2, 64), (R4 + 2 * SLOT, 300))):
        a2sets.append({
            "qf": A(o_, [128, 8, 192], F32), "qrb": A(o_ + 6144, [128, 8, 64], BF16), "qn": A(o_ + 7168, [128, 8, 128], BF16),
            "qs": A(o_ + 9216, [128, 512], BF16), "qlTb": A(o_ + 10240, [128, 4, 128], BF16), "qsq": A(o_ + 11264, [128, 192], BF16),
            "qrt": A(o_ + 11648, [128, 4, 8, 32], F32), "sc": scb,
            "B": {k: Buf(f"a2_{si_}_{k}") for k in ("qs", "ql", "qf", "jk", "qr", "qn", "s")}})
    a2_pq = {}

    def a2_s0(t):
        S_ = a2sets[t % 2]
        Bd, sc = S_["B"], S_["sc"]
        pq, PQ = next_pz()
        for kc in range(16):
            MM(pq[:, 0:512], xnT[:, kc, t * 128:(t + 1) * 128], wq[:, kc, :], kc == 0, kc == 15, [B_xnT[t], WB[0]], [PQ])
        ACT(S_["qs"][:], pq[:, 0:512], AF.Square, [PQ], [Bd["qs"], Bd["s"]], accum_out=st[:, sc:sc + 1])
        rstd_of(st[:, sc + 2:sc + 3], st[:, sc:sc + 1], 512, [Bd["s"]], [Bd["s"]], st[:, sc + 1:sc + 2])
        TS("dve", S_["qs"][:], pq[:, 0:512], st[:, sc + 2:sc + 3], None, ALU.mult, None, [PQ, Bd["s"]], [Bd["qs"]])

    def a2_s1(t):
        S_ = a2sets[t % 2]
        Bd = S_["B"]
        qs, qlTb, qf = S_["qs"], S_["qlTb"], S_["qf"]
        pt_, PB = next_ptr()
        for j in range(4):
            TR(pt_[:, j, :], qs[:, j * 128:(j + 1) * 128], [Bd["qs"]], [PB])
        CP("act", qlTb[:], pt_, [PB], [Bd["ql"]])
        qff = qf[:].rearrange("p h d -> p (h d)")
        for c in range(3):
            pr, PR = next_pz()
            for kc in range(4):
                MM(pr[:, 0:512], qlTb[:, kc, :], wuq[:, kc, c * 512:(c + 1) * 512], kc == 0, kc == 3, [Bd["ql"], B_wsm], [PR])
            CP("act" if c != 1 else "dve", qff[:, c * 512:(c + 1) * 512], pr[:, 0:512], [PR], [Bd["qf"]])

    def a2_s2(t):
        S_ = a2sets[t % 2]
        Bd, sc = S_["B"], S_["sc"]
        qf, qsq, qrt, qn, qrb = S_["qf"], S_["qsq"], S_["qrt"], S_["qn"], S_["qrb"]
        for h in range(8):
            ACT(qsq[:, 0:192], qf[:, h, :], AF.Square, [Bd["qf"]], [Bd["jk"], Bd["s"]], accum_out=st[:, sc + 8 + h:sc + 9 + h])
        ACT(st[:, sc + 24:sc + 32], st[:, sc + 8:sc + 16], AF.Sqrt, [Bd["s"]], [Bd["s"]], scale=1.0 / 192, bias=EPS)
        RECIP(st[:, sc + 16:sc + 24], st[:, sc + 24:sc + 32], [Bd["s"]], [Bd["s"]])
        TS("dve", st[:, sc + 24:sc + 32], st[:, sc + 16:sc + 24], 192.0 ** -0.5, None, ALU.mult, None, [Bd["s"]], [Bd["s"]])
        rq = st[:, sc + 24:sc + 32]
        cos = cs_own[:, t, 0:32].unsqueeze(1).to_broadcast([128, 8, 32])
        sin = cs_own[:, t, 32:64].unsqueeze(1).to_broadcast([128, 8, 32])
        x1, x2 = qf[:, :, 128:160], qf[:, :, 160:192]
        TT("dve", qrt[:, 0], x1, cos, ALU.mult, [Bd["qf"], B_const], [Bd["qr"]])
        TT("dve", qrt[:, 1], x2, sin, ALU.mult, [Bd["qf"], B_const], [Bd["qr"]])
        TT("dve", qrt[:, 2], x1, sin, ALU.mult, [Bd["qf"], B_const], [Bd["qr"]])
        TT("dve", qrt[:, 3], x2, cos, ALU.mult, [Bd["qf"], B_const], [Bd["qr"]])
        TT("dve", qf[:, :, 128:160], qrt[:, 0], qrt[:, 1], ALU.subtract, [Bd["qr"]], [Bd["qf"]])
        TT("dve", qf[:, :, 160:192], qrt[:, 2], qrt[:, 3], ALU.add, [Bd["qr"]], [Bd["qf"]])
        TT("dve", qf[:], qf[:], rq.unsqueeze(2).to_broadcast([128, 8, 192]), ALU.mult, [Bd["s"]], [Bd["qf"]])
        gq3 = gqk[:].unsqueeze(1).to_broadcast([128, 8, 192])
        TT("dve", qn[:], qf[:, :, 0:128], gq3[:, :, 0:128], ALU.mult, [Bd["qf"], B_const], [Bd["qn"]])
        TT("dve", qrb[:], qf[:, :, 128:192], gq3[:, :, 128:192], ALU.mult, [Bd["qf"], B_const], [Bd["qn"]])

    def a2_s3(t):
        S_ = a2sets[t % 2]
        Bd = S_["B"]
        qn, qrb = S_["qn"], S_["qrb"]
        for hg in range(2):
            pt_, PB = next_ptr()
            for j in range(4):
                TR(pt_[:, j, :], qn[:, hg * 4 + j, :], [Bd["qn"]], [PB])
            CP("act" if hg == 0 else "dve", QT[:, hg * 4:(hg + 1) * 4, t * 128:(t + 1) * 128], pt_, [PB], [B_QT])
        pt_, PB = next_ptr()
        for j in range(4):
            TR(pt_[:, j, :], qrb[:, 2 * j:2 * j + 2, :].rearrange("p a b -> p (a b)"), [Bd["qn"]], [PB])
        CP("act", QrT[:, :, t * 128:(t + 1) * 128], pt_, [PB], [B_QT])

    for k_ in range(NT + 3):
        if k_ < NT:
            a2_s0(k_)
        if 0 <= k_ - 1 < NT:
            a2_s1(k_ - 1)
        if 0 <= k_ - 2 < NT:
            a2_s2(k_ - 2)
        if 0 <= k_ - 3 < NT:
            a2_s3(k_ - 3)

    if STOP == 'A2':
        return finish()
    P.barrier()
    pzpool[0] = pz2
    DMA("sp", ga_b[:], ga_d.partition_broadcast(128), [], [B_const], ldsem)
    w_out_v = w_out.rearrange("(kc p) n -> p kc n", p=128)

    def wout_view(slot):
        return wslot[slot][:, 0:8192].rearrange("p (k n) -> p k n", k=16)

    load_w(0, wout_view(0), w_out_v[:, :, 0:512])
    load_w(1, wout_view(1), w_out_v[:, :, 512:1024])

    B_KT = [Buf("KT0"), Buf("KT1")]
    B_V4 = Buf("V4")
    B_PT = [Buf("PT0"), Buf("PT1")]
    B_ksq = Buf("ksq")
    B_rk = Buf("rstdk")
    B_ob = Buf("ob")
    PSS = PSM
    PSUMS = [PSU, PSU]
    SSA = 200
    SUMS = 130
    SSKS = 140
    P.op("dve", lambda e: e.memset(tmpk[:], 1.0), writes=[B_rk])
    P.op("dve", lambda e: e.memset(PTs[:], 0.0), writes=[B_PT[0]])

    kst = [0]

    def expand_K(cT, B_cT, nkeys, h, kbuf, pss_col0):
        for k0 in range(0, nkeys, 512):
            n = min(512, nkeys - k0)
            pk, PK = next_pz()
            for kc in range(2):
                MM(pk[:, 0:n], wuk[:, kc, h * 128:(h + 1) * 128], cT[:, kc, k0:k0 + n], kc == 0, kc == 1, [B_wsm, B_cT], [PK])
            kst[0] += 1
            if STOP == f"K{kst[0]}":
                return True
            CP("dve", KT2[:, kbuf, k0:k0 + n], pk[:, 0:n], [PK], [B_KT[kbuf]])
            kst[0] += 1
            if STOP == f"K{kst[0]}":
                return True
            ACT(ksq[:, 0:n], pk[:, 0:n], AF.Square, [PK], [B_ksq])
            kst[0] += 1
            if STOP == f"K{kst[0]}":
                return True
            for j in range(0, n, 128):
                m = min(128, n - j)
                kt = (k0 + j) // 128
                col = pss_col0 + kt * 8 + h
                MM(psm[0:m, col:col + 1], ksq[:, j:j + m], ones[:, 0:1], True, True, [B_ksq, B_const], [PSS])
            kst[0] += 1
            if STOP == f"K{kst[0]}":
                return True
        return False

    def expand_V(cT, B_cT, nkeys, hg):
        for kt in range((nkeys + 127) // 128):
            m = min(128, nkeys - kt * 128)
            pv, PV = next_pz()
            for kc in range(2):
                MM(pv[0:m, 0:512], cT[:, kc, kt * 128:kt * 128 + m], wuv[:, kc, hg * 512:(hg + 1) * 512], kc == 0, kc == 1,
                   [B_cT, B_wsm], [PV])
            CP("act" if kt % 2 == 0 else "dve", V4[0:m, kt, :, :].rearrange("p a b -> p (a b)"), pv[0:m, 0:512], [PV], [B_V4])

    ob2 = [ob, A(R3 + 52096, [128, 4, 128], BF16)]
    B_ob2 = [B_ob, Buf("ob1")]
    B_fst = [Buf("fst0"), Buf("fst1")]
    fh_i = [0]

    def finish_heads(pa, PA, sums_ap, B_sums, h, tiles, sc):
        n = len(tiles)
        k_ = fh_i[0] % 2
        fh_i[0] += 1
        obk, Bobk, Bf = ob2[k_], B_ob2[k_], B_fst[k_]
        RECIP(st[:, sc:sc + n], sums_ap, [B_sums], [Bf])
        for i, tq in enumerate(tiles):
            ACT(junkb[:, 0:128], pa[:, i, :], AF.Square, [PA, Bf], [B_tmp, B_ssa], scale=st[:, sc + i:sc + i + 1],
                accum_out=st[:, SSA + tq * 8 + h:SSA + tq * 8 + h + 1])
            STT(obk[:, i, :], pa[:, i, :], st[:, sc + i:sc + i + 1], ga_b[:, h * 128:(h + 1) * 128], ALU.mult, ALU.mult,
                [PA, Bf, B_const], [Bobk])

        def later():
            pt_, PB = next_ptr()
            for i, tq in enumerate(tiles):
                TR(pt_[:, i, :], obk[:, i, :], [Bobk], [PB])
            for i, tq in enumerate(tiles):
                CP("act", mT[:, h, tq * 128:(tq + 1) * 128], pt_[:, i, :], [PB], [B_mT[tq]])
        return later

    B_ssa = Buf("ssa")
    pending_fh = []

    if STOP == 'B1a0':
        return finish()
    for hg in range(2):
        expand_V(ckvT, B_ckvT, 2048, hg)
        if STOP == 'B1a1':
            return finish()
        for hh in range(4):
            h = hg * 4 + hh
            kb = h % 2
            if expand_K(ckvT, B_ckvT, 2048, h, kb, 0):
                return finish()
            B_tk = Buf("tk")
            TT("dve", tmpk[:, 0:16, h], psm[:, 0:128].rearrange("p (k h) -> p k h", h=8)[:, :, h], st[:, SSKR:SSKR + 16], ALU.add,
               [PSS, B_st], [B_tk])
            ACT(tmpk[:, 0:16, h], tmpk[:, 0:16, h], AF.Ln, [B_tk], [B_tk], scale=1.0 / 192, bias=EPS)
            ACT(rstdk[:, 0:16, h], tmpk[:, 0:16, h], AF.Exp, [B_tk], [B_rk], scale=-0.5)
            r0 = (h % 2) * 64
            if STOP == 'B1a':
                return finish()
            for qg in range(2):
                nkt_own = 4 * qg + 4
                steps = [(kt, True) for kt in range(8)] + [(8 + kt, False) for kt in range(nkt_own)]
                pa, PA = pacc[qg], PACC[qg]
                first = [True] * 4
                q0 = qg * 512
                scol = qg * 4
                P.op("dve", lambda e, pa=pa: e.memset(pa[:], 0.0), writes=[PA])
                P.op("dve", lambda e, scol=scol: e.memset(psu[:, scol:scol + 4], 0.0), writes=[PSU])
                def emit_S(si, h=h, hh=hh, kb=kb, r0=r0, qg=qg, q0=q0, steps=steps):
                    kt, is_pre = steps[si]
                    okt = kt - 8
                    i0 = 0 if is_pre else max(0, okt - 4 * qg)
                    c0 = i0 * 128
                    ps_, PS = next_pz()
                    MM(ps_[:, c0:512], KT2[:, kb, kt * 128:(kt + 1) * 128], QT[:, h, q0 + c0:q0 + 512], True, False,
                       [B_KT[kb], B_QT], [PS])
                    MM(ps_[:, c0:512], krT[r0:r0 + 64, kt * 128:(kt + 1) * 128], QrT[r0:r0 + 64, h // 2, q0 + c0:q0 + 512], False, True,
                       [B_krT, B_QT], [PS])
                    pb = si % 2
                    if is_pre:
                        ACT(PT[:, pb, c0:512], ps_[:, c0:512], AF.Exp, [PS, B_rk, B_const], [B_PT[pb]],
                            scale=rstdk[:, kt, h:h + 1], bias=pmask[:, 0:1])
                    else:
                        ACT(PT[:, pb, c0:512], ps_[:, c0:512], AF.Exp, [PS, B_rk], [B_PT[pb]], scale=rstdk[:, kt, h:h + 1])
                        if okt >= 4 * qg:
                            P.op("dve", lambda e, pb=pb, c0=c0: e.memset(PT[64:128, pb, c0:c0 + 64], 0.0), reads=[B_PT[pb]], writes=[B_PT[pb]])

                def emit_PV(si, hh=hh, qg=qg, steps=steps, pa=pa, PA=PA, scol=scol):
                    kt, is_pre = steps[si]
                    okt = kt - 8
                    i0 = 0 if is_pre else max(0, okt - 4 * qg)
                    pb = si % 2
                    for i in range(i0, 4):
                        MM(pa[:, i, :], PT[:, pb, i * 128:(i + 1) * 128], V4[:, kt, hh, :], False, False, [B_PT[pb], B_V4], [PA], skip=True)
                        MM(psu[:, scol + i:scol + i + 1], PT[:, pb, i * 128:(i + 1) * 128], ones[:, 0:1], False, False,
                           [B_PT[pb], B_const], [PSU], skip=True)

                emit_S(0)
                for si in range(len(steps)):
                    if si + 1 < len(steps):
                        emit_S(si + 1)
                    emit_PV(si)
                if STOP == 'B1b':
                    return finish()
                pending_fh.append(finish_heads(pa, PA, psu[:, scol:scol + 4], PSU, h, [qg * 4 + i for i in range(4)], SUMS + qg * 4))
                if len(pending_fh) > 1:
                    pending_fh.pop(0)()
                if STOP == 'B1c':
                    return finish()

    while pending_fh:
        pending_fh.pop(0)()
    if STOP == 'B1':
        return finish()
    B_cck, B_ckr, B_ckvTs, B_krTs = Buf("cck"), Buf("ckr"), Buf("ckvTs"), Buf("krTs")
    csem_p = P.dma_sem("csem_p")
    B_PTs = [Buf(f"PTs{b}") for b in range(4)]
    B_scf = Buf("scf")
    PSN = PSM
    for hg_ in range(2):
        P.op("dve", lambda e, hg_=hg_: e.memset(pacc[hg_][:], 0.0), writes=[PACC[hg_]])
    P.op("dve", lambda e: e.memset(psu[:, 8:16], 0.0), writes=[PSU])
    P.op("dve", lambda e: e.memset(psm[:], 0.0), writes=[PSM])
    for b in range(4):
        DMA("pool", cck[:], cache_ckv[b].rearrange("(t p) r -> p t r", p=128), [], [B_cck], csem_p)
        DMA("sp", ckrf[:], cache_kr[b].rearrange("(t p) r -> p t r", p=128), [], [B_ckr], csem)
        for kt in range(8):
            ACT(junkb[:, 0:64], ckrf[:, kt, :], AF.Square, [B_ckr], [B_tmp, B_st], accum_out=st[:, SSKS + kt:SSKS + kt + 1])
        B_ckrd = Buf("ckrd")
        CP("pool", ckrd[:, :, 0:64], ckrf[:], [B_ckr], [B_ckrd])
        CP("pool", ckrd[:, :, 64:128], ckrf[:], [B_ckr], [B_ckrd])
        for kt in range(8):
            pt_, PB = next_ptr()
            for j in range(2):
                TR(pt_[:, j, :], cck[:, kt, j * 128:(j + 1) * 128], [B_cck], [PB])
            TR(pt_[:, 2, :], ckrd[:, kt, :], [B_ckrd], [PB])
            CP("act", ckvTs[:, :, kt * 128:(kt + 1) * 128], pt_[:, 0:2, :], [PB], [B_ckvTs])
            CP("dve", krTs[:, kt * 128:(kt + 1) * 128], pt_[:, 2, :], [PB], [B_krTs])
        nc0 = 2048 + b * 32
        CP("pool", ckvTs[:, :, 1024:1056], ckvT[:, :, nc0:nc0 + 32], [B_ckvT], [B_ckvTs])
        CP("pool", krTs[:, 1024:1056], krT[:, nc0:nc0 + 32], [B_krT], [B_krTs])
        TT("dve", ksq[0:64, 0:32], krTs[0:64, 1024:1056], krTs[0:64, 1024:1056], ALU.mult, [B_krTs], [B_ksq])
        MM(psm[0:32, 300:301], ksq[0:64, 0:32], ones[0:64, 0:1], True, True, [B_ksq, B_const], [PSN])
        CP("dve", st[0:32, SSKS + 8:SSKS + 9], psm[0:32, 300:301], [PSN], [B_st])
        def samp_A(h, b=b):
            kb = h % 2
            expand_K(ckvTs, B_ckvTs, 1056, h, kb, 128)
            B_tk = Buf("tk")
            TT("dve", tmpk[:, 0:9, h], psm[:, 128:200].rearrange("p (k h) -> p k h", h=8)[:, :, h], st[:, SSKS:SSKS + 9], ALU.add,
               [PSS, B_st], [B_tk])
            ACT(tmpk[:, 0:9, h], tmpk[:, 0:9, h], AF.Ln, [B_tk], [B_tk], scale=1.0 / 192, bias=EPS)
            ACT(rstdk[:, 0:9, h], tmpk[:, 0:9, h], AF.Exp, [B_tk], [B_rk], scale=-0.5)

        def samp_B(h, b=b):
            hg, hh = divmod(h, 4)
            kb = h % 2
            r0 = (h % 2) * 64
            qc = 1024 + b * 32
            ps_, PS = next_pz()
            psv = ps_[:, 0:288].rearrange("p (k q) -> p k q", k=9)
            for kt in range(9):
                m = 128 if kt < 8 else 32
                MM(psv[0:m, kt, :], KT2[:, kb, kt * 128:kt * 128 + m], QT[:, h, qc:qc + 32], True, False, [B_KT[kb], B_QT], [PS])
                MM(psv[0:m, kt, :], krTs[r0:r0 + 64, kt * 128:kt * 128 + m], QrT[r0:r0 + 64, h // 2, qc:qc + 32], False, True,
                   [B_krTs, B_QT], [PS])
            TT("dve", scf[:, 0:8, :], psv[:, 0:8, :], rstdk[:, 0:8, h].unsqueeze(2).to_broadcast([128, 8, 32]), ALU.mult,
               [PS, B_rk], [B_scf])
            TT("dve", scf[0:32, 8, :], psv[0:32, 8, :], rstdk[0:32, 8, h:h + 1].to_broadcast([32, 32]), ALU.mult, [PS, B_rk], [B_scf])
            ACT(PTs[:, b, 0:8, b * 32:(b + 1) * 32], scf[:, 0:8, :], AF.Exp, [B_scf], [B_PTs[b]])
            ACT(PTs[0:32, b, 8, b * 32:(b + 1) * 32], scf[0:32, 8, :], AF.Exp, [B_scf], [B_PTs[b]])
            pa, PA = pacc[hg], PACC[hg]
            for kt in range(9):
                m = 128 if kt < 8 else 32
                MM(pa[:, hh, :], PTs[0:m, b, kt, :], V4[0:m, kt, hh, :], False, False, [B_PTs[b], B_V4], [PA], skip=True)
                MM(psu[:, 8 + h:9 + h], PTs[0:m, b, kt, :], ones[0:m, 0:1], False, False, [B_PTs[b], B_const], [PSU], skip=True)

        expand_V(ckvTs, B_ckvTs, 1056, 0)
        for h in range(8):
            samp_A(h)
            if h >= 1:
                samp_B(h - 1)
            if h == 4:
                expand_V(ckvTs, B_ckvTs, 1056, 1)
        samp_B(7)
    for hg in range(2):
        for hh in range(4):
            h = hg * 4 + hh
            finish_heads(pacc[hg][:, hh:hh + 1, :], PACC[hg], psu[:, 8 + h:9 + h], PSU, h, [8], SUMS + h)()
    RED(st[:, SCR + 16:SCR + 25], st[:, SSA:SSA + 72].rearrange("p (t h) -> p t h", h=8), [B_st, B_ssa], [B_st])
    rstd_of(st[:, RSTD_A:RSTD_A + 9], st[:, SCR + 16:SCR + 25], 1024, [B_st], [B_st], st[:, SCR + 26:SCR + 35])

    if STOP == 'B2':
        return finish()
    P.barrier()
    pzpool[0] = pz6
    B_y = [Buf(f"y{t}") for t in range(NT)]
    ysem0 = P.dma_sem("ysem0")
    for t in range(NT):
        ytok = DMA("sp", y_acc[:, t, :], x_own[t * 128:(t + 1) * 128, :], [], [B_y[t]], ysem0 if t < 2 else ysem)
        if t == 1:
            ytok0 = ytok
    for t in range(NT):
        B_y[t].writers = [ytok0 if t < 2 else ytok]
    B_hnT = Buf("hnT")
    B_hs = Buf("hs")
    rflat = rbuf[:].rearrange("p a b -> p (a b)")

    hsb2 = [hsb[:, 0:256], hsb[:, 256:512]]
    B_hs2 = [Buf("hs0"), Buf("hs1")]
    B_hst = [Buf(f"hst{t}") for t in range(NT)]
    HST = 340
    hi_ = [0]

    def hn_stats(t):
        c = HST + t * 4
        for n in range(4):
            ACT(rflat[:, 0:512], y_acc[:, t, n * 512:(n + 1) * 512], AF.Square, [B_y[t]], [B_tmp, B_hst[t]], accum_out=st[:, c + n:c + n + 1])
        RED(st[:, c:c + 1], st[:, c:c + 4], [B_hst[t]], [B_hst[t]])
        rstd_of(st[:, c + 2:c + 3], st[:, c:c + 1], D, [B_hst[t]], [B_hst[t]], st[:, c + 1:c + 2])

    def hn_trans(t):
        c = HST + t * 4
        for n in range(4):
            pt_, PB = next_ptr()
            for hf in range(2):
                k_ = hi_[0] % 2
                hi_[0] += 1
                cs_ = slice(n * 512 + hf * 256, n * 512 + hf * 256 + 256)
                TS("dve", hsb2[k_], y_acc[:, t, cs_], st[:, c + 2:c + 3], None, ALU.mult, None, [B_y[t], B_hst[t]], [B_hs2[k_]])
                for j in range(2):
                    TR(pt_[:, hf * 2 + j, :], hsb2[k_][:, j * 128:(j + 1) * 128], [B_hs2[k_]], [PB])
            TT("dve", xnT[:, n * 4:(n + 1) * 4, t * 128:(t + 1) * 128], pt_, gffn[:, n * 4:(n + 1) * 4].unsqueeze(2).to_broadcast([128, 4, 128]),
               ALU.mult, [PB, B_const], [B_hnT])

    w_up_v = w_up.rearrange("(kc p) n -> p kc n", p=128)
    w_dn_v = w_down.rearrange("(j p) n -> p j n", p=128)

    def up_view(slot):
        return wslot[slot][:, 0:8192].rearrange("p (k n) -> p k n", k=16)

    def dn_view(slot):
        return wslot[slot][:, 0:8192].rearrange("p (j n) -> p j n", j=4)

    def load_up(c):
        load_w((2 * c) % 3, up_view((2 * c) % 3), w_up_v[:, :, c * 512:(c + 1) * 512])

    def load_dn(c):
        load_w((2 * c + 1) % 3, dn_view((2 * c + 1) % 3), w_dn_v[:, c * 4:(c + 1) * 4, :])

    for n in range(4):
        slot = n % 3
        if n == 0:
            load_w(2, wout_view(2), w_out_v[:, :, 1024:1536])
        if n == 1:
            load_w(0, wout_view(0), w_out_v[:, :, 1536:2048])
        if n == 2:
            load_dn(0)
        wo = wout_view(slot)
        cols = slice(n * 512, (n + 1) * 512)
        for t in range(NT):
            pa_, PA_ = next_pz()
            for kc in range(8):
                MM(pa_[:, 0:512], mT[:, kc, t * 128:(t + 1) * 128], wo[:, kc, :], kc == 0, kc == 7, [B_mT[t], WB[slot]], [PA_])
            pg_, PG_ = next_pz()
            for kc in range(8, 16):
                MM(pg_[:, 0:512], mT[:, kc, t * 128:(t + 1) * 128], wo[:, kc, :], kc == 8, kc == 15, [B_mT[t], WB[slot]], [PG_])
            STT(y_acc[:, t, cols], pa_[:, 0:512], st[:, RSTD_A + t:RSTD_A + t + 1], y_acc[:, t, cols], ALU.mult, ALU.add,
                [PA_, B_st], [B_y[t]])
            STT(y_acc[:, t, cols], pg_[:, 0:512], st[:, RSTD_G + t:RSTD_G + t + 1], y_acc[:, t, cols], ALU.mult, ALU.add,
                [PG_, B_st], [B_y[t]])
            if n == 3:
                hn_stats(t)
                if t >= 2:
                    hn_trans(t - 2)
    load_up(0)
    hn_trans(NT - 2)
    hn_trans(NT - 1)
    if STOP == 'C':
        for t in range(NT):
            out_toks.append(DMA("sp", y_d[t * 128:(t + 1) * 128, :], y_acc[:, t, :], [B_y[t]], [], osem_y))
        return finish()
    P.barrier()
    B_aT = Buf("aT")
    B_r = [Buf("r0"), Buf("r1")]
    NCH = DFF // 512
    ri = 0
    for c in range(NCH):
        if c + 1 < NCH:
            load_up(c + 1)
        su, sd = (2 * c) % 3, (2 * c + 1) % 3
        wup, wdn = up_view(su), dn_view(sd)
        for j in range(4):
            for tg in range(3):
                pu, PU = next_pz()
                for kc in range(16):
                    MM(pu[:, 0:384], wup[:, kc, j * 128:(j + 1) * 128], xnT[:, kc, tg * 384:(tg + 1) * 384], kc == 0, kc == 15,
                       [WB[su], B_hnT], [PU])
                rb = ri % 2
                ri += 1
                ACT(rbuf[:, rb, :], pu[:, 0:384], AF.Relu, [PU], [B_r[rb]])
                TT("dve", aT[:, j, tg * 384:(tg + 1) * 384], rbuf[:, rb, :], rbuf[:, rb, :], ALU.mult, [B_r[rb]], [B_aT])
        if c + 1 < NCH:
            load_dn(c + 1)
        for t in range(NT):
            for n in range(4):
                pd, PD = next_pz()
                for j in range(4):
                    MM(pd[:, 0:512], aT[:, j, t * 128:(t + 1) * 128], wdn[:, j, n * 512:(n + 1) * 512], j == 0, j == 3, [B_aT, WB[sd]], [PD])
                cols = slice(n * 512, (n + 1) * 512)
                TT("dve", y_acc[:, t, cols], pd[:, 0:512], y_acc[:, t, cols], ALU.add, [PD], [B_y[t]])
            if c == NCH - 1:
                out_toks.append(DMA("sp", y_d[t * 128:(t + 1) * 128, :], y_acc[:, t, :], [B_y[t]], [], osem_y))
    return finish()


_NC_CACHE = {}


def _rope_table(pos):
    half = 32
    inv = 10000.0 ** (-np.arange(half, dtype=np.float64) / half)
    ang = pos.astype(np.float64)[:, None] * inv[None, :]
    return np.concatenate([np.cos(ang), np.sin(ang)], axis=-1).astype(np.float32)


def kernel(x_prompt, x_sample, cache_c_kv, cache_k_rope, norm_mix, w_in, q_lat_norm, kv_lat_norm,
           w_uq, w_uk, w_uv, q_norm_nope, q_norm_rope, k_norm_nope, k_norm_rope, v_norm,
           w_spatial, b_spatial, out_norm_attn, out_norm_gmlp, w_out, norm_ffn, w_up, w_down):
    f = lambda a: np.ascontiguousarray(np.asarray(a, dtype=np.float32))
    x_prompt, x_sample = f(x_prompt), f(x_sample)
    cache_c_kv, cache_k_rope = f(cache_c_kv), f(cache_k_rope)
    if "nc" not in _NC_CACHE:
        _NC_CACHE["nc"] = build_program()
    nc = _NC_CACHE["nc"]

    ws = f(w_spatial)[0]
    bs = f(b_spatial)[0]
    wsT = np.zeros((128, 2, 8, 128), np.float32)
    wsT[:, 0] = ws.transpose(2, 0, 1)
    for j in range(4):
        wsT[32 * j:32 * j + 32, 1, :, 32 * j:32 * j + 32] = ws[:, :32, :32].transpose(2, 0, 1)
    bT = np.zeros((128, 2, 8), np.float32)
    bT[:, 0] = bs.T
    bT[:, 1] = np.tile(bs[:, :32].T, (4, 1))
    sidx = np.arange(128)
    trilT = (sidx[:, None] <= sidx[None, :]).astype(np.float32)
    common = {
        "w_in": f(w_in)[0], "w_uq": f(w_uq)[0], "w_uk": f(w_uk)[0], "w_uv": f(w_uv)[0],
        "w_out": f(w_out)[0], "w_up": f(w_up)[0], "w_down": f(w_down)[0],
        "gmix_fm": f(f(norm_mix)[0].reshape(16, 128).T), "gffn_fm": f(f(norm_ffn)[0].reshape(16, 128).T),
        "gq_fm": f(f(q_lat_norm)[0].reshape(4, 128).T),
        "kv_lat_norm": f(kv_lat_norm).reshape(1, 256),
        "q_norm_nope": f(q_norm_nope).reshape(1, 128), "q_norm_rope": f(q_norm_rope).reshape(1, 32),
        "k_norm_nope": f(k_norm_nope).reshape(1, 128), "k_norm_rope": f(k_norm_rope).reshape(1, 32),
        "v_norm": f(v_norm).reshape(1, 1024),
        "out_norm_attn": f(out_norm_attn).reshape(1, 1024), "out_norm_gmlp": f(out_norm_gmlp).reshape(1, 1024),
        "wsT": wsT, "trilT": trilT, "bT": bT, "ident": np.eye(128, dtype=np.float32),
    }
    cs_pre = _rope_table(np.arange(1024))
    in_maps = []
    for c in range(8):
        b, half = c // 2, c % 2
        xo = np.concatenate([x_prompt[b, half * 1024:(half + 1) * 1024], x_sample[4 * c:4 * c + 4].reshape(128, D)], axis=0)
        pos = np.concatenate([half * 1024 + np.arange(1024), np.tile(1024 + np.arange(32), 4)])
        m = dict(common)
        m["x_own"] = f(xo)
        m["x_pre"] = f(x_prompt[b, 0:1024])
        m["cs_own"] = _rope_table(pos)
        m["cs_pre"] = cs_pre
        m["pmask"] = np.full((128, 1), 0.0 if half == 1 else -30000.0, np.float32)
        m["cache_ckv"] = f(cache_c_kv[0, 4 * c:4 * c + 4])
        m["cache_kr"] = f(cache_k_rope[0, 4 * c:4 * c + 4])
        in_maps.append(m)
    res = run_bass_kernel_spmd(nc, in_maps, core_ids=list(range(8)))
    y_p = np.zeros((4, 2048, D), np.float32)
    y_s = np.zeros((32, 32, D), np.float32)
    ckv_p = np.zeros((1, 4, 2048, 256), np.float32)
    kr_p = np.zeros((1, 4, 2048, 64), np.float32)
    ckv_s = np.zeros((1, 32, 32, 256), np.float32)
    kr_s = np.zeros((1, 32, 32, 64), np.float32)
    v_s = np.zeros((1, 32, 32, 1024), np.float32)
    for c in range(8):
        b, half = c // 2, c % 2
        r = res.results[c]
        sl = slice(half * 1024, (half + 1) * 1024)
        y_p[b, sl] = r["y"][0:1024]
        y_s[4 * c:4 * c + 4] = r["y"][1024:].reshape(4, 32, D)
        ckv_p[0, b, sl] = r["o_ckv"][0:1024]
        ckv_s[0, 4 * c:4 * c + 4] = r["o_ckv"][1024:].reshape(4, 32, 256)
        kr_p[0, b, sl] = r["o_kr"][0:1024]
        kr_s[0, 4 * c:4 * c + 4] = r["o_kr"][1024:].reshape(4, 32, 64)
        v_s[0, 4 * c:4 * c + 4] = r["o_v"].reshape(4, 32, 1024)
    return (y_p, y_s, ckv_p, kr_p, ckv_s, kr_s, v_s)
```

```python
import numpy as np
import concourse.bass as bass
import concourse.mybir as mybir
from concourse.bass_utils import run_bass_kernel_spmd

F32 = mybir.dt.float32
BF16 = mybir.dt.bfloat16
AF = mybir.ActivationFunctionType
ALU = mybir.AluOpType
AX = mybir.AxisListType

D = 2048
NT = 9
TOK = NT * 128
NPRE = 8
EPS = 1e-6
DFF = 8192


class Tok:
    __slots__ = ("sem", "val", "eng", "op")

    def __init__(self, sem, val, eng, op):
        self.sem, self.val, self.eng, self.op = sem, val, eng, op


class Buf:
    __slots__ = ("name", "writers", "readers", "excl")

    def __init__(self, name="", excl=False):
        self.name = name
        self.writers = []
        self.readers = []
        self.excl = excl


class Op:
    __slots__ = ("fn", "deps", "tok", "needs_inc", "dma_sem")


class Eng:
    def __init__(self, name, sem, in_order=False):
        self.name, self.sem, self.ops, self.in_order = name, sem, [], in_order


class Prog:
    def __init__(self, nc):
        self.nc = nc
        self.stack = []
        self.engs = {}
        self.dma_sems = []

    def enter(self, cm):
        v = cm.__enter__()
        self.stack.append(cm)
        return v

    def close(self):
        while self.stack:
            self.stack.pop().__exit__(None, None, None)

    def sem(self, name):
        return self.enter(self.nc.semaphore(name))

    def add_engine(self, key, in_order=False):
        e = Eng(key, self.sem("s_" + key), in_order)
        self.engs[key] = e
        return e

    def dma_sem(self, name):
        s = [self.sem(name), 0]
        self.dma_sems.append(s)
        return s

    def op(self, eng, fn, reads=(), writes=(), deps=(), dma_sem=None):
        e = self.engs[eng]
        o = Op()
        o.fn = fn
        o.needs_inc = False
        o.dma_sem = dma_sem
        d = list(deps)
        for b in reads:
            d.extend(b.writers)
            if b.excl:
                d.extend(t for t in b.readers if t.eng is not e)
        for b in writes:
            d.extend(b.readers)
            d.extend(b.writers)
        o.deps = d
        if dma_sem is not None:
            dma_sem[1] += 16
            o.tok = Tok(dma_sem, dma_sem[1], e, o)
        else:
            o.tok = Tok(None, None, e, o)
        for b in reads:
            b.readers.append(o.tok)
            if len(b.readers) > 1:
                b.readers = _compress(b.readers)
        for b in writes:
            if b.readers:
                b.writers = [o.tok]
                b.readers = []
            else:
                b.writers.append(o.tok)
                if len(b.writers) > 1:
                    b.writers = _compress(b.writers)
        e.ops.append(o)
        return o.tok

    def barrier(self):
        toks = []
        for e in self.engs.values():
            for o in reversed(e.ops):
                if o.dma_sem is None and o.fn is not None:
                    toks.append(o.tok)
                    break
        for s in self.dma_sems:
            if s[1] > 0:
                toks.append(Tok(s, s[1], None, None))
        for k in self.engs:
            self.op(k, None, deps=toks)

    def finalize(self):
        for e in self.engs.values():
            for o in e.ops:
                for t in o.deps:
                    if t.sem is None:
                        if t.eng is e and e.in_order:
                            continue
                        t.op.needs_inc = True
        for e in self.engs.values():
            c = 0
            for o in e.ops:
                if o.dma_sem is None and o.needs_inc:
                    c += 1
                    o.tok.val = c

    def emit(self, ekey, h):
        e = self.engs[ekey]
        waited = {}
        for o in e.ops:
            need = {}
            for t in o.deps:
                if t.sem is None:
                    if t.eng is e and e.in_order:
                        continue
                    key = ("c", t.eng.name)
                    semh = t.eng.sem
                else:
                    key = ("d", id(t.sem))
                    semh = t.sem[0]
                v = t.val
                if v > waited.get(key, 0) and v > need.get(key, (None, 0))[1]:
                    need[key] = (semh, v)
            for key, (semh, v) in need.items():
                h.wait_ge(semh, v)
                waited[key] = v
            if o.fn is None:
                continue
            ins = o.fn(h)
            if o.dma_sem is not None:
                ins.then_inc(o.dma_sem[0], 16)
            elif o.needs_inc:
                ins.then_inc(e.sem, 1)


def _compress(toks):
    best = {}
    for t in toks:
        k = ("c", t.eng.name) if t.sem is None else ("d", id(t.sem))
        best[k] = t
    return list(best.values())


def build_program():
    nc = bass.Bass("TRN2", target_bir_lowering=False)
    P = Prog(nc)
    for k in ["sp", "act", "dve", "pool"]:
        P.add_engine(k)
    P.add_engine("pe", in_order=True)

    def din(name, shape):
        return nc.dram_tensor(name, list(shape), F32, kind="ExternalInput").ap()

    def dout(name, shape):
        return nc.dram_tensor(name, list(shape), F32, kind="ExternalOutput").ap()

    x_own = din("x_own", [TOK, D])
    x_pre = din("x_pre", [1024, D])
    cs_own_d = din("cs_own", [TOK, 64])
    cs_pre_d = din("cs_pre", [1024, 64])
    pmask_d = din("pmask", [128, 1])
    cache_ckv = din("cache_ckv", [4, 1024, 256])
    cache_kr = din("cache_kr", [4, 1024, 64])
    w_in = din("w_in", [D, 2880])
    w_uq = din("w_uq", [512, 1536])
    w_uk = din("w_uk", [256, 1024])
    w_uv = din("w_uv", [256, 1024])
    w_out = din("w_out", [D, D])
    w_up = din("w_up", [D, DFF])
    w_down = din("w_down", [DFF, D])
    gmix_d = din("gmix_fm", [128, 16])
    gffn_d = din("gffn_fm", [128, 16])
    gqf_d = din("gq_fm", [128, 4])
    gkv_d = din("kv_lat_norm", [1, 256])
    qn_nope_d = din("q_norm_nope", [1, 128])
    qn_rope_d = din("q_norm_rope", [1, 32])
    kn_nope_d = din("k_norm_nope", [1, 128])
    kn_rope_d = din("k_norm_rope", [1, 32])
    gv_d = din("v_norm", [1, 1024])
    ga_d = din("out_norm_attn", [1, 1024])
    gg_d = din("out_norm_gmlp", [1, 1024])
    wsT_d = din("wsT", [128, 2, 8, 128])
    trilT_d = din("trilT", [128, 128])
    bT_d = din("bT", [128, 2, 8])
    ident_d = din("ident", [128, 128])
    y_d = dout("y", [TOK, D])
    ockv_d = dout("o_ckv", [TOK, 256])
    okr_d = dout("o_kr", [TOK, 64])
    ov_d = dout("o_v", [128, 1024])

    BASE = 16512
    P0, R1, R2, R3, R4 = 0, 14336, 51200, 88064, 161792
    SLOT = 16384
    cnt = [0]

    def A(off, shape, dt):
        cnt[0] += 1
        return nc.alloc_sbuf_tensor_at(f"t{cnt[0]}", list(shape), dt, offset=BASE + off)

    ident = A(P0 + 0, [128, 128], BF16)
    ones = A(P0 + 256, [128, 2], BF16)
    gmix = A(P0 + 320, [128, 16], F32)
    gffn = A(P0 + 384, [128, 16], F32)
    gqf = A(P0 + 448, [128, 4], F32)
    gkv_b = A(P0 + 512, [128, 256], F32)
    gqk = A(P0 + 1536, [128, 192], F32)
    cs_own = A(P0 + 2304, [128, NT, 64], F32)
    cs_pre = A(P0 + 4608, [128, NPRE, 64], F32)
    bT = A(P0 + 6656, [128, 2, 8], F32)
    WT = A(P0 + 6784, [128, 2, 8, 128], BF16)
    pmask = A(P0 + 10880, [128, 1], F32)
    st = A(P0 + 10944, [128, 384], F32)
    rbuf = A(P0 + 12480, [128, 2, 384], BF16)
    xnT = A(R1, [128, 16, TOK], BF16)
    KT2 = A(R1, [128, 2, 2048], BF16)
    V4 = A(R1 + 8192, [128, 16, 4, 128], BF16)
    PT = A(R1 + 24576, [128, 2, 512], BF16)
    ksq = A(R1 + 26624, [128, 512], BF16)
    PTs = A(R1 + 27648, [128, 4, 9, 128], BF16)
    mT = A(R2, [128, 16, TOK], BF16)
    aT = A(R2, [128, 4, TOK], BF16)
    xt = A(R2, [128, 2048], F32)
    xs = A(R2 + 8192, [128, 2048], BF16)
    u_g = A(R2 + 12288, [128, 256], F32)
    v_g = A(R2 + 13312, [128, 256], F32)
    v_n = A(R2 + 14336, [128, 256], F32)
    vsq = A(R2 + 15360, [128, 256], F32)
    v_nb = A(R2 + 16384, [128, 256], BF16)
    gm = A(R2 + 16896, [128, 256], F32)
    gmb = A(R2 + 17920, [128, 256], BF16)
    kvf = A(R2 + 12288, [128, 256], F32)
    kvb = A(R2 + 13312, [128, 256], BF16)
    krt = A(R2 + 13824, [128, 6, 32], F32)
    krf = A(R2 + 14592, [128, 64], F32)
    krb = A(R2 + 14848, [128, 128], BF16)
    qf = A(R2, [128, 8, 192], F32)
    qrb = A(R2 + 6144, [128, 8, 64], BF16)
    qn = A(R2 + 7168, [128, 8, 128], BF16)
    qs = A(R2 + 9216, [128, 512], BF16)
    qlTb = A(R2 + 10240, [128, 4, 128], BF16)
    qsq = A(R2 + 11264, [128, 1536], BF16)
    qrt = A(R2 + 14336, [128, 4, 8, 32], F32)
    y_acc = A(R3, [128, NT, D], F32)
    QT = A(R3, [128, 8, TOK], BF16)
    QrT = A(R3 + 18432, [128, 4, TOK], BF16)
    NK = 2048 + 128
    ckvT = A(R3 + 27648, [128, 2, NK], BF16)
    krT = A(R3 + 36352, [128, NK], BF16)
    wuk = A(R3 + 40704, [128, 2, 1024], BF16)
    wuv = A(R3 + 44800, [128, 2, 1024], BF16)
    wuq = A(R3 + 48896, [128, 4, 1536], BF16)
    ob = A(R3 + 48896, [128, 4, 128], BF16)
    junkb = A(R3 + 49920, [128, 512], BF16)
    scf = A(R3 + 50944, [128, 9, 32], F32)
    gv_b = A(R3 + 61184, [128, 1024], F32)
    gg_b = A(R3 + 65280, [128, 1024], F32)
    ga_b = A(R3 + 65280, [128, 1024], F32)
    rstdk = A(R3 + 69376, [128, 17, 8], F32)
    tmpk = A(R3 + 69920, [128, 17, 8], F32)
    wslot = [A(R4 + i * SLOT, [128, 8192], BF16) for i in range(3)]
    WB = [Buf(f"W{i}") for i in range(3)]
    wsem = [P.dma_sem(f"wsem{i}") for i in range(3)]
    hsb = A(210944, [128, 512], BF16)
    xpT = A(R4 + 2 * SLOT, [128, 2, 16, 128], BF16)
    cck = A(R4 + 2 * SLOT, [128, 8, 256], BF16)
    ckrf = A(R4 + 2 * SLOT + 4096, [128, 8, 64], F32)
    ckrd = A(R4 + 2 * SLOT + 6144, [128, 8, 128], BF16)
    ckvTs = A(R4 + 2 * SLOT + 8192, [128, 2, 1056], BF16)
    krTs = A(R4 + 2 * SLOT + 12416, [128, 1056], BF16)

    pz = [P.enter(nc.psum_tensor(f"pz{i}", [128, 512], F32)) for i in range(2)]
    psu = P.enter(nc.psum_tensor("psu", [128, 512], F32))
    PSU = Buf("psu", True)
    ptrs = [P.enter(nc.psum_tensor(f"ptr{i}", [128, 8, 128], BF16)) for i in range(2)]
    psm = P.enter(nc.psum_tensor("psm", [128, 512], F32))
    pacc = [P.enter(nc.psum_tensor(f"pacc{i}", [128, 4, 128], F32)) for i in range(2)]
    PZ = [Buf(f"pz{i}", True) for i in range(2)]
    PTR = [Buf("ptr0", True), Buf("ptr1", True)]
    PSM = Buf("psm", True)
    PACC = [Buf("pacc0", True), Buf("pacc1", True)]
    pzi = [0]
    ptri = [0]

    pz2 = [(pz[0], PZ[0]), (pz[1], PZ[1])]
    pz6 = pz2 + [(pacc[0][:].rearrange("p a b -> p (a b)"), PACC[0]), (pacc[1][:].rearrange("p a b -> p (a b)"), PACC[1]),
                 (psm, PSM), (psu, PSU)]
    pzpool = [pz6]

    def next_pz():
        pool = pzpool[0]
        i = pzi[0] % len(pool)
        pzi[0] += 1
        return pool[i]

    def next_ptr():
        i = ptri[0] % 2
        ptri[0] += 1
        return ptrs[i][:, 0:4, :], PTR[i]

    ldsem = P.dma_sem("ldsem")
    xsem = P.dma_sem("xsem")
    osem_kv = [P.dma_sem(f"osem_kv{i}") for i in range(3)]
    osem_kr = [P.dma_sem(f"osem_kr{i}") for i in range(3)]
    osem_v = P.dma_sem("osem_v")
    osem_y = P.dma_sem("osem_y")
    ysem = P.dma_sem("ysem")
    csem = P.dma_sem("csem")
    out_toks = []

    def MM(out, lhsT, rhs, start, stop, reads, writes, skip=False):
        if skip:
            return P.op("pe", lambda e: e.matmul(out, lhsT=lhsT, rhs=rhs, start=False, stop=False, skip_group_check=True),
                        reads=reads, writes=writes)
        return P.op("pe", lambda e: e.matmul(out, lhsT=lhsT, rhs=rhs, start=start, stop=stop), reads=reads, writes=writes)

    def TR(out, in_, reads, writes):
        return P.op("pe", lambda e: e.transpose(out=out, in_=in_, identity=ident[:]), reads=reads + [B_const], writes=writes)

    def ACT(out, in_, func, reads, writes, scale=1.0, bias=0.0, accum_out=None):
        if accum_out is None:
            return P.op("act", lambda e: e.activation(out=out, in_=in_, func=func, bias=bias, scale=scale), reads=reads, writes=writes)
        return P.op("act", lambda e: e.activation(out=out, in_=in_, func=func, bias=bias, scale=scale, accum_out=accum_out), reads=reads, writes=writes)

    def TT(eng, out, in0, in1, op, reads, writes):
        return P.op(eng, lambda e: e.tensor_tensor(out=out, in0=in0, in1=in1, op=op), reads=reads, writes=writes)

    def TS(eng, out, in0, s1, s2, op0, op1, reads, writes):
        if s2 is None:
            return P.op(eng, lambda e: e.tensor_scalar(out=out, in0=in0, scalar1=s1, scalar2=None, op0=op0), reads=reads, writes=writes)
        return P.op(eng, lambda e: e.tensor_scalar(out=out, in0=in0, scalar1=s1, scalar2=s2, op0=op0, op1=op1), reads=reads, writes=writes)

    def STT(out, in0, scalar, in1, op0, op1, reads, writes):
        return P.op("dve", lambda e: e.scalar_tensor_tensor(out=out, in0=in0, scalar=scalar, in1=in1, op0=op0, op1=op1), reads=reads, writes=writes)

    def CP(eng, out, in_, reads, writes):
        if eng == "act":
            return P.op("act", lambda e: e.copy(out=out, in_=in_), reads=reads, writes=writes)
        return P.op(eng, lambda e: e.tensor_copy(out=out, in_=in_), reads=reads, writes=writes)

    def RED(out, in_, reads, writes):
        return P.op("dve", lambda e: e.tensor_reduce(out=out, in_=in_, axis=AX.X, op=ALU.add), reads=reads, writes=writes)

    def RECIP(out, in_, reads, writes):
        return P.op("dve", lambda e: e.reciprocal(out=out, in_=in_), reads=reads, writes=writes)

    def DMA(eng, out, in_, reads, writes, sem):
        return P.op(eng, lambda e: e.dma_start(out=out, in_=in_), reads=reads, writes=writes, dma_sem=sem)

    def rstd_of(out, ss, n, reads, writes, tmp):
        ACT(tmp, ss, AF.Sqrt, reads, writes, scale=1.0 / n, bias=EPS)
        RECIP(out, tmp, writes, writes)

    w_in_v = w_in.rearrange("(kc p) n -> p kc n", p=128)

    def wA3_view(slot):
        return wslot[slot][:, 0:8192].rearrange("p (k n) -> p k n", k=16)

    def load_A3(q, slot):
        v = wA3_view(slot)
        u0 = 832 + q * 256
        v0 = 1856 + q * 256
        for k0 in (0, 8):
            DMA("pool", v[:, k0:k0 + 8, 0:256], w_in_v[:, k0:k0 + 8, u0:u0 + 256], [], [WB[slot]], wsem[slot])
            DMA("pool", v[:, k0:k0 + 8, 256:512], w_in_v[:, k0:k0 + 8, v0:v0 + 256], [], [WB[slot]], wsem[slot])

    B_const = Buf("const")
    B_st = Buf("st")
    stage = A(R2, [128, 2, 8, 128], F32)
    stage2 = A(R2 + 8192, [128, 128], F32)
    stage3 = A(R2 + 8704, [128, 128], F32)
    stage4 = A(R2 + 9216, [128, 2, 192], F32)
    B_stage = Buf("stage")
    ldsem2 = P.dma_sem("ldsem2")
    DMA("sp", stage2[:], ident_d[:, :], [], [], ldsem)
    DMA("act", stage3[:], trilT_d[:, :], [], [], ldsem2)
    DMA("sp", stage[:], wsT_d[:, :, :, :], [], [], ldsem)
    DMA("act", gmix[:], gmix_d[:, :], [], [], ldsem2)
    DMA("sp", gffn[:], gffn_d[:, :], [], [], ldsem)
    DMA("act", gqf[:], gqf_d[:, :], [], [], ldsem2)
    DMA("sp", gkv_b[:], gkv_d.partition_broadcast(128), [], [], ldsem)
    DMA("act", stage4[:, 0, 0:128], qn_nope_d.partition_broadcast(128), [], [], ldsem2)
    DMA("sp", stage4[:, 0, 128:160], qn_rope_d.partition_broadcast(128), [], [], ldsem)
    DMA("act", stage4[:, 0, 160:192], qn_rope_d.partition_broadcast(128), [], [], ldsem2)
    DMA("sp", stage4[:, 1, 0:128], kn_nope_d.partition_broadcast(128), [], [], ldsem)
    DMA("act", stage4[:, 1, 128:160], kn_rope_d.partition_broadcast(128), [], [], ldsem2)
    DMA("sp", stage4[:, 1, 160:192], kn_rope_d.partition_broadcast(128), [], [], ldsem)
    DMA("act", cs_own[:], cs_own_d.rearrange("(t p) c -> p t c", p=128), [], [], ldsem2)
    DMA("sp", cs_pre[:], cs_pre_d.rearrange("(t p) c -> p t c", p=128), [], [], ldsem)
    DMA("act", bT[:], bT_d[:, :, :], [], [], ldsem2)
    DMA("sp", pmask[:], pmask_d[:, :], [], [], ldsem)
    DMA("act", gv_b[:], gv_d.partition_broadcast(128), [], [], ldsem2)
    DMA("sp", gg_b[:], gg_d.partition_broadcast(128), [], [], ldsem)
    load_A3(0, 0)
    load_A3(1, 1)
    P.op("dve", lambda e: e.memset(st[:], 0.0), writes=[B_st])
    P.barrier()
    CP("dve", ident[:], stage2[:], [B_stage], [B_const])
    P.op("dve", lambda e: e.memset(ones[:], 1.0), writes=[B_const])
    for v in range(2):
        TT("dve", WT[:, v], stage[:, v], stage3[:].unsqueeze(1).to_broadcast([128, 8, 128]), ALU.mult, [B_stage], [B_const])
    TT("dve", gqk[:], stage4[:, 0, :], stage4[:, 1, :], ALU.mult, [B_stage], [B_const])

    w_in_v = w_in.rearrange("(kc p) n -> p kc n", p=128)

    def load_w(slot, dst_view, src_view, nsplit=2):
        K = dst_view.shape[1]
        step = (K + nsplit - 1) // nsplit
        for k0 in range(0, K, step):
            k1 = min(K, k0 + step)
            DMA("pool", dst_view[:, k0:k1, :], src_view[:, k0:k1, :], [], [WB[slot]], wsem[slot])

    import os as _os
    STOP = _os.environ.get('MK_STOP', '')

    def finish():
        P.op("sp", lambda e: e.nop(), deps=out_toks)

        P.finalize()
        with nc.Block() as block:
            @block.sync
            def _(e):
                P.emit("sp", e)

            @block.scalar
            def _(e):
                P.emit("act", e)

            @block.vector
            def _(e):
                P.emit("dve", e)

            @block.gpsimd
            def _(e):
                P.emit("pool", e)

            @block.tensor
            def _(e):
                P.emit("pe", e)
        P.close()
        return nc

    B_xt, B_xs, B_xnT = Buf("xt"), Buf("xs"), [Buf(f"xnT{t}") for t in range(NT)]
    B_mT = [Buf(f"mT{t}") for t in range(NT)]
    B_tmp = Buf("tmpA")
    SSG = 0
    RSTD_A = 40
    RSTD_G = 50
    SCR = 64

    xsem1 = P.dma_sem("xsem1")
    nt_sets = [
        {"xt": xt, "xs": xs, "Bxt": B_xt, "Bxs": B_xs, "Bs": Buf("nts0"), "sc": 336, "sem": xsem},
        {"xt": A(R3 + 8448, [128, 2048], F32), "xs": A(R3 + 8448 + 8192, [128, 2048], BF16), "Bxt": Buf("xt1"), "Bxs": Buf("xs1"),
         "Bs": Buf("nts1"), "sc": 376, "sem": xsem1},
    ]

    xsem2 = P.dma_sem("xsem2")
    nt_sets_a3 = [nt_sets[0],
                  {"xt": A(R3 + 30720, [128, 2048], F32), "xs": A(R3 + 30720 + 8192, [128, 2048], BF16), "Bxt": Buf("xt2"), "Bxs": Buf("xs2"),
                   "Bs": Buf("nts2"), "sc": 380, "sem": xsem2}]

    def nt_pre(src_dram_tile, S_):
        xt_, xs_, Bxt, Bxs, Bs, sc = S_["xt"], S_["xs"], S_["Bxt"], S_["Bxs"], S_["Bs"], S_["sc"]
        DMA("sp", xt_[:], src_dram_tile, [], [Bxt], S_["sem"])
        ACT(xs_[:], xt_[:], AF.Square, [Bxt], [Bxs, Bs], accum_out=st[:, sc:sc + 1])
        rstd_of(st[:, sc + 2:sc + 3], st[:, sc:sc + 1], D, [Bs], [Bs], st[:, sc + 1:sc + 2])
        TS("dve", xs_[:], xt_[:], st[:, sc + 2:sc + 3], None, ALU.mult, None, [Bxt, Bs], [Bxs])

    def nt_group(g4, dstT, B_dst, gain_fm, S_):
        xs_, Bxs = S_["xs"], S_["Bxs"]
        pt_, PB = next_ptr()
        for j in range(4):
            kc = g4 * 4 + j
            TR(pt_[:, j, :], xs_[:, kc * 128:(kc + 1) * 128], [Bxs], [PB])
        TT("dve", dstT(g4), pt_, gain_fm[:, g4 * 4:(g4 + 1) * 4].unsqueeze(2).to_broadcast([128, 4, 128]), ALU.mult,
           [PB, B_const], [B_dst])

    def norm_transpose(src_dram_tile, dstT, B_dst, gain_fm, t_writes, S_=None):
        S_ = S_ or nt_sets[0]
        nt_pre(src_dram_tile, S_)
        for g4 in range(4):
            nt_group(g4, dstT, B_dst, gain_fm, S_)

    P.barrier()
    if STOP == 'C0':
        return finish()
    a3sets = []
    for si_ in range(5):
        o_ = R3 + si_ * 6144
        tens = (A(o_, [128, 256], F32), A(o_ + 1024, [128, 256], F32), A(o_ + 2048, [128, 256], F32), A(o_ + 3072, [128, 256], F32),
                A(o_ + 4096, [128, 256], BF16), A(o_ + 4608, [128, 256], F32), A(o_ + 5632, [128, 256], BF16))
        a3sets.append({"t": tens, "b": tuple(Buf(f"a3_{si_}_{k}") for k in range(8)), "sa": 280 + si_ * 8})
    a3i = [0]
    B_ssg = Buf("ssg")
    def a3_stage0(it):
        q, t = divmod(it, NT)
        slot = q % 3
        if t == 0 and q == 1:
            load_A3(2, 2)
        if t == 0 and q == 2:
            load_A3(3, 0)
        wv = wA3_view(slot)
        S_ = a3sets[it % 5]
        u_g, v_g, v_n, vsq, v_nb, gm, gmb = S_["t"]
        Bu, Bv, Bq, Bn, Bnb, Bgm, Bgb, Bs = S_["b"]
        nxt = t + 1 if (q == 0 and t + 1 < NT) else None
        if nxt is not None:
            nt_pre(x_own[nxt * 128:(nxt + 1) * 128, :], nt_sets_a3[nxt % 2])
        pu, PU = next_pz()
        for g4 in range(4):
            if nxt is not None:
                nt_group(g4, lambda g, n_=nxt: xnT[:, g * 4:(g + 1) * 4, n_ * 128:(n_ + 1) * 128], B_xnT[nxt], gmix, nt_sets_a3[nxt % 2])
            for kc in range(g4 * 4, g4 * 4 + 4):
                MM(pu[:, 0:512], xnT[:, kc, t * 128:(t + 1) * 128], wv[:, kc, 0:512], kc == 0, kc == 15, [B_xnT[t], WB[slot]], [PU])
        ACT(u_g[:], pu[:, 0:256], AF.Gelu_apprx_tanh, [PU], [Bu])
        ACT(v_g[:], pu[:, 256:512], AF.Gelu_apprx_tanh, [PU], [Bv])

    def a3_stage1(it):
        q, t = divmod(it, NT)
        v = 0 if t < 8 else 1
        S_ = a3sets[it % 5]
        u_g, v_g, v_n, vsq, v_nb, gm, gmb = S_["t"]
        Bu, Bv, Bq, Bn, Bnb, Bgm, Bgb, Bs = S_["b"]
        SA = S_["sa"]
        TT("dve", vsq[:], v_g[:], v_g[:], ALU.mult, [Bv], [Bq])
        RED(st[:, SA:SA + 2], vsq[:].rearrange("p (g c) -> p g c", g=2), [Bq], [Bs])
        rstd_of(st[:, SA + 4:SA + 6], st[:, SA:SA + 2], 128, [Bs], [Bs], st[:, SA + 2:SA + 4])
        TT("dve", vsq[:], v_g[:], gv_b[:, q * 256:(q + 1) * 256], ALU.mult, [Bv, B_const], [Bq])
        TT("dve", v_n[:].rearrange("p (g c) -> p g c", g=2), vsq[:].rearrange("p (g c) -> p g c", g=2),
           st[:, SA + 4:SA + 6].unsqueeze(2).to_broadcast([128, 2, 128]), ALU.mult, [Bq, Bs], [Bn])
        if t == 8:
            out_toks.append(DMA("sp", ov_d[:, q * 256:(q + 1) * 256], v_n[:], [Bn], [], osem_v))
        CP("pool", v_nb[:], v_n[:], [Bn], [Bnb])

    def a3_stage1b(it):
        q, t = divmod(it, NT)
        v = 0 if t < 8 else 1
        S_ = a3sets[it % 5]
        u_g, v_g, v_n, vsq, v_nb, gm, gmb = S_["t"]
        Bu, Bv, Bq, Bn, Bnb, Bgm, Bgb, Bs = S_["b"]
        ps_, PS = next_pz()
        for g in range(2):
            MM(ps_[:, g * 128:(g + 1) * 128], WT[:, v, q * 2 + g, :], v_nb[:, g * 128:(g + 1) * 128], True, True, [B_const, Bnb], [PS])
        for g in range(2):
            STT(gm[:, g * 128:(g + 1) * 128], ps_[:, g * 128:(g + 1) * 128], bT[:, v, q * 2 + g:q * 2 + g + 1],
                u_g[:, g * 128:(g + 1) * 128], ALU.add, ALU.mult, [PS, B_const, Bu], [Bgm])
        ACT(vsq[:], gm[:], AF.Square, [Bgm], [Bq, B_ssg], accum_out=st[:, SSG + t * 4 + q:SSG + t * 4 + q + 1])
        TT("dve", gmb[:], gm[:], gg_b[:, q * 256:(q + 1) * 256], ALU.mult, [Bgm, B_const], [Bgb])

    def a3_stage2(it):
        q, t = divmod(it, NT)
        S_ = a3sets[it % 5]
        gmb = S_["t"][6]
        Bgb = S_["b"][6]
        pt_, PB = next_ptr()
        for g in range(2):
            TR(pt_[:, g, :], gmb[:, g * 128:(g + 1) * 128], [Bgb], [PB])
        CP("act", mT[:, 8 + q * 2:8 + q * 2 + 2, t * 128:(t + 1) * 128], pt_[:, 0:2, :], [PB], [B_mT[t]])

    norm_transpose(x_own[0:128, :], lambda g4: xnT[:, g4 * 4:(g4 + 1) * 4, 0:128], B_xnT[0], gmix, None, nt_sets_a3[0])
    NIT = 4 * NT
    for k_ in range(NIT + 4):
        if 0 <= k_ - 2 < NIT:
            a3_stage1(k_ - 2)
        if k_ < NIT:
            a3_stage0(k_)
        if 0 <= k_ - 2 < NIT:
            a3_stage1b(k_ - 2)
        if 0 <= k_ - 4 < NIT:
            a3_stage2(k_ - 4)
    RED(st[:, SCR + 16:SCR + 25], st[:, SSG:SSG + 36].rearrange("p (t q) -> p t q", q=4), [B_st, B_ssg], [B_st])
    rstd_of(st[:, RSTD_G:RSTD_G + 9], st[:, SCR + 16:SCR + 25], 1024, [B_st], [B_st], st[:, SCR + 26:SCR + 35])

    if STOP == 'A3':
        return finish()
    wkv = wslot[1][:, 0:5120].rearrange("p (k n) -> p k n", k=16)
    load_w(1, wkv, w_in_v[:, :, 512:832])
    P.barrier()
    wq = wslot[0][:, 0:8192].rearrange("p (k n) -> p k n", k=16)
    load_w(0, wq, w_in_v[:, :, 0:512])
    B_wsm = Buf("wsmall")
    wsm_sem = P.dma_sem("wsmsem")
    DMA("pool", wuq[:], w_uq.rearrange("(kc p) n -> p kc n", p=128), [], [B_wsm], wsm_sem)
    DMA("pool", wuk[:], w_uk.rearrange("(kc p) n -> p kc n", p=128), [], [B_wsm], wsm_sem)
    DMA("pool", wuv[:], w_uv.rearrange("(kc p) n -> p kc n", p=128), [], [B_wsm], wsm_sem)
    for kc in range(4):
        TS("dve", wuq[:, kc, :], wuq[:, kc, :], gqf[:, kc:kc + 1], None, ALU.mult, None, [B_wsm, B_const], [B_wsm])

    B_xpT = [Buf("xpT0"), Buf("xpT1")]
    B_ckvT, B_krT = Buf("ckvT"), Buf("krT")
    SSKR = 96

    a1sets = []
    for si_ in range(3):
        o_ = R3 + si_ * 2816
        a1sets.append({"kvf": A(o_, [128, 256], F32), "kvb": A(o_ + 1024, [128, 256], BF16), "krt": A(o_ + 1536, [128, 6, 32], F32),
                       "krf": A(o_ + 2304, [128, 64], F32), "krb": A(o_ + 2560, [128, 128], BF16),
                       "B": [Buf(f"a1_{si_}_{k}") for k in range(6)], "sc": 300 + si_ * 8})
    B_sskr = Buf("sskr")
    a1_items = [("pre", p_) for p_ in range(NPRE)] + [("own", t) for t in range(NT)]
    a1_pk = {}

    def a1_info(it):
        kind, idx = a1_items[it]
        if kind == "pre":
            i = idx % 2
            return (lambda kc, i=i: xpT[:, i, kc, :]), B_xpT[i], cs_pre[:, idx, :], idx * 128, idx, None
        return (lambda kc, t=idx: xnT[:, kc, t * 128:(t + 1) * 128]), B_xnT[idx], cs_own[:, idx, :], 1024 + idx * 128, 8 + idx, \
            slice(idx * 128, (idx + 1) * 128)

    def a1_stage0(it):
        kind, idx = a1_items[it]
        if kind == "pre":
            i = idx % 2
            norm_transpose(x_pre[idx * 128:(idx + 1) * 128, :], lambda g4, i=i: xpT[:, i, g4 * 4:(g4 + 1) * 4, :], B_xpT[i], gmix, None,
                           nt_sets[idx % 2])
        lhs_fn, B_lhs, cs_tile, keycol, kti, out_rows = a1_info(it)
        pk, PK = next_pz()
        a1_pk[it] = (pk, PK)
        for kc in range(16):
            MM(pk[:, 0:320], lhs_fn(kc), wkv[:, kc, :], kc == 0, kc == 15, [B_lhs, WB[1]], [PK])

    def a1_stage1(it):
        lhs_fn, B_lhs, cs_tile, keycol, kti, out_rows = a1_info(it)
        pk, PK = a1_pk[it]
        S_ = a1sets[it % 3]
        kvf, kvb, krt, krf, krb = S_["kvf"], S_["kvb"], S_["krt"], S_["krf"], S_["krb"]
        Bkvf, Bkvb, Bkrt, Bkrf, Bkrb, Bs = S_["B"]
        sc = S_["sc"]
        ACT(kvb[:], pk[:, 0:256], AF.Square, [PK], [Bkvb, Bs], accum_out=st[:, sc:sc + 1])
        rstd_of(st[:, sc + 2:sc + 3], st[:, sc:sc + 1], 256, [Bs], [Bs], st[:, sc + 1:sc + 2])
        cos, sin = cs_tile[:, 0:32], cs_tile[:, 32:64]
        x1, x2 = pk[:, 256:288], pk[:, 288:320]
        TT("dve", krt[:, 0, :], x1, cos, ALU.mult, [PK, B_const], [Bkrt])
        TT("dve", krt[:, 1, :], x2, sin, ALU.mult, [PK, B_const], [Bkrt])
        TT("dve", krt[:, 2, :], x1, sin, ALU.mult, [PK, B_const], [Bkrt])
        TT("dve", krt[:, 3, :], x2, cos, ALU.mult, [PK, B_const], [Bkrt])
        STT(kvf[:], pk[:, 0:256], st[:, sc + 2:sc + 3], gkv_b[:], ALU.mult, ALU.mult, [PK, Bs, B_const], [Bkvf])
        if out_rows is not None:
            out_toks.append(DMA("sp", ockv_d[out_rows, :], kvf[:], [Bkvf], [], osem_kv[it % 3]))
        CP("pool", kvb[:], kvf[:], [Bkvf], [Bkvb])
        TT("dve", krf[:, 0:32], krt[:, 0, :], krt[:, 1, :], ALU.subtract, [Bkrt], [Bkrf])
        TT("dve", krf[:, 32:64], krt[:, 2, :], krt[:, 3, :], ALU.add, [Bkrt], [Bkrf])
        if out_rows is not None:
            out_toks.append(DMA("sp", okr_d[out_rows, :], krf[:], [Bkrf], [], osem_kr[it % 3]))
        ACT(krt[:, 4:6, :].rearrange("p a b -> p (a b)"), krf[:], AF.Square, [Bkrf], [Bkrt, B_sskr],
            accum_out=st[:, SSKR + kti:SSKR + kti + 1])
        CP("pool", krb[:, 0:64], krf[:], [Bkrf], [Bkrb])
        CP("pool", krb[:, 64:128], krf[:], [Bkrf], [Bkrb])

    def a1_stage2(it):
        lhs_fn, B_lhs, cs_tile, keycol, kti, out_rows = a1_info(it)
        S_ = a1sets[it % 3]
        kvb, krb = S_["kvb"], S_["krb"]
        Bkvb, Bkrb = S_["B"][1], S_["B"][4]
        pt_, PB = next_ptr()
        for j in range(2):
            TR(pt_[:, j, :], kvb[:, j * 128:(j + 1) * 128], [Bkvb], [PB])
        TR(pt_[:, 2, :], krb[:], [Bkrb], [PB])
        CP("act", ckvT[:, :, keycol:keycol + 128], pt_[:, 0:2, :], [PB], [B_ckvT])
        CP("act", krT[:, keycol:keycol + 128], pt_[:, 2, :], [PB], [B_krT])

    NA1 = len(a1_items)
    for k_ in range(NA1 + 2):
        if k_ < NA1:
            a1_stage0(k_)
        if 0 <= k_ - 1 < NA1:
            a1_stage1(k_ - 1)
        if 0 <= k_ - 2 < NA1:
            a1_stage2(k_ - 2)

    if STOP == 'A1':
        return finish()
    P.barrier()
    B_QT = Buf("QT")
    a2sets = []
    for si_, (o_, scb) in enumerate(((R2, 64), (R4 + 2 * SLOT, 300))):
        a2sets.append({
            "qf": A(o_, [128, 8, 192], F32), "qrb": A(o_ + 6144, [128, 8, 64], BF16), "qn": A(o_ + 7168, [128, 8, 128], BF16),
            "qs": A(o_ + 9216, [128, 512], BF16), "qlTb": A(o_ + 10240, [128, 4, 128], BF16), "qsq": A(o_ + 11264, [128, 192], BF16),
            "qrt": A(o_ + 11648, [128, 4, 8, 32], F32), "sc": scb,
            "B": {k: Buf(f"a2_{si_}_{k}") for k in ("qs", "ql", "qf", "jk", "qr", "qn", "s")}})
    a2_pq = {}

    def a2_s0(t):
        S_ = a2sets[t % 2]
        Bd, sc = S_["B"], S_["sc"]
        pq, PQ = next_pz()
        for kc in range(16):
            MM(pq[:, 0:512], xnT[:, kc, t * 128:(t + 1) * 128], wq[:, kc, :], kc == 0, kc == 15, [B_xnT[t], WB[0]], [PQ])
        ACT(S_["qs"][:], pq[:, 0:512], AF.Square, [PQ], [Bd["qs"], Bd["s"]], accum_out=st[:, sc:sc + 1])
        rstd_of(st[:, sc + 2:sc + 3], st[:, sc:sc + 1], 512, [Bd["s"]], [Bd["s"]], st[:, sc + 1:sc + 2])
        TS("dve", S_["qs"][:], pq[:, 0:512], st[:, sc + 2:sc + 3], None, ALU.mult, None, [PQ, Bd["s"]], [Bd["qs"]])

    def a2_s1(t):
        S_ = a2sets[t % 2]
        Bd = S_["B"]
        qs, qlTb, qf = S_["qs"], S_["qlTb"], S_["qf"]
        pt_, PB = next_ptr()
        for j in range(4):
            TR(pt_[:, j, :], qs[:, j * 128:(j + 1) * 128], [Bd["qs"]], [PB])
        CP("act", qlTb[:], pt_, [PB], [Bd["ql"]])
        qff = qf[:].rearrange("p h d -> p (h d)")
        for c in range(3):
            pr, PR = next_pz()
            for kc in range(4):
                MM(pr[:, 0:512], qlTb[:, kc, :], wuq[:, kc, c * 512:(c + 1) * 512], kc == 0, kc == 3, [Bd["ql"], B_wsm], [PR])
            CP("act" if c != 1 else "dve", qff[:, c * 512:(c + 1) * 512], pr[:, 0:512], [PR], [Bd["qf"]])

    def a2_s2(t):
        S_ = a2sets[t % 2]
        Bd, sc = S_["B"], S_["sc"]
        qf, qsq, qrt, qn, qrb = S_["qf"], S_["qsq"], S_["qrt"], S_["qn"], S_["qrb"]
        for h in range(8):
            ACT(qsq[:, 0:192], qf[:, h, :], AF.Square, [Bd["qf"]], [Bd["jk"], Bd["s"]], accum_out=st[:, sc + 8 + h:sc + 9 + h])
        ACT(st[:, sc + 24:sc + 32], st[:, sc + 8:sc + 16], AF.Sqrt, [Bd["s"]], [Bd["s"]], scale=1.0 / 192, bias=EPS)
        RECIP(st[:, sc + 16:sc + 24], st[:, sc + 24:sc + 32], [Bd["s"]], [Bd["s"]])
        TS("dve", st[:, sc + 24:sc + 32], st[:, sc + 16:sc + 24], 192.0 ** -0.5, None, ALU.mult, None, [Bd["s"]], [Bd["s"]])
        rq = st[:, sc + 24:sc + 32]
        cos = cs_own[:, t, 0:32].unsqueeze(1).to_broadcast([128, 8, 32])
        sin = cs_own[:, t, 32:64].unsqueeze(1).to_broadcast([128, 8, 32])
        x1, x2 = qf[:, :, 128:160], qf[:, :, 160:192]
        TT("dve", qrt[:, 0], x1, cos, ALU.mult, [Bd["qf"], B_const], [Bd["qr"]])
        TT("dve", qrt[:, 1], x2, sin, ALU.mult, [Bd["qf"], B_const], [Bd["qr"]])
        TT("dve", qrt[:, 2], x1, sin, ALU.mult, [Bd["qf"], B_const], [Bd["qr"]])
        TT("dve", qrt[:, 3], x2, cos, ALU.mult, [Bd["qf"], B_const], [Bd["qr"]])
        TT("dve", qf[:, :, 128:160], qrt[:, 0], qrt[:, 1], ALU.subtract, [Bd["qr"]], [Bd["qf"]])
        TT("dve", qf[:, :, 160:192], qrt[:, 2], qrt[:, 3], ALU.add, [Bd["qr"]], [Bd["qf"]])
        TT("dve", qf[:], qf[:], rq.unsqueeze(2).to_broadcast([128, 8, 192]), ALU.mult, [Bd["s"]], [Bd["qf"]])
        gq3 = gqk[:].unsqueeze(1).to_broadcast([128, 8, 192])
        TT("dve", qn[:], qf[:, :, 0:128], gq3[:, :, 0:128], ALU.mult, [Bd["qf"], B_const], [Bd["qn"]])
        TT("dve", qrb[:], qf[:, :, 128:192], gq3[:, :, 128:192], ALU.mult, [Bd["qf"], B_const], [Bd["qn"]])

    def a2_s3(t):
        S_ = a2sets[t % 2]
        Bd = S_["B"]
        qn, qrb = S_["qn"], S_["qrb"]
        for hg in range(2):
            pt_, PB = next_ptr()
            for j in range(4):
                TR(pt_[:, j, :], qn[:, hg * 4 + j, :], [Bd["qn"]], [PB])
            CP("act" if hg == 0 else "dve", QT[:, hg * 4:(hg + 1) * 4, t * 128:(t + 1) * 128], pt_, [PB], [B_QT])
        pt_, PB = next_ptr()
        for j in range(4):
            TR(pt_[:, j, :], qrb[:, 2 * j:2 * j + 2, :].rearrange("p a b -> p (a b)"), [Bd["qn"]], [PB])
        CP("act", QrT[:, :, t * 128:(t + 1) * 128], pt_, [PB], [B_QT])

    for k_ in range(NT + 3):
        if k_ < NT:
            a2_s0(k_)
        if 0 <= k_ - 1 < NT:
            a2_s1(k_ - 1)
        if 0 <= k_ - 2 < NT:
            a2_s2(k_ - 2)
        if 0 <= k_ - 3 < NT:
            a2_s3(k_ - 3)

    if STOP == 'A2':
        return finish()
    P.barrier()
    pzpool[0] = pz2
    DMA("sp", ga_b[:], ga_d.partition_broadcast(128), [], [B_const], ldsem)
    w_out_v = w_out.rearrange("(kc p) n -> p kc n", p=128)

    def wout_view(slot):
        return wslot[slot][:, 0:8192].rearrange("p (k n) -> p k n", k=16)

    load_w(0, wout_view(0), w_out_v[:, :, 0:512])
    load_w(1, wout_view(1), w_out_v[:, :, 512:1024])

    B_KT = [Buf("KT0"), Buf("KT1")]
    B_V4 = Buf("V4")
    B_PT = [Buf("PT0"), Buf("PT1")]
    B_ksq = Buf("ksq")
    B_rk = Buf("rstdk")
    B_ob = Buf("ob")
    PSS = PSM
    PSUMS = [PSU, PSU]
    SSA = 200
    SUMS = 130
    SSKS = 140
    P.op("dve", lambda e: e.memset(tmpk[:], 1.0), writes=[B_rk])
    P.op("dve", lambda e: e.memset(PTs[:], 0.0), writes=[B_PT[0]])

    kst = [0]

    def expand_K(cT, B_cT, nkeys, h, kbuf, pss_col0):
        for k0 in range(0, nkeys, 512):
            n = min(512, nkeys - k0)
            pk, PK = next_pz()
            for kc in range(2):
                MM(pk[:, 0:n], wuk[:, kc, h * 128:(h + 1) * 128], cT[:, kc, k0:k0 + n], kc == 0, kc == 1, [B_wsm, B_cT], [PK])
            kst[0] += 1
            if STOP == f"K{kst[0]}":
                return True
            CP("dve", KT2[:, kbuf, k0:k0 + n], pk[:, 0:n], [PK], [B_KT[kbuf]])
            kst[0] += 1
            if STOP == f"K{kst[0]}":
                return True
            ACT(ksq[:, 0:n], pk[:, 0:n], AF.Square, [PK], [B_ksq])
            kst[0] += 1
            if STOP == f"K{kst[0]}":
                return True
            for j in range(0, n, 128):
                m = min(128, n - j)
                kt = (k0 + j) // 128
                col = pss_col0 + kt * 8 + h
                MM(psm[0:m, col:col + 1], ksq[:, j:j + m], ones[:, 0:1], True, True, [B_ksq, B_const], [PSS])
            kst[0] += 1
            if STOP == f"K{kst[0]}":
                return True
        return False

    def expand_V(cT, B_cT, nkeys, hg):
        for kt in range((nkeys + 127) // 128):
            m = min(128, nkeys - kt * 128)
            pv, PV = next_pz()
            for kc in range(2):
                MM(pv[0:m, 0:512], cT[:, kc, kt * 128:kt * 128 + m], wuv[:, kc, hg * 512:(hg + 1) * 512], kc == 0, kc == 1,
                   [B_cT, B_wsm], [PV])
            CP("act" if kt % 2 == 0 else "dve", V4[0:m, kt, :, :].rearrange("p a b -> p (a b)"), pv[0:m, 0:512], [PV], [B_V4])

    ob2 = [ob, A(R3 + 52096, [128, 4, 128], BF16)]
    B_ob2 = [B_ob, Buf("ob1")]
    B_fst = [Buf("fst0"), Buf("fst1")]
    fh_i = [0]

    def finish_heads(pa, PA, sums_ap, B_sums, h, tiles, sc):
        n = len(tiles)
        k_ = fh_i[0] % 2
        fh_i[0] += 1
        obk, Bobk, Bf = ob2[k_], B_ob2[k_], B_fst[k_]
        RECIP(st[:, sc:sc + n], sums_ap, [B_sums], [Bf])
        for i, tq in enumerate(tiles):
            ACT(junkb[:, 0:128], pa[:, i, :], AF.Square, [PA, Bf], [B_tmp, B_ssa], scale=st[:, sc + i:sc + i + 1],
                accum_out=st[:, SSA + tq * 8 + h:SSA + tq * 8 + h + 1])
            STT(obk[:, i, :], pa[:, i, :], st[:, sc + i:sc + i + 1], ga_b[:, h * 128:(h + 1) * 128], ALU.mult, ALU.mult,
                [PA, Bf, B_const], [Bobk])

        def later():
            pt_, PB = next_ptr()
            for i, tq in enumerate(tiles):
                TR(pt_[:, i, :], obk[:, i, :], [Bobk], [PB])
            for i, tq in enumerate(tiles):
                CP("act", mT[:, h, tq * 128:(tq + 1) * 128], pt_[:, i, :], [PB], [B_mT[tq]])
        return later

    B_ssa = Buf("ssa")
    pending_fh = []

    if STOP == 'B1a0':
        return finish()
    for hg in range(2):
        expand_V(ckvT, B_ckvT, 2048, hg)
        if STOP == 'B1a1':
            return finish()
        for hh in range(4):
            h = hg * 4 + hh
            kb = h % 2
            if expand_K(ckvT, B_ckvT, 2048, h, kb, 0):
                return finish()
            B_tk = Buf("tk")
            TT("dve", tmpk[:, 0:16, h], psm[:, 0:128].rearrange("p (k h) -> p k h", h=8)[:, :, h], st[:, SSKR:SSKR + 16], ALU.add,
               [PSS, B_st], [B_tk])
            ACT(tmpk[:, 0:16, h], tmpk[:, 0:16, h], AF.Ln, [B_tk], [B_tk], scale=1.0 / 192, bias=EPS)
            ACT(rstdk[:, 0:16, h], tmpk[:, 0:16, h], AF.Exp, [B_tk], [B_rk], scale=-0.5)
            r0 = (h % 2) * 64
            if STOP == 'B1a':
                return finish()
            for qg in range(2):
                nkt_own = 4 * qg + 4
                steps = [(kt, True) for kt in range(8)] + [(8 + kt, False) for kt in range(nkt_own)]
                pa, PA = pacc[qg], PACC[qg]
                first = [True] * 4
                q0 = qg * 512
                scol = qg * 4
                P.op("dve", lambda e, pa=pa: e.memset(pa[:], 0.0), writes=[PA])
                P.op("dve", lambda e, scol=scol: e.memset(psu[:, scol:scol + 4], 0.0), writes=[PSU])
                def emit_S(si, h=h, hh=hh, kb=kb, r0=r0, qg=qg, q0=q0, steps=steps):
                    kt, is_pre = steps[si]
                    okt = kt - 8
                    i0 = 0 if is_pre else max(0, okt - 4 * qg)
                    c0 = i0 * 128
                    ps_, PS = next_pz()
                    MM(ps_[:, c0:512], KT2[:, kb, kt * 128:(kt + 1) * 128], QT[:, h, q0 + c0:q0 + 512], True, False,
                       [B_KT[kb], B_QT], [PS])
                    MM(ps_[:, c0:512], krT[r0:r0 + 64, kt * 128:(kt + 1) * 128], QrT[r0:r0 + 64, h // 2, q0 + c0:q0 + 512], False, True,
                       [B_krT, B_QT], [PS])
                    pb = si % 2
                    if is_pre:
                        ACT(PT[:, pb, c0:512], ps_[:, c0:512], AF.Exp, [PS, B_rk, B_const], [B_PT[pb]],
                            scale=rstdk[:, kt, h:h + 1], bias=pmask[:, 0:1])
                    else:
                        ACT(PT[:, pb, c0:512], ps_[:, c0:512], AF.Exp, [PS, B_rk], [B_PT[pb]], scale=rstdk[:, kt, h:h + 1])
                        if okt >= 4 * qg:
                            P.op("dve", lambda e, pb=pb, c0=c0: e.memset(PT[64:128, pb, c0:c0 + 64], 0.0), reads=[B_PT[pb]], writes=[B_PT[pb]])

                def emit_PV(si, hh=hh, qg=qg, steps=steps, pa=pa, PA=PA, scol=scol):
                    kt, is_pre = steps[si]
                    okt = kt - 8
                    i0 = 0 if is_pre else max(0, okt - 4 * qg)
                    pb = si % 2
                    for i in range(i0, 4):
                        MM(pa[:, i, :], PT[:, pb, i * 128:(i + 1) * 128], V4[:, kt, hh, :], False, False, [B_PT[pb], B_V4], [PA], skip=True)
                        MM(psu[:, scol + i:scol + i + 1], PT[:, pb, i * 128:(i + 1) * 128], ones[:, 0:1], False, False,
                           [B_PT[pb], B_const], [PSU], skip=True)

                emit_S(0)
                for si in range(len(steps)):
                    if si + 1 < len(steps):
                        emit_S(si + 1)
                    emit_PV(si)
                if STOP == 'B1b':
                    return finish()
                pending_fh.append(finish_heads(pa, PA, psu[:, scol:scol + 4], PSU, h, [qg * 4 + i for i in range(4)], SUMS + qg * 4))
                if len(pending_fh) > 1:
                    pending_fh.pop(0)()
                if STOP == 'B1c':
                    return finish()

    while pending_fh:
        pending_fh.pop(0)()
    if STOP == 'B1':
        return finish()
    B_cck, B_ckr, B_ckvTs, B_krTs = Buf("cck"), Buf("ckr"), Buf("ckvTs"), Buf("krTs")
    csem_p = P.dma_sem("csem_p")
    B_PTs = [Buf(f"PTs{b}") for b in range(4)]
    B_scf = Buf("scf")
    PSN = PSM
    for hg_ in range(2):
        P.op("dve", lambda e, hg_=hg_: e.memset(pacc[hg_][:], 0.0), writes=[PACC[hg_]])
    P.op("dve", lambda e: e.memset(psu[:, 8:16], 0.0), writes=[PSU])
    P.op("dve", lambda e: e.memset(psm[:], 0.0), writes=[PSM])
    for b in range(4):
        DMA("pool", cck[:], cache_ckv[b].rearrange("(t p) r -> p t r", p=128), [], [B_cck], csem_p)
        DMA("sp", ckrf[:], cache_kr[b].rearrange("(t p) r -> p t r", p=128), [], [B_ckr], csem)
        for kt in range(8):
            ACT(junkb[:, 0:64], ckrf[:, kt, :], AF.Square, [B_ckr], [B_tmp, B_st], accum_out=st[:, SSKS + kt:SSKS + kt + 1])
        B_ckrd = Buf("ckrd")
        CP("pool", ckrd[:, :, 0:64], ckrf[:], [B_ckr], [B_ckrd])
        CP("pool", ckrd[:, :, 64:128], ckrf[:], [B_ckr], [B_ckrd])
        for kt in range(8):
            pt_, PB = next_ptr()
            for j in range(2):
                TR(pt_[:, j, :], cck[:, kt, j * 128:(j + 1) * 128], [B_cck], [PB])
            TR(pt_[:, 2, :], ckrd[:, kt, :], [B_ckrd], [PB])
            CP("act", ckvTs[:, :, kt * 128:(kt + 1) * 128], pt_[:, 0:2, :], [PB], [B_ckvTs])
            CP("dve", krTs[:, kt * 128:(kt + 1) * 128], pt_[:, 2, :], [PB], [B_krTs])
        nc0 = 2048 + b * 32
        CP("pool", ckvTs[:, :, 1024:1056], ckvT[:, :, nc0:nc0 + 32], [B_ckvT], [B_ckvTs])
        CP("pool", krTs[:, 1024:1056], krT[:, nc0:nc0 + 32], [B_krT], [B_krTs])
        TT("dve", ksq[0:64, 0:32], krTs[0:64, 1024:1056], krTs[0:64, 1024:1056], ALU.mult, [B_krTs], [B_ksq])
        MM(psm[0:32, 300:301], ksq[0:64, 0:32], ones[0:64, 0:1], True, True, [B_ksq, B_const], [PSN])
        CP("dve", st[0:32, SSKS + 8:SSKS + 9], psm[0:32, 300:301], [PSN], [B_st])
        def samp_A(h, b=b):
            kb = h % 2
            expand_K(ckvTs, B_ckvTs, 1056, h, kb, 128)
            B_tk = Buf("tk")
            TT("dve", tmpk[:, 0:9, h], psm[:, 128:200].rearrange("p (k h) -> p k h", h=8)[:, :, h], st[:, SSKS:SSKS + 9], ALU.add,
               [PSS, B_st], [B_tk])
            ACT(tmpk[:, 0:9, h], tmpk[:, 0:9, h], AF.Ln, [B_tk], [B_tk], scale=1.0 / 192, bias=EPS)
            ACT(rstdk[:, 0:9, h], tmpk[:, 0:9, h], AF.Exp, [B_tk], [B_rk], scale=-0.5)

        def samp_B(h, b=b):
            hg, hh = divmod(h, 4)
            kb = h % 2
            r0 = (h % 2) * 64
            qc = 1024 + b * 32
            ps_, PS = next_pz()
            psv = ps_[:, 0:288].rearrange("p (k q) -> p k q", k=9)
            for kt in range(9):
                m = 128 if kt < 8 else 32
                MM(psv[0:m, kt, :], KT2[:, kb, kt * 128:kt * 128 + m], QT[:, h, qc:qc + 32], True, False, [B_KT[kb], B_QT], [PS])
                MM(psv[0:m, kt, :], krTs[r0:r0 + 64, kt * 128:kt * 128 + m], QrT[r0:r0 + 64, h // 2, qc:qc + 32], False, True,
                   [B_krTs, B_QT], [PS])
            TT("dve", scf[:, 0:8, :], psv[:, 0:8, :], rstdk[:, 0:8, h].unsqueeze(2).to_broadcast([128, 8, 32]), ALU.mult,
               [PS, B_rk], [B_scf])
            TT("dve", scf[0:32, 8, :], psv[0:32, 8, :], rstdk[0:32, 8, h:h + 1].to_broadcast([32, 32]), ALU.mult, [PS, B_rk], [B_scf])
            ACT(PTs[:, b, 0:8, b * 32:(b + 1) * 32], scf[:, 0:8, :], AF.Exp, [B_scf], [B_PTs[b]])
            ACT(PTs[0:32, b, 8, b * 32:(b + 1) * 32], scf[0:32, 8, :], AF.Exp, [B_scf], [B_PTs[b]])
            pa, PA = pacc[hg], PACC[hg]
            for kt in range(9):
                m = 128 if kt < 8 else 32
                MM(pa[:, hh, :], PTs[0:m, b, kt, :], V4[0:m, kt, hh, :], False, False, [B_PTs[b], B_V4], [PA], skip=True)
                MM(psu[:, 8 + h:9 + h], PTs[0:m, b, kt, :], ones[0:m, 0:1], False, False, [B_PTs[b], B_const], [PSU], skip=True)

        expand_V(ckvTs, B_ckvTs, 1056, 0)
        for h in range(8):
            samp_A(h)
            if h >= 1:
                samp_B(h - 1)
            if h == 4:
                expand_V(ckvTs, B_ckvTs, 1056, 1)
        samp_B(7)
    for hg in range(2):
        for hh in range(4):
            h = hg * 4 + hh
            finish_heads(pacc[hg][:, hh:hh + 1, :], PACC[hg], psu[:, 8 + h:9 + h], PSU, h, [8], SUMS + h)()
    RED(st[:, SCR + 16:SCR + 25], st[:, SSA:SSA + 72].rearrange("p (t h) -> p t h", h=8), [B_st, B_ssa], [B_st])
    rstd_of(st[:, RSTD_A:RSTD_A + 9], st[:, SCR + 16:SCR + 25], 1024, [B_st], [B_st], st[:, SCR + 26:SCR + 35])

    if STOP == 'B2':
        return finish()
    P.barrier()
    pzpool[0] = pz6
    B_y = [Buf(f"y{t}") for t in range(NT)]
    ysem0 = P.dma_sem("ysem0")
    for t in range(NT):
        ytok = DMA("sp", y_acc[:, t, :], x_own[t * 128:(t + 1) * 128, :], [], [B_y[t]], ysem0 if t < 2 else ysem)
        if t == 1:
            ytok0 = ytok
    for t in range(NT):
        B_y[t].writers = [ytok0 if t < 2 else ytok]
    B_hnT = Buf("hnT")
    B_hs = Buf("hs")
    rflat = rbuf[:].rearrange("p a b -> p (a b)")

    hsb2 = [hsb[:, 0:256], hsb[:, 256:512]]
    B_hs2 = [Buf("hs0"), Buf("hs1")]
    B_hst = [Buf(f"hst{t}") for t in range(NT)]
    HST = 340
    hi_ = [0]

    def hn_stats(t):
        c = HST + t * 4
        for n in range(4):
            ACT(rflat[:, 0:512], y_acc[:, t, n * 512:(n + 1) * 512], AF.Square, [B_y[t]], [B_tmp, B_hst[t]], accum_out=st[:, c + n:c + n + 1])
        RED(st[:, c:c + 1], st[:, c:c + 4], [B_hst[t]], [B_hst[t]])
        rstd_of(st[:, c + 2:c + 3], st[:, c:c + 1], D, [B_hst[t]], [B_hst[t]], st[:, c + 1:c + 2])

    def hn_trans(t):
        c = HST + t * 4
        for n in range(4):
            pt_, PB = next_ptr()
            for hf in range(2):
                k_ = hi_[0] % 2
                hi_[0] += 1
                cs_ = slice(n * 512 + hf * 256, n * 512 + hf * 256 + 256)
                TS("dve", hsb2[k_], y_acc[:, t, cs_], st[:, c + 2:c + 3], None, ALU.mult, None, [B_y[t], B_hst[t]], [B_hs2[k_]])
                for j in range(2):
                    TR(pt_[:, hf * 2 + j, :], hsb2[k_][:, j * 128:(j + 1) * 128], [B_hs2[k_]], [PB])
            TT("dve", xnT[:, n * 4:(n + 1) * 4, t * 128:(t + 1) * 128], pt_, gffn[:, n * 4:(n + 1) * 4].unsqueeze(2).to_broadcast([128, 4, 128]),
               ALU.mult, [PB, B_const], [B_hnT])

    w_up_v = w_up.rearrange("(kc p) n -> p kc n", p=128)
    w_dn_v = w_down.rearrange("(j p) n -> p j n", p=128)

    def up_view(slot):
        return wslot[slot][:, 0:8192].rearrange("p (k n) -> p k n", k=16)

    def dn_view(slot):
        return wslot[slot][:, 0:8192].rearrange("p (j n) -> p j n", j=4)

    def load_up(c):
        load_w((2 * c) % 3, up_view((2 * c) % 3), w_up_v[:, :, c * 512:(c + 1) * 512])

    def load_dn(c):
        load_w((2 * c + 1) % 3, dn_view((2 * c + 1) % 3), w_dn_v[:, c * 4:(c + 1) * 4, :])

    for n in range(4):
        slot = n % 3
        if n == 0:
            load_w(2, wout_view(2), w_out_v[:, :, 1024:1536])
        if n == 1:
            load_w(0, wout_view(0), w_out_v[:, :, 1536:2048])
        if n == 2:
            load_dn(0)
        wo = wout_view(slot)
        cols = slice(n * 512, (n + 1) * 512)
        for t in range(NT):
            pa_, PA_ = next_pz()
            for kc in range(8):
                MM(pa_[:, 0:512], mT[:, kc, t * 128:(t + 1) * 128], wo[:, kc, :], kc == 0, kc == 7, [B_mT[t], WB[slot]], [PA_])
            pg_, PG_ = next_pz()
            for kc in range(8, 16):
                MM(pg_[:, 0:512], mT[:, kc, t * 128:(t + 1) * 128], wo[:, kc, :], kc == 8, kc == 15, [B_mT[t], WB[slot]], [PG_])
            STT(y_acc[:, t, cols], pa_[:, 0:512], st[:, RSTD_A + t:RSTD_A + t + 1], y_acc[:, t, cols], ALU.mult, ALU.add,
                [PA_, B_st], [B_y[t]])
            STT(y_acc[:, t, cols], pg_[:, 0:512], st[:, RSTD_G + t:RSTD_G + t + 1], y_acc[:, t, cols], ALU.mult, ALU.add,
                [PG_, B_st], [B_y[t]])
            if n == 3:
                hn_stats(t)
                if t >= 2:
                    hn_trans(t - 2)
    load_up(0)
    hn_trans(NT - 2)
    hn_trans(NT - 1)
    if STOP == 'C':
        for t in range(NT):
            out_toks.append(DMA("sp", y_d[t * 128:(t + 1) * 128, :], y_acc[:, t, :], [B_y[t]], [], osem_y))
        return finish()
    B_aT = Buf("aT")
    B_r = [Buf("r0"), Buf("r1")]
    NCH = DFF // 512
    ri = 0
    for c in range(NCH):
        if c + 1 < NCH:
            load_up(c + 1)
        su, sd = (2 * c) % 3, (2 * c + 1) % 3
        wup, wdn = up_view(su), dn_view(sd)
        for j in range(4):
            for tg in range(3):
                pu, PU = next_pz()
                for kc in range(16):
                    MM(pu[:, 0:384], wup[:, kc, j * 128:(j + 1) * 128], xnT[:, kc, tg * 384:(tg + 1) * 384], kc == 0, kc == 15,
                       [WB[su], B_hnT], [PU])
                rb = ri % 2
                ri += 1
                ACT(rbuf[:, rb, :], pu[:, 0:384], AF.Relu, [PU], [B_r[rb]])
                TT("dve", aT[:, j, tg * 384:(tg + 1) * 384], rbuf[:, rb, :], rbuf[:, rb, :], ALU.mult, [B_r[rb]], [B_aT])
        if c + 1 < NCH:
            load_dn(c + 1)
        for t in range(NT):
            for n in range(4):
                pd, PD = next_pz()
                for j in range(4):
                    MM(pd[:, 0:512], aT[:, j, t * 128:(t + 1) * 128], wdn[:, j, n * 512:(n + 1) * 512], j == 0, j == 3, [B_aT, WB[sd]], [PD])
                cols = slice(n * 512, (n + 1) * 512)
                TT("dve", y_acc[:, t, cols], pd[:, 0:512], y_acc[:, t, cols], ALU.add, [PD], [B_y[t]])
            if c == NCH - 1:
                out_toks.append(DMA("sp", y_d[t * 128:(t + 1) * 128, :], y_acc[:, t, :], [B_y[t]], [], osem_y))
    return finish()


_NC_CACHE = {}


def _rope_table(pos):
    half = 32
    inv = 10000.0 ** (-np.arange(half, dtype=np.float64) / half)
    ang = pos.astype(np.float64)[:, None] * inv[None, :]
    return np.concatenate([np.cos(ang), np.sin(ang)], axis=-1).astype(np.float32)


def kernel(x_prompt, x_sample, cache_c_kv, cache_k_rope, norm_mix, w_in, q_lat_norm, kv_lat_norm,
           w_uq, w_uk, w_uv, q_norm_nope, q_norm_rope, k_norm_nope, k_norm_rope, v_norm,
           w_spatial, b_spatial, out_norm_attn, out_norm_gmlp, w_out, norm_ffn, w_up, w_down):
    f = lambda a: np.ascontiguousarray(np.asarray(a, dtype=np.float32))
    x_prompt, x_sample = f(x_prompt), f(x_sample)
    cache_c_kv, cache_k_rope = f(cache_c_kv), f(cache_k_rope)
    if "nc" not in _NC_CACHE:
        _NC_CACHE["nc"] = build_program()
    nc = _NC_CACHE["nc"]

    ws = f(w_spatial)[0]
    bs = f(b_spatial)[0]
    wsT = np.zeros((128, 2, 8, 128), np.float32)
    wsT[:, 0] = ws.transpose(2, 0, 1)
    for j in range(4):
        wsT[32 * j:32 * j + 32, 1, :, 32 * j:32 * j + 32] = ws[:, :32, :32].transpose(2, 0, 1)
    bT = np.zeros((128, 2, 8), np.float32)
    bT[:, 0] = bs.T
    bT[:, 1] = np.tile(bs[:, :32].T, (4, 1))
    sidx = np.arange(128)
    trilT = (sidx[:, None] <= sidx[None, :]).astype(np.float32)
    common = {
        "w_in": f(w_in)[0], "w_uq": f(w_uq)[0], "w_uk": f(w_uk)[0], "w_uv": f(w_uv)[0],
        "w_out": f(w_out)[0], "w_up": f(w_up)[0], "w_down": f(w_down)[0],
        "gmix_fm": f(f(norm_mix)[0].reshape(16, 128).T), "gffn_fm": f(f(norm_ffn)[0].reshape(16, 128).T),
        "gq_fm": f(f(q_lat_norm)[0].reshape(4, 128).T),
        "kv_lat_norm": f(kv_lat_norm).reshape(1, 256),
        "q_norm_nope": f(q_norm_nope).reshape(1, 128), "q_norm_rope": f(q_norm_rope).reshape(1, 32),
        "k_norm_nope": f(k_norm_nope).reshape(1, 128), "k_norm_rope": f(k_norm_rope).reshape(1, 32),
        "v_norm": f(v_norm).reshape(1, 1024),
        "out_norm_attn": f(out_norm_attn).reshape(1, 1024), "out_norm_gmlp": f(out_norm_gmlp).reshape(1, 1024),
        "wsT": wsT, "trilT": trilT, "bT": bT, "ident": np.eye(128, dtype=np.float32),
    }
    cs_pre = _rope_table(np.arange(1024))
    in_maps = []
    for c in range(8):
        b, half = c // 2, c % 2
        xo = np.concatenate([x_prompt[b, half * 1024:(half + 1) * 1024], x_sample[4 * c:4 * c + 4].reshape(128, D)], axis=0)
        pos = np.concatenate([half * 1024 + np.arange(1024), np.tile(1024 + np.arange(32), 4)])
        m = dict(common)
        m["x_own"] = f(xo)
        m["x_pre"] = f(x_prompt[b, 0:1024])
        m["cs_own"] = _rope_table(pos)
        m["cs_pre"] = cs_pre
        m["pmask"] = np.full((128, 1), 0.0 if half == 1 else -30000.0, np.float32)
        m["cache_ckv"] = f(cache_c_kv[0, 4 * c:4 * c + 4])
        m["cache_kr"] = f(cache_k_rope[0, 4 * c:4 * c + 4])
        in_maps.append(m)
    res = run_bass_kernel_spmd(nc, in_maps, core_ids=list(range(8)))
    y_p = np.zeros((4, 2048, D), np.float32)
    y_s = np.zeros((32, 32, D), np.float32)
    ckv_p = np.zeros((1, 4, 2048, 256), np.float32)
    kr_p = np.zeros((1, 4, 2048, 64), np.float32)
    ckv_s = np.zeros((1, 32, 32, 256), np.float32)
    kr_s = np.zeros((1, 32, 32, 64), np.float32)
    v_s = np.zeros((1, 32, 32, 1024), np.float32)
    for c in range(8):
        b, half = c // 2, c % 2
        r = res.results[c]
        sl = slice(half * 1024, (half + 1) * 1024)
        y_p[b, sl] = r["y"][0:1024]
        y_s[4 * c:4 * c + 4] = r["y"][1024:].reshape(4, 32, D)
        ckv_p[0, b, sl] = r["o_ckv"][0:1024]
        ckv_s[0, 4 * c:4 * c + 4] = r["o_ckv"][1024:].reshape(4, 32, 256)
        kr_p[0, b, sl] = r["o_kr"][0:1024]
        kr_s[0, 4 * c:4 * c + 4] = r["o_kr"][1024:].reshape(4, 32, 64)
        v_s[0, 4 * c:4 * c + 4] = r["o_v"].reshape(4, 32, 1024)
    return (y_p, y_s, ckv_p, kr_p, ckv_s, kr_s, v_s)
```
